# Optimizing a Trainium2 kernel written in Bass

```python
import math
import jax, jax.numpy as jnp
from jax import lax
import numpy as np

D_MODEL = 1024
BATCH = 8
SEQ = 4096
DEPTH = 2

N_MIXERS = 2
N_MOBA = (DEPTH + 1) // 2
N_GDN = DEPTH // 2

MOBA_HEAD_DIM = 128
MOBA_HEADS = D_MODEL // MOBA_HEAD_DIM
MOBA_BLOCK = 256
MOBA_TOPK = 3
MOBA_QUERY_CHUNK = 16
ROPE_THETA = 10000.0

GDN_HEAD_DIM = 128
GDN_QK_HEADS = D_MODEL // GDN_HEAD_DIM
GDN_V_HEADS = 2 * GDN_QK_HEADS
GDN_QK_DIM = GDN_QK_HEADS * GDN_HEAD_DIM
GDN_V_DIM = GDN_V_HEADS * GDN_HEAD_DIM
GDN_CONV_DIM = 2 * GDN_QK_DIM + GDN_V_DIM
GDN_PROJ_DIM = GDN_CONV_DIM + GDN_V_DIM + 2 * GDN_V_HEADS
GDN_CONV = 4
GDN_CHUNK = 64

D_FF = 2816
HALF_STEP = 0.5

DEEPNORM_ALPHA = (2 * DEPTH) ** 0.25
DEEPNORM_BETA = (8 * DEPTH) ** -0.25
LN_EPS = 1e-5
RMS_EPS = 1e-6
NEG_INF = -1e30

kernel_name = "hybrid_moba_gdn_macaron_deepnorm"

f32 = jnp.float32


def layer_norm(x, g, b):
    xf = x.astype(f32)
    mu = xf.mean(-1, keepdims=True)
    var = jnp.square(xf - mu).mean(-1, keepdims=True)
    y = (xf - mu) * lax.rsqrt(var + LN_EPS) * g.astype(f32) + b.astype(f32)
    return y.astype(x.dtype)


def swiglu_ffn(x, w_in, w_out):
    gate, up = jnp.split(x @ w_in, 2, axis=-1)
    return (jax.nn.silu(gate) * up) @ w_out


def rope(x, pos):
    half = x.shape[-1] // 2
    inv_freq = ROPE_THETA ** (-jnp.arange(half, dtype=f32) / half)
    ang = pos.astype(f32)[:, None] * inv_freq[None, :]
    cos = jnp.cos(ang)[None, :, None, :]
    sin = jnp.sin(ang)[None, :, None, :]
    x1 = x[..., :half].astype(f32)
    x2 = x[..., half:].astype(f32)
    out = jnp.concatenate([x1 * cos - x2 * sin, x2 * cos + x1 * sin], axis=-1)
    return out.astype(x.dtype)


def moba_attention(x, w_in, w_out):
    B, T, _ = x.shape
    H, Dh, BS, QC = MOBA_HEADS, MOBA_HEAD_DIM, MOBA_BLOCK, MOBA_QUERY_CHUNK
    q, k, v = jnp.split(x @ w_in, 3, axis=-1)
    pos = jnp.arange(T)
    q = rope(q.reshape(B, T, H, Dh), pos)
    k = rope(k.reshape(B, T, H, Dh), pos)
    v = v.reshape(B, T, H, Dh)
    n_blk = -(-T // BS)
    Tp = n_blk * BS
    pad = ((0, 0), (0, 0), (0, Tp - T), (0, 0))
    q, k, v = [jnp.pad(a.transpose(0, 2, 1, 3), pad) for a in (q, k, v)]
    kb = k.reshape(B, H, n_blk, BS, Dh)
    vb = v.reshape(B, H, n_blk, BS, Dh)
    scale = Dh ** -0.5

    k_mean = kb.astype(f32).mean(axis=3)
    gate = jnp.einsum('bhtd,bhnd->bhtn', q.astype(f32), k_mean)
    q_blk = jnp.arange(Tp) // BS
    past = jnp.arange(n_blk)[None, :] < q_blk[:, None]
    gate = jnp.where(past, gate, NEG_INF)
    n_sel = min(MOBA_TOPK, n_blk)
    _, sel = lax.top_k(gate, n_sel)
    sel_valid = sel < q_blk[:, None]

    n_qc = Tp // QC
    q_c = q.reshape(B, H, n_qc, QC, Dh).transpose(2, 0, 1, 3, 4)
    sel_c = sel.reshape(B, H, n_qc, QC, n_sel).transpose(2, 0, 1, 3, 4)
    valid_c = sel_valid.reshape(B, H, n_qc, QC, n_sel).transpose(2, 0, 1, 3, 4)
    b_idx = jnp.arange(B)[:, None, None, None]
    h_idx = jnp.arange(H)[None, :, None, None]

    def attend_chunk(args):
        c, qq, ss, vv = args
        t0 = c * QC
        blk = t0 // BS
        k_sel = kb[b_idx, h_idx, ss]
        v_sel = vb[b_idx, h_idx, ss]
        k_own = lax.dynamic_index_in_dim(kb, blk, axis=2, keepdims=False)
        v_own = lax.dynamic_index_in_dim(vb, blk, axis=2, keepdims=False)
        s_past = jnp.einsum('bhqd,bhqspd->bhqsp', qq, k_sel).astype(f32) * scale
        s_past = jnp.where(vv[..., None], s_past, NEG_INF).reshape(B, H, QC, n_sel * BS)
        s_own = jnp.einsum('bhqd,bhpd->bhqp', qq, k_own).astype(f32) * scale
        q_off = t0 - blk * BS + jnp.arange(QC)
        causal = jnp.arange(BS)[None, :] <= q_off[:, None]
        s_own = jnp.where(causal, s_own, NEG_INF)
        p = jax.nn.softmax(jnp.concatenate([s_past, s_own], axis=-1), axis=-1).astype(v.dtype)
        p_past = p[..., :n_sel * BS].reshape(B, H, QC, n_sel, BS)
        p_own = p[..., n_sel * BS:]
        return (jnp.einsum('bhqsp,bhqspd->bhqd', p_past, v_sel)
                + jnp.einsum('bhqp,bhpd->bhqd', p_own, v_own))

    o = lax.map(attend_chunk, (jnp.arange(n_qc), q_c, sel_c, valid_c))
    o = o.transpose(1, 0, 3, 2, 4).reshape(B, Tp, H * Dh)[:, :T]
    return o @ w_out


def causal_depthwise_conv(x, w):
    K, C = w.shape
    return lax.conv_general_dilated(
        x, w[:, None, :].astype(x.dtype), window_strides=(1,), padding=[(K - 1, 0)],
        dimension_numbers=('NWC', 'WIO', 'NWC'), feature_group_count=C)


def l2norm(x):
    xf = x.astype(f32)
    return xf * lax.rsqrt(jnp.sum(xf * xf, axis=-1, keepdims=True) + RMS_EPS)


def chunk_gated_delta_rule(q, k, v, g, beta):
    B, T, H, Dk = q.shape
    Dv = v.shape[-1]
    C = GDN_CHUNK
    n = T // C

    def chunked(a):
        a = a.astype(f32).reshape((B, n, C, H) + a.shape[3:])
        return jnp.moveaxis(a, 3, 1)

    q, k, v, g, beta = [chunked(a) for a in (q, k, v, g, beta)]
    gc = jnp.cumsum(g, axis=-1)
    lower_incl = jnp.tril(jnp.ones((C, C), bool))
    strict = jnp.tril(jnp.ones((C, C), bool), -1)
    decay = jnp.exp(jnp.where(lower_incl, gc[..., :, None] - gc[..., None, :], NEG_INF))
    kb = k * beta[..., None]
    a_mat = jnp.where(strict, jnp.einsum('bhnid,bhnjd->bhnij', kb, k) * decay, 0.0)
    rhs = jnp.concatenate([v * beta[..., None], kb * jnp.exp(gc)[..., None]], axis=-1)
    uw = lax.linalg.triangular_solve(a_mat, rhs, left_side=True, lower=True, unit_diagonal=True)
    u, w = uw[..., :Dv], uw[..., Dv:]
    attn_intra = jnp.where(lower_incl, jnp.einsum('bhnid,bhnjd->bhnij', q, k) * decay, 0.0)
    q_dec = q * jnp.exp(gc)[..., None]
    k_dec = k * jnp.exp(gc[..., -1:] - gc)[..., None]
    g_last = jnp.exp(gc[..., -1])
    xs = tuple(jnp.moveaxis(a, 2, 0) for a in (u, w, attn_intra, q_dec, k_dec, g_last))

    def step(S, inp):
        u_c, w_c, a_c, q_c, k_c, gl = inp
        v_new = u_c - jnp.einsum('bhck,bhkv->bhcv', w_c, S)
        o_c = jnp.einsum('bhck,bhkv->bhcv', q_c, S) + jnp.einsum('bhij,bhjv->bhiv', a_c, v_new)
        S = S * gl[..., None, None] + jnp.einsum('bhck,bhcv->bhkv', k_c, v_new)
        return S, o_c

    S0 = jnp.zeros((B, H, Dk, Dv), f32)
    _, o = lax.scan(step, S0, xs)
    return o.transpose(1, 0, 3, 2, 4).reshape(B, T, H, Dv)


def gated_deltanet(x, w_in, conv_w, a_log, dt_bias, norm_w, w_out):
    B, T, _ = x.shape
    Hk, Hv, Dh = GDN_QK_HEADS, GDN_V_HEADS, GDN_HEAD_DIM
    proj = x @ w_in
    qkv, z, b, a = jnp.split(
        proj, [GDN_CONV_DIM, GDN_CONV_DIM + GDN_V_DIM, GDN_CONV_DIM + GDN_V_DIM + Hv], axis=-1)
    qkv = jax.nn.silu(causal_depthwise_conv(qkv, conv_w))
    q, k, v = jnp.split(qkv, [GDN_QK_DIM, 2 * GDN_QK_DIM], axis=-1)
    rep = Hv // Hk
    q = jnp.repeat(q.reshape(B, T, Hk, Dh), rep, axis=2)
    k = jnp.repeat(k.reshape(B, T, Hk, Dh), rep, axis=2)
    v = v.reshape(B, T, Hv, Dh)
    q = l2norm(q) * (Dh ** -0.5)
    k = l2norm(k)
    beta = jax.nn.sigmoid(b.astype(f32))
    g = -jnp.exp(a_log.astype(f32)) * jax.nn.softplus(a.astype(f32) + dt_bias.astype(f32))
    o = chunk_gated_delta_rule(q, k, v, g, beta)
    zf = z.reshape(B, T, Hv, Dh).astype(f32)
    o = o * lax.rsqrt(jnp.mean(o * o, axis=-1, keepdims=True) + RMS_EPS) * norm_w.astype(f32) * jax.nn.silu(zf)
    return o.reshape(B, T, GDN_V_DIM).astype(x.dtype) @ w_out


def setup_inputs(seed: int = 0) -> dict:
    key = jax.random.key(seed)
    ks = jax.random.split(key, 16)
    D = D_MODEL

    def normal(k, shape, fan_in, scale=1.0):
        return jax.random.normal(k, shape, f32) * (scale * fan_in ** -0.5)

    x = jax.random.normal(ks[0], (BATCH, SEQ, D), f32)
    ln_g = 1.0 + 0.02 * jax.random.normal(ks[1], (DEPTH, 3, D), f32)
    ln_b = 0.02 * jax.random.normal(ks[2], (DEPTH, 3, D), f32)
    ffn_pre_w_in = normal(ks[3], (DEPTH, D, 2 * D_FF), D)
    ffn_pre_w_out = normal(ks[4], (DEPTH, D_FF, D), D_FF, DEEPNORM_BETA)
    ffn_post_w_in = normal(ks[5], (DEPTH, D, 2 * D_FF), D)
    ffn_post_w_out = normal(ks[6], (DEPTH, D_FF, D), D_FF, DEEPNORM_BETA)
    moba_dim = MOBA_HEADS * MOBA_HEAD_DIM
    moba_w_in = normal(ks[7], (N_MOBA, D, 3 * moba_dim), D)
    moba_w_out = normal(ks[8], (N_MOBA, moba_dim, D), moba_dim, DEEPNORM_BETA)
    gdn_w_in = normal(ks[9], (N_GDN, D, GDN_PROJ_DIM), D)
    gdn_conv_w = normal(ks[10], (N_GDN, GDN_CONV, GDN_CONV_DIM), GDN_CONV)
    gdn_a_log = jnp.log(jax.random.uniform(ks[11], (N_GDN, GDN_V_HEADS), f32, 1.0, 16.0))
    dt = jnp.exp(jax.random.uniform(ks[12], (N_GDN, GDN_V_HEADS), f32, math.log(1e-3), math.log(1e-1)))
    gdn_dt_bias = dt + jnp.log(-jnp.expm1(-dt))
    gdn_norm_w = 1.0 + 0.02 * jax.random.normal(ks[13], (N_GDN, GDN_HEAD_DIM), f32)
    gdn_w_out = normal(ks[14], (N_GDN, GDN_V_DIM, D), GDN_V_DIM, DEEPNORM_BETA)
    return {"x": x, "ln_g": ln_g, "ln_b": ln_b,
            "ffn_pre_w_in": ffn_pre_w_in, "ffn_pre_w_out": ffn_pre_w_out,
            "ffn_post_w_in": ffn_post_w_in, "ffn_post_w_out": ffn_post_w_out,
            "moba_w_in": moba_w_in, "moba_w_out": moba_w_out,
            "gdn_w_in": gdn_w_in, "gdn_conv_w": gdn_conv_w, "gdn_a_log": gdn_a_log,
            "gdn_dt_bias": gdn_dt_bias, "gdn_norm_w": gdn_norm_w, "gdn_w_out": gdn_w_out}


def reference(x, ln_g, ln_b, ffn_pre_w_in, ffn_pre_w_out, ffn_post_w_in, ffn_post_w_out,
              moba_w_in, moba_w_out, gdn_w_in, gdn_conv_w, gdn_a_log, gdn_dt_bias,
              gdn_norm_w, gdn_w_out):
    h = x
    for i in range(DEPTH):
        h = layer_norm(DEEPNORM_ALPHA * h + HALF_STEP * swiglu_ffn(h, ffn_pre_w_in[i], ffn_pre_w_out[i]),
                       ln_g[i, 0], ln_b[i, 0])
        j = i // N_MIXERS
        if i % N_MIXERS == 0:
            mix = moba_attention(h, moba_w_in[j], moba_w_out[j])
        else:
            mix = gated_deltanet(h, gdn_w_in[j], gdn_conv_w[j], gdn_a_log[j], gdn_dt_bias[j],
                                 gdn_norm_w[j], gdn_w_out[j])
        h = layer_norm(DEEPNORM_ALPHA * h + mix, ln_g[i, 1], ln_b[i, 1])
        h = layer_norm(DEEPNORM_ALPHA * h + HALF_STEP * swiglu_ffn(h, ffn_post_w_in[i], ffn_post_w_out[i]),
                       ln_g[i, 2], ln_b[i, 2])
    return h
```

```python
import math
import numpy as np
import concourse.bass as bass
import concourse.mybir as mybir
from concourse.bass_utils import run_bass_kernel_spmd

F32 = mybir.dt.float32
BF16 = mybir.dt.bfloat16
AF = mybir.ActivationFunctionType
ALU = mybir.AluOpType
AX = mybir.AxisListType

D = 1024
DFF = 2816
SEQ = 4096
NB = 8
ALPHA = (2 * 2) ** 0.25
LN_EPS = 1e-5
RMS_EPS = 1e-6


class Buf:
    __slots__ = ("name", "last_w", "readers", "dsem", "dcount", "excl")

    def __init__(self, name):
        self.name = name
        self.excl = False
        self.last_w = None
        self.readers = []
        self.dsem = None
        self.dcount = 0


class Op:
    __slots__ = ("eng", "fn", "deps", "sig", "need_sig", "is_dma", "dbuf", "dval", "idx", "dslot")

    def __init__(self, eng, fn, is_dma=False):
        self.eng = eng
        self.fn = fn
        self.deps = []
        self.sig = None
        self.need_sig = False
        self.is_dma = is_dma
        self.dbuf = None
        self.dval = 0
        self.idx = 0
        self.dslot = None


ENGS = ("pe", "act", "dve", "pool", "sp")


class Prog:
    def __init__(self, nc):
        self.nc = nc
        self.ops = {e: [] for e in ENGS}
        self.bufs = []
        self.dma_bufs = []
        self.nops = 0
        self.slots = []
        self.free_slots = []
        self.free_slots_sw = []

    def buf(self, name):
        b = Buf(name)
        self.bufs.append(b)
        return b

    def bufs_n(self, name, n):
        return [self.buf("%s%d" % (name, i)) for i in range(n)]

    def _track(self, op, reads, writes):
        seen = set()
        for b in reads:
            w = b.last_w
            if w is not None and id(w) not in seen:
                seen.add(id(w))
                op.deps.append((w, True))
            if b.excl:
                for r in b.readers:
                    if id(r) not in seen:
                        seen.add(id(r))
                        op.deps.append((r, False))
                b.readers = []
        for b in writes:
            w = b.last_w
            if w is not None and id(w) not in seen:
                seen.add(id(w))
                op.deps.append((w, False))
            for r in b.readers:
                if id(r) not in seen:
                    seen.add(id(r))
                    op.deps.append((r, False))
        for b in reads:
            b.readers.append(op)
        for b in writes:
            b.last_w = op
            b.readers = []

    def add(self, eng, fn, reads=(), writes=()):
        op = Op(eng, fn)
        self._track(op, reads, writes)
        op.idx = self.nops
        self.nops += 1
        self.ops[eng].append(op)
        return op

    def dma(self, out_ap, in_ap, reads=(), writes=(), sembuf=None, eng="sp"):
        op = Op(eng, None, is_dma=True)
        op.fn = (out_ap, in_ap)
        self._track(op, reads, writes)
        kind = 1 if eng == "pool" else 0
        if sembuf.dsem is None:
            fl = self.free_slots_sw if kind else self.free_slots
            if fl:
                sembuf.dsem = fl.pop()
            else:
                sembuf.dsem = [0, None, kind]
                self.slots.append(sembuf.dsem)
            self.dma_bufs.append(sembuf)
        assert sembuf.dsem[2] == kind, "mixing SW/HW DGE on one semaphore"
        sembuf.dsem[0] += 16
        op.dbuf = sembuf
        op.dslot = sembuf.dsem
        op.dval = sembuf.dsem[0]
        op.idx = self.nops
        self.nops += 1
        self.ops[eng].append(op)
        return op

    def barrier(self):
        lasts = []
        for e in ENGS:
            for o in reversed(self.ops[e]):
                if not o.is_dma and o.fn is not None:
                    lasts.append(o)
                    break
        dmas = [(b.dsem, b.dsem[0]) for b in self.dma_bufs]
        for b in self.dma_bufs:
            (self.free_slots_sw if b.dsem[2] else self.free_slots).append(b.dsem)
            b.dsem = None
        self.dma_bufs = []
        for e in ENGS:
            op = Op(e, None)
            op.deps = [(o, True) for o in lasts]
            op.dval = dmas
            op.idx = self.nops
            self.nops += 1
            self.ops[e].append(op)
        for b in self.bufs:
            b.last_w = None
            b.readers = []

    def emit(self):
        nc = self.nc
        for e in ENGS:
            for op in self.ops[e]:
                for d, raw in op.deps:
                    if d.is_dma:
                        continue
                    if d.eng != op.eng or (raw and op.eng != "pe") or op.fn is None or op.is_dma:
                        d.need_sig = True
        for e in ENGS:
            c = 0
            for op in self.ops[e]:
                if op.need_sig:
                    c += 1
                    op.sig = c
        sems = {e: nc.alloc_semaphore("s_" + e) for e in ENGS}
        for i, sl in enumerate(self.slots):
            sl[1] = nc.alloc_semaphore("dsem%d" % i)
        engobj = {"pe": nc.tensor, "act": nc.scalar, "dve": nc.vector, "pool": nc.gpsimd, "sp": nc.sync}
        self.nwaits = 0
        with nc.Block() as block:
            def run(e, eng):
                seen = {}
                for op in self.ops[e]:
                    waits = {}
                    for d, raw in op.deps:
                        if d.is_dma:
                            key = ("d", id(d.dslot))
                            if waits.get(key, (None, 0))[1] < d.dval:
                                waits[key] = (d.dslot[1], d.dval)
                        else:
                            if (d.eng == op.eng and not (raw and op.eng != "pe")
                                    and op.fn is not None and not op.is_dma):
                                continue
                            key = ("e", d.eng)
                            if waits.get(key, (None, 0))[1] < d.sig:
                                waits[key] = (sems[d.eng], d.sig)
                    if op.fn is None and not op.is_dma:
                        for sl, v in op.dval:
                            waits[("d", id(sl))] = (sl[1], v)
                    for key, (s, v) in waits.items():
                        if seen.get(key, 0) >= v:
                            continue
                        seen[key] = v
                        eng.wait_ge(s, v)
                        self.nwaits += 1
                    if op.is_dma:
                        o, i = op.fn
                        eng.dma_start(out=o, in_=i).then_inc(op.dslot[1], 16)
                    elif op.fn is not None:
                        ins = op.fn(eng)
                        if op.need_sig:
                            ins.then_inc(sems[e], 1)

            @block.tensor
            def _(eng):
                run("pe", eng)

            @block.scalar
            def _(eng):
                run("act", eng)

            @block.vector
            def _(eng):
                run("dve", eng)

            @block.gpsimd
            def _(eng):
                run("pool", eng)

            @block.sync
            def _(eng):
                run("sp", eng)


class Arena:
    def __init__(self, nc, base=16512, limit=229344):
        self.nc = nc
        self.top = base
        self.limit = limit
        self.n = 0

    def mark(self):
        return self.top

    def release(self, m):
        self.top = m

    def alloc(self, shape, dtype, name=None):
        nbytes = int(np.prod(shape[1:])) * (4 if dtype == F32 else 2)
        off = (self.top + 63) // 64 * 64
        assert off + nbytes <= self.limit, ("SBUF arena overflow", name, off, nbytes)
        self.top = off + nbytes
        self.n += 1
        t = self.nc.alloc_sbuf_tensor_at("%s_%d" % (name or "t", self.n), list(shape), dtype, offset=off)
        return t


class Ctx:
    pass


def make_ctx(nc):
    c = Ctx()
    c.nc = nc
    c.P = Prog(nc)
    c.A = Arena(nc)
    c.ps = [nc.alloc_psum_tensor("psb%d" % i, [128, 512], F32) for i in range(8)]
    c.psb = c.P.bufs_n("psb", 8)
    for b in c.psb:
        b.excl = True
    return c


def load_consts(c, consts_ap):
    P, A = c.P, c.A
    c.ident = A.alloc([128, 128], F32, "ident")
    c.ident_b = P.buf("ident")
    P.dma(c.ident[:], consts_ap[:, 0:128], writes=[c.ident_b], sembuf=c.ident_b)
    c.identb = A.alloc([128, 128], BF16, "identb")
    c.identb_b = P.buf("identb")
    P.add("dve", lambda e: e.tensor_copy(c.identb[:], c.ident[:]), reads=[c.ident_b], writes=[c.identb_b])


def bcast_rows(ap2d, nrows_part=128):
    return ap2d.partition_broadcast(nrows_part)


def ln_tile(c, xr, xr_b, gt, bt, gb_b, dst_ap, st):
    P = c.P
    eps = LN_EPS / (ALPHA * ALPHA)
    stats, mv, rstd, nmr = st["stats"], st["mv"], st["rstd"], st["nmr"]
    sb = st["b"]
    P.add("dve", lambda e: e.bn_stats(stats[:, 0, :], xr[:, 0:512]), reads=[xr_b], writes=[sb])
    P.add("dve", lambda e: e.bn_stats(stats[:, 1, :], xr[:, 512:1024]), reads=[xr_b], writes=[sb])
    P.add("dve", lambda e: e.bn_aggr(mv[:], stats[:].rearrange("p a b -> p (a b)")), reads=[sb], writes=[sb])
    P.add("act", lambda e: e.activation(rstd[:], mv[:, 1:2], AF.Sqrt, bias=c.epsln[:], scale=1.0),
          reads=[sb, c.cst_b], writes=[sb])
    P.add("dve", lambda e: e.reciprocal(rstd[:], rstd[:]), reads=[sb], writes=[sb])
    P.add("dve", lambda e: e.tensor_scalar(nmr[:], mv[:, 0:1], -1.0, rstd[:], ALU.mult, ALU.mult),
          reads=[sb], writes=[sb])
    P.add("act", lambda e: e.activation(xr[:], xr[:], AF.Identity, bias=nmr[:], scale=rstd[:]),
          reads=[sb, xr_b], writes=[xr_b])
    P.add("dve", lambda e: e.tensor_tensor(xr[:], xr[:], gt[:], ALU.mult), reads=[xr_b, gb_b], writes=[xr_b])
    P.add("dve", lambda e: e.tensor_tensor(xr[:], xr[:], bt[:], ALU.add), reads=[xr_b, gb_b], writes=[xr_b])
    P.dma(dst_ap, xr[:], reads=[xr_b], sembuf=xr_b)


def alloc_ln_small(c, n=2):
    out = []
    for i in range(n):
        st = {
            "stats": c.A.alloc([128, 2, 6], F32, "stats"),
            "mv": c.A.alloc([128, 2], F32, "mv"),
            "rstd": c.A.alloc([128, 1], F32, "rstd"),
            "nmr": c.A.alloc([128, 1], F32, "nmr"),
            "b": c.P.buf("lnsmall%d" % i),
        }
        out.append(st)
    return out


def load_ln_params(c, ln_g_ap, ln_b_ap, li, si):
    P, A = c.P, c.A
    gt = A.alloc([128, 1024], F32, "lng")
    bt = A.alloc([128, 1024], F32, "lnb")
    gb_b = P.buf("lngb")
    b2 = P.buf("lngb2")
    P.dma(gt[:], ln_g_ap[li, si:si + 1, :].partition_broadcast(128), writes=[gb_b], sembuf=gb_b)
    P.dma(bt[:], ln_b_ap[li, si:si + 1, :].partition_broadcast(128), writes=[b2], sembuf=b2)
    P.add("dve", lambda e: e.tensor_copy(bt[:, 0:1], bt[:, 0:1]), reads=[b2, gb_b], writes=[gb_b])
    return gt, bt, gb_b


def transpose_in(c, xs, xs_b, hT, hT_b, col0, banks):
    P = c.P
    for half in range(2):
        bk = banks[half]
        ps, psb = c.ps[bk], c.psb[bk]

        def f(e, half=half, ps=ps):
            ins = None
            for q in range(4):
                kc = half * 4 + q
                ins = e.transpose(ps[:, q * 128:(q + 1) * 128], xs[:, kc * 128:(kc + 1) * 128], c.ident[:])
            return ins
        P.add("pe", f, reads=[xs_b, c.ident_b], writes=[psb])
        eng = "act" if half == 0 else "dve"
        if eng == "act":
            P.add("act", lambda e, half=half, ps=ps: e.activation(
                hT[:, half * 4:half * 4 + 4, col0:col0 + 128],
                ps[:].rearrange("p (a b) -> p a b", a=4), AF.Copy),
                reads=[psb], writes=[hT_b])
        else:
            P.add("dve", lambda e, half=half, ps=ps: e.tensor_copy(
                hT[:, half * 4:half * 4 + 4, col0:col0 + 128],
                ps[:].rearrange("p (a b) -> p a b", a=4)),
                reads=[psb], writes=[hT_b])


def stage_ffn(c, src, dst, w_in, w_out, ln_g, ln_b, li, si, T):
    P, A, nc = c.P, c.A, c.nc
    m0 = A.mark()
    TT = 512
    NJ = DFF // 128
    win = A.alloc([128, 8, 2 * DFF], BF16, "win")
    wout = A.alloc([128, NJ, D], BF16, "wout")
    NWG = 11
    win_b = P.bufs_n("win", NWG)
    wout_b = P.bufs_n("wout", NJ)
    w_in_v = w_in.rearrange("(kc p) n -> p kc n", p=128)
    w_out_v = w_out.rearrange("(j p) d -> p j d", p=128)
    CW = 2 * DFF // NWG
    order = []
    for g in range(NWG // 2 + 1):
        for gg in (g, g + (NWG + 1) // 2):
            if gg < NWG and gg not in order:
                order.append(gg)
    for g in order:
        P.dma(win[:, :, g * CW:(g + 1) * CW], w_in_v[:, :, g * CW:(g + 1) * CW],
              writes=[win_b[g]], sembuf=win_b[g], eng="pool")
    for j in range(NJ):
        P.dma(wout[:, j, :], w_out_v[:, j, :], writes=[wout_b[j]], sembuf=wout_b[j], eng="pool")
    gt, bt, gb_b = load_ln_params(c, ln_g, ln_b, li, si)
    hT = A.alloc([128, 8, TT], BF16, "hT")
    hT_b = P.buf("hT")
    aT = A.alloc([128, NJ, TT], BF16, "aT")
    aT_b = P.bufs_n("aT", NJ)
    NX = 3
    xs = [A.alloc([128, D], F32, "xs") for _ in range(NX)]
    xs_b = P.bufs_n("xs", NX)
    sg = [A.alloc([128, TT], F32, "sg") for _ in range(2)]
    sg_b = P.bufs_n("sg", 2)
    lns = alloc_ln_small(c, 2)
    xi = 0
    cres = 0.5 / ALPHA
    for t in range(T // TT):
        r0 = t * TT
        for s in range(TT // 128):
            x, xb = xs[xi % NX], xs_b[xi % NX]
            xi += 1
            P.dma(x[:], src[r0 + s * 128:r0 + (s + 1) * 128, :], writes=[xb], sembuf=xb)
            transpose_in(c, x, xb, hT, hT_b, s * 128, (0, 1))
        if getattr(c, "cut", 9) <= 1:
            continue
        for j in range(NJ):
            gcol = j * 128
            ucol = DFF + j * 128
            bg, bu = (0, 1) if j % 2 == 0 else (2, 3)

            def mmg(e, col=gcol, bk=bg):
                ins = None
                for kc in range(8):
                    ins = e.matmul(c.ps[bk][:, 0:TT], win[:, kc, col:col + 128], hT[:, kc, :],
                                   start=(kc == 0), stop=(kc == 7))
                return ins
            P.add("pe", mmg, reads=[hT_b, win_b[gcol // CW]], writes=[c.psb[bg]])
            P.add("pe", lambda e, col=ucol, bk=bu: mmg(e, col, bk), reads=[hT_b, win_b[ucol // CW]],
                  writes=[c.psb[bu]])
            s_, s_b = sg[j % 2], sg_b[j % 2]
            P.add("act", lambda e, bk=bg, s_=s_: e.activation(s_[:], c.ps[bk][:, 0:TT], AF.Silu),
                  reads=[c.psb[bg]], writes=[s_b])
            P.add("dve", lambda e, bk=bu, s_=s_, j=j: e.tensor_tensor(aT[:, j, :], c.ps[bk][:, 0:TT], s_[:], ALU.mult),
                  reads=[c.psb[bu], s_b], writes=[aT_b[j]])
        if getattr(c, "cut", 9) <= 2:
            continue
        for s in range(TT // 128):
            bk0, bk1 = (4, 5) if s % 2 == 0 else (6, 7)
            for hh, bk in enumerate((bk0, bk1)):
                def mmo(e, hh=hh, bk=bk, s=s):
                    ins = None
                    for j in range(NJ):
                        ins = e.matmul(c.ps[bk][:, :], aT[:, j, s * 128:(s + 1) * 128],
                                       wout[:, j, hh * 512:(hh + 1) * 512], start=(j == 0), stop=(j == NJ - 1))
                    return ins
                P.add("pe", mmo, reads=list(aT_b) + list(wout_b), writes=[c.psb[bk]])
            if getattr(c, "cut", 9) <= 3:
                continue
            x, xb = xs[xi % NX], xs_b[xi % NX]
            xi += 1
            P.dma(x[:], src[r0 + s * 128:r0 + (s + 1) * 128, :], writes=[xb], sembuf=xb)
            for hh, bk in enumerate((bk0, bk1)):
                P.add("dve", lambda e, hh=hh, bk=bk, x=x: e.scalar_tensor_tensor(
                    x[:, hh * 512:(hh + 1) * 512], c.ps[bk][:, :], cres, x[:, hh * 512:(hh + 1) * 512],
                    ALU.mult, ALU.add), reads=[c.psb[bk], xb], writes=[xb])
            if getattr(c, "cut", 9) <= 4:
                P.dma(dst[r0 + s * 128:r0 + (s + 1) * 128, :], x[:], reads=[xb], sembuf=xb)
                continue
            ln_tile(c, x, xb, gt, bt, gb_b, dst[r0 + s * 128:r0 + (s + 1) * 128, :], lns[s % 2])
    P.barrier()
    A.release(m0)


def load_small_consts(c):
    P, A = c.P, c.A
    c.epsln = A.alloc([128, 1], F32, "epsln")
    c.cst_b = P.buf("cst")
    P.add("dve", lambda e: e.memset(c.epsln[:], LN_EPS / (ALPHA * ALPHA)), writes=[c.cst_b])
    c.one_col = A.alloc([128, 1], F32, "onecol")
    P.add("dve", lambda e: e.memset(c.one_col[:], 1.0), writes=[c.cst_b])


C_IDENT = 0
C_ROT = 128
C_PASTM = 256
C_PASTS = 512
C_ONES = 768
C_UT = 896
C_SMASK = 1024
C_IMASKT = 1152
NF32 = 1280
CB_IDENT = 0
CB_CM = 128
CB_EN = 640
CB_ONES = 2688
NBF = 2816
NCONST = NF32 + NBF
NEG = -30000.0


def make_consts():
    c = np.zeros((128, NCONST), np.float32)
    c[:, C_IDENT:C_IDENT + 128] = np.eye(128)
    rot = np.zeros((128, 128), np.float32)
    for p in range(64):
        rot[p, p + 64] = 1.0
        rot[p + 64, p] = -1.0
    c[:, C_ROT:C_ROT + 128] = rot
    k = np.arange(128)[:, None]
    q = np.arange(256)[None, :]
    c[:, NF32 + CB_CM:NF32 + CB_CM + 256] = np.where(q >= k, 0.0, NEG)
    c[:, NF32 + CB_CM + 256:NF32 + CB_CM + 512] = np.where(q >= k + 128, 0.0, NEG)
    en = np.zeros((128, 16, 128), np.float32)
    for n in range(16):
        en[n, n, :] = 1.0
    c[:, NF32 + CB_EN:NF32 + CB_EN + 2048] = en.reshape(128, 2048)
    j = np.arange(16)[:, None]
    n = np.arange(16)[None, :]
    c[:, C_PASTM:C_PASTM + 256] = np.where(n < j, 0.0, -1e30).reshape(1, 256)
    c[:, C_PASTS:C_PASTS + 256] = np.where(n < j, 1.0, 0.0).reshape(1, 256)
    c[:, C_ONES:C_ONES + 128] = 1.0
    c[:, NF32 + CB_ONES:NF32 + CB_ONES + 128] = 1.0
    c[:, NF32 + CB_IDENT:NF32 + CB_IDENT + 128] = np.eye(128)
    a = np.arange(128)
    c[:, C_UT:C_UT + 128] = (a[:, None] <= a[None, :]).astype(np.float32)
    c[:, C_SMASK:C_SMASK + 128] = np.where(a[:, None] > a[None, :], 0.0, NEG)
    c[:, C_IMASKT:C_IMASKT + 128] = np.where(a[None, :] >= a[:, None], 0.0, NEG)
    return c


def make_rope(T):
    half = 64
    inv = (10000.0 ** (-np.arange(half, dtype=np.float32) / np.float32(half))).astype(np.float32)
    ang = (np.arange(T, dtype=np.float32)[None, :] * inv[:, None]).astype(np.float32)
    cs = np.cos(ang).astype(np.float32)
    sn = np.sin(ang).astype(np.float32)
    out = np.zeros((2, 128, T), np.float32)
    out[0, :64] = cs
    out[0, 64:] = cs
    out[1, :64] = sn
    out[1, 64:] = sn
    return out


def load_consts_full(c, consts_ap):
    P, A = c.P, c.A
    c.cst = A.alloc([128, NF32], F32, "cst")
    c.cstb = P.buf("cstf")
    P.dma(c.cst[:], consts_ap[:, 0:NF32], writes=[c.cstb], sembuf=c.cstb)
    c.ident = c.cst[:, C_IDENT:C_IDENT + 128]
    c.ident_b = c.cstb
    c.cstbf = A.alloc([128, NBF], BF16, "cstbf")
    c.cstbf_b = P.buf("cstbf")
    m = A.mark()
    tmp = A.alloc([128, NBF], F32, "csttmp")
    tb = P.buf("csttmp")
    P.dma(tmp[:], consts_ap[:, NF32:NCONST], writes=[tb], sembuf=tb)
    P.add("dve", lambda e: e.tensor_copy(c.cstbf[:], tmp[:]), reads=[tb], writes=[c.cstbf_b])
    A.release(m)
    c.identb = c.cstbf[:, CB_IDENT:CB_IDENT + 128]
    c.identb_b = c.cstbf_b


class _V:
    def __init__(self, ap):
        self.ap = ap

    def __getitem__(self, k):
        return self.ap[k]


def resid_ln(c, src, dst, r0, banks, cres, x, xb, gt, bt, gb_b, st):
    P = c.P
    P.dma(x[:], src[r0:r0 + 128, :], writes=[xb], sembuf=xb)
    for hh, bk in enumerate(banks):
        P.add("dve", lambda e, hh=hh, bk=bk: e.scalar_tensor_tensor(
            x[:, hh * 512:(hh + 1) * 512], c.ps[bk][:, :], cres, x[:, hh * 512:(hh + 1) * 512],
            ALU.mult, ALU.add), reads=[c.psb[bk], xb], writes=[xb])
    ln_tile(c, x, xb, gt, bt, gb_b, dst[r0:r0 + 128, :], st)


def stage_moba_proj(c, src, w_in, rope, qT_d, kT_d, v_d, biasT_d, T):
    P, A = c.P, c.A
    m0 = A.mark()
    TT = 512
    win = A.alloc([128, 8, 3 * D], BF16, "mwin")
    win_b = P.bufs_n("mwin", 6)
    w_in_v = w_in.rearrange("(kc p) n -> p kc n", p=128)
    for g in (2, 3, 0, 1, 4, 5):
        P.dma(win[:, :, g * 512:(g + 1) * 512], w_in_v[:, :, g * 512:(g + 1) * 512],
              writes=[win_b[g]], sembuf=win_b[g], eng="pool")
    hT = A.alloc([128, 8, TT], BF16, "hT")
    hT_b = P.buf("hT")
    NX = 3
    xs = [A.alloc([128, D], F32, "xs") for _ in range(NX)]
    xs_b = P.bufs_n("xs", NX)
    cs = [A.alloc([128, 2, TT], F32, "cs") for _ in range(2)]
    cs_b = P.bufs_n("cs", 2)
    qf = [A.alloc([128, TT], F32, "qf") for _ in range(2)]
    qf_b = P.bufs_n("qf", 2)
    t1 = [A.alloc([128, TT], F32, "t1") for _ in range(2)]
    t1_b = P.bufs_n("t1", 2)
    kr32 = [A.alloc([128, TT], F32, "kr32") for _ in range(2)]
    kr32_b = P.bufs_n("kr32", 2)
    q32 = A.alloc([128, 8, TT], F32, "q32")
    q32_b = P.bufs_n("q32", 8)
    ob = [A.alloc([128, TT], BF16, "ob") for _ in range(3)]
    ob_b = P.bufs_n("ob", 3)
    vb = [A.alloc([128, D], BF16, "vb") for _ in range(2)]
    vb_b = P.bufs_n("vb", 2)
    kmean = A.alloc([128, 8, 16], F32, "kmean")
    kmean_b = P.buf("kmean")
    P.add("dve", lambda e: e.memset(kmean[:], 0.0), writes=[kmean_b])
    gm = [A.alloc([128, 8, 16], F32, "gm") for _ in range(2)]
    gm_b = P.bufs_n("gm", 2)
    top8 = [A.alloc([128, 8, 8], F32, "top8") for _ in range(2)]
    selt = [A.alloc([128, 8, 16], F32, "selt") for _ in range(2)]
    bT = [A.alloc([16, 8, 128], BF16, "bT") for _ in range(2)]
    bT_b = P.bufs_n("bT", 2)
    xi = 0
    oi = 0
    for t in range(T // TT):
        r0 = t * TT
        for s in range(4):
            x, xb = xs[xi % NX], xs_b[xi % NX]
            xi += 1
            P.dma(x[:], src[r0 + s * 128:r0 + (s + 1) * 128, :], writes=[xb], sembuf=xb)
            transpose_in(c, x, xb, hT, hT_b, s * 128, (0, 1))
        ct, ctb = cs[t % 2], cs_b[t % 2]
        P.dma(ct[:], rope[:, :, r0:r0 + TT].rearrange("a p t -> p a t"), writes=[ctb], sembuf=ctb)
        it = 0
        for qk in (1, 0):
            for h in range(8):
                col = qk * D + h * 128
                bk = 2 + (it % 2)
                bkr = 4 + (it % 2)

                def mmp(e, col=col, bk=bk):
                    ins = None
                    for kc in range(8):
                        ins = e.matmul(c.ps[bk][:, :], win[:, kc, col:col + 128], hT[:, kc, :],
                                       start=(kc == 0), stop=(kc == 7))
                    return ins
                P.add("pe", mmp, reads=[hT_b, win_b[col // 512]], writes=[c.psb[bk]])
                f, fb = qf[it % 2], qf_b[it % 2]
                P.add("act", lambda e, f=f, bk=bk: e.activation(f[:], c.ps[bk][:, :], AF.Copy),
                      reads=[c.psb[bk]], writes=[fb])
                P.add("pe", lambda e, f=f, bkr=bkr: e.matmul(c.ps[bkr][:, :], c.cst[:, C_ROT:C_ROT + 128], f[:],
                                                            start=True, stop=True),
                      reads=[fb, c.cstb], writes=[c.psb[bkr]])
                tt, ttb = t1[it % 2], t1_b[it % 2]
                P.add("dve", lambda e, tt=tt, f=f, ct=ct: e.tensor_tensor(tt[:], f[:], ct[:, 0, :], ALU.mult),
                      reads=[fb, ctb], writes=[ttb])
                if qk == 1:
                    dst32, dst32_b = kr32[it % 2], kr32_b[it % 2]
                    d32 = dst32[:]
                else:
                    d32, dst32_b = q32[:, h, :], q32_b[h]
                P.add("dve", lambda e, d32=d32, bkr=bkr, ct=ct: e.tensor_tensor(d32, c.ps[bkr][:, :], ct[:, 1, :], ALU.mult),
                      reads=[c.psb[bkr], ctb], writes=[dst32_b])
                P.add("dve", lambda e, d32=d32, tt=tt: e.tensor_tensor(d32, d32, tt[:], ALU.add),
                      reads=[ttb, dst32_b], writes=[dst32_b])
                o, o_b = ob[oi % 3], ob_b[oi % 3]
                oi += 1
                P.add("act", lambda e, o=o, d32=d32: e.activation(o[:], d32, AF.Copy), reads=[dst32_b], writes=[o_b])
                if qk == 1:
                    P.add("dve", lambda e, d32=d32, h=h, t=t: e.tensor_reduce(
                        kmean[:, h, 2 * t:2 * t + 2], d32.rearrange("p (a b) -> p a b", a=2), AX.X, ALU.add),
                        reads=[dst32_b], writes=[kmean_b])
                    P.dma(kT_d[h, :, r0:r0 + TT], o[:], reads=[o_b], sembuf=o_b)
                else:
                    P.dma(qT_d[h, :, r0:r0 + TT], o[:], reads=[o_b], sembuf=o_b)
                it += 1
        for s in range(4):
            for hh in range(2):
                def mmv(e, s=s, hh=hh):
                    ins = None
                    for kc in range(8):
                        ins = e.matmul(c.ps[hh][:, :], hT[:, kc, s * 128:(s + 1) * 128],
                                       win[:, kc, 2 * D + hh * 512:2 * D + (hh + 1) * 512],
                                       start=(kc == 0), stop=(kc == 7))
                    return ins
                P.add("pe", mmv, reads=[hT_b, win_b[4 + hh]], writes=[c.psb[hh]])
            v, v_b = vb[s % 2], vb_b[s % 2]
            P.add("act", lambda e, v=v: e.activation(v[:, 0:512], c.ps[0][:, :], AF.Copy),
                  reads=[c.psb[0]], writes=[v_b])
            P.add("dve", lambda e, v=v: e.tensor_copy(v[:, 512:1024], c.ps[1][:, :]),
                  reads=[c.psb[1]], writes=[v_b])
            P.dma(v_d[r0 + s * 128:r0 + (s + 1) * 128, :], v[:], reads=[v_b], sembuf=v_b)
        for s in range(4):
            jb = (r0 + s * 128) // 256
            g, g_b = gm[s % 2], gm_b[s % 2]
            t8, sl = top8[s % 2], selt[s % 2]

            def mmgate(e, s=s):
                ins = None
                for h in range(8):
                    ins = e.matmul(c.ps[6][:, h * 16:(h + 1) * 16], q32[:, h, s * 128:(s + 1) * 128],
                                   kmean[:, h, :], start=True, stop=True)
                return ins
            P.add("pe", mmgate, reads=list(q32_b) + [kmean_b], writes=[c.psb[6]])
            pm = c.cst[:, C_PASTM + jb * 16:C_PASTM + jb * 16 + 16]
            psel = c.cst[:, C_PASTS + jb * 16:C_PASTS + jb * 16 + 16]
            P.add("dve", lambda e, g=g, pm=pm: e.tensor_tensor(
                g[:], c.ps[6][:, 0:128].rearrange("p (h n) -> p h n", h=8),
                pm.rearrange("p (o n) -> p o n", o=1).to_broadcast([128, 8, 16]), ALU.add),
                reads=[c.psb[6], c.cstb], writes=[g_b])
            for h in range(8):
                P.add("dve", lambda e, g=g, t8=t8, h=h: e.max(t8[:, h, :], g[:, h, :]), reads=[g_b], writes=[g_b])
            P.add("dve", lambda e, g=g, t8=t8, sl=sl: e.tensor_tensor(
                sl[:], g[:], t8[:, :, 2:3].to_broadcast([128, 8, 16]), ALU.is_ge), reads=[g_b], writes=[g_b])
            P.add("dve", lambda e, sl=sl, psel=psel: e.tensor_tensor(
                sl[:], sl[:], psel.rearrange("p (o n) -> p o n", o=1).to_broadcast([128, 8, 16]), ALU.mult),
                reads=[g_b, c.cstb], writes=[g_b])
            P.add("dve", lambda e, sl=sl: e.tensor_scalar(sl[:], sl[:], -1.0, -NEG, ALU.add, ALU.mult),
                  reads=[g_b], writes=[g_b])
            for half in range(2):
                def tr(e, half=half, sl=sl):
                    ins = None
                    for hq in range(4):
                        h = half * 4 + hq
                        ins = e.transpose(c.ps[7][0:16, hq * 128:(hq + 1) * 128], sl[:, h, :], c.ident)
                    return ins
                P.add("pe", tr, reads=[g_b, c.ident_b], writes=[c.psb[7]])
                b, b_b = bT[s % 2], bT_b[s % 2]
                P.add("act", lambda e, b=b, half=half: e.activation(
                    b[:, half * 4:half * 4 + 4, :], c.ps[7][0:16, :].rearrange("p (a q) -> p a q", a=4), AF.Copy),
                    reads=[c.psb[7]], writes=[b_b])
            b, b_b = bT[s % 2], bT_b[s % 2]
            P.dma(biasT_d[:, :, r0 + s * 128:r0 + (s + 1) * 128].rearrange("h n q -> n h q"), b[:],
                  reads=[b_b], sembuf=b_b)
    P.barrier()
    A.release(m0)


def stage_moba_attn(c, qT_d, kT_d, v_d, biasT_d, oT_d, T):
    P, A = c.P, c.A
    m0 = A.mark()
    NBLK = T // 256
    scale = 128 ** -0.5
    kT = [A.alloc([128, T], BF16, "kT") for _ in range(2)]
    qT = [A.alloc([128, T], BF16, "qT") for _ in range(2)]
    vh = [A.alloc([128, T // 128, 128], BF16, "vh") for _ in range(2)]
    bT = [A.alloc([128, T], BF16, "bTh") for _ in range(2)]
    kT_b, qT_b, vh_b, bT_b = (P.bufs_n("kT", 2), P.bufs_n("qT", 2), P.bufs_n("vh", 2), P.bufs_n("bTh", 2))
    for i in range(2):
        P.add("dve", lambda e, i=i: e.memset(bT[i][:], 0.0), writes=[bT_b[i]])
    pT = [A.alloc([128, 256], BF16, "pT") for _ in range(3)]
    pT_b = P.bufs_n("pT", 3)
    rd = [A.alloc([128, 256], F32, "rd") for _ in range(2)]
    rd_b = P.bufs_n("rd", 2)
    oo = [A.alloc([128, 256], BF16, "oo") for _ in range(2)]
    oo_b = P.bufs_n("oo", 2)
    onesb = c.cstbf[:, CB_ONES:CB_ONES + 128]
    pi = 0
    for h in range(8):
        hb = h % 2
        P.dma(kT[hb][:], kT_d[h, :, :], writes=[kT_b[hb]], sembuf=kT_b[hb])
        P.dma(qT[hb][:], qT_d[h, :, :], writes=[qT_b[hb]], sembuf=qT_b[hb])
        P.dma(vh[hb][:], v_d[:, h * 128:(h + 1) * 128].rearrange("(c p) d -> p c d", p=128),
              writes=[vh_b[hb]], sembuf=vh_b[hb])
        P.dma(bT[hb][0:16, :], biasT_d[h, :, :], writes=[bT_b[hb]], sembuf=bT_b[hb])
        for j in range(NBLK):
            q0 = j * 256
            ob_, db_ = 3 + (j % 2), 5 + (j % 2)
            nkt = 2 * j + 2
            for kt in range(nkt):
                sb_ = kt % 3
                n = kt // 2
                own = (n == j)

                def mms(e, kt=kt, n=n, own=own, sb_=sb_, hb=hb, q0=q0):
                    e.matmul(c.ps[sb_][:, 0:256], kT[hb][:, kt * 128:(kt + 1) * 128], qT[hb][:, q0:q0 + 256],
                             start=True, stop=False)
                    if own:
                        return e.matmul(c.ps[sb_][:, 0:256], c.identb,
                                        c.cstbf[:, CB_CM + (kt % 2) * 256:CB_CM + (kt % 2) * 256 + 256],
                                        start=False, stop=True)
                    return e.matmul(c.ps[sb_][:, 0:256], c.cstbf[:, CB_EN + n * 128:CB_EN + (n + 1) * 128],
                                    bT[hb][:, q0:q0 + 256], start=False, stop=True)
                P.add("pe", mms, reads=[kT_b[hb], qT_b[hb], bT_b[hb], c.cstbf_b], writes=[c.psb[sb_]])
                p, p_b = pT[pi % 3], pT_b[pi % 3]
                pi += 1
                P.add("act", lambda e, p=p, sb_=sb_: e.activation(p[:], c.ps[sb_][:, 0:256], AF.Exp, scale=scale),
                      reads=[c.psb[sb_]], writes=[p_b])

                def mmpv(e, kt=kt, p=p, hb=hb, ob_=ob_, db_=db_, nkt=nkt):
                    e.matmul(c.ps[ob_][:, 0:256], vh[hb][:, kt, :], p[:], start=(kt == 0), stop=(kt == nkt - 1))
                    return e.matmul(c.ps[db_][:, 0:256], onesb, p[:], start=(kt == 0), stop=(kt == nkt - 1))
                P.add("pe", mmpv, reads=[vh_b[hb], p_b, c.cstbf_b], writes=[c.psb[ob_], c.psb[db_]])
            r, r_b = rd[j % 2], rd_b[j % 2]
            o, o_b = oo[j % 2], oo_b[j % 2]
            P.add("dve", lambda e, r=r, db_=db_: e.reciprocal(r[:], c.ps[db_][:, 0:256]), reads=[c.psb[db_]], writes=[r_b])
            P.add("dve", lambda e, r=r, o=o, ob_=ob_: e.tensor_tensor(o[:], c.ps[ob_][:, 0:256], r[:], ALU.mult),
                  reads=[c.psb[ob_], r_b], writes=[o_b])
            P.dma(oT_d[h, :, q0:q0 + 256], o[:], reads=[o_b], sembuf=o_b)
    P.barrier()
    A.release(m0)


def stage_outproj_ln(c, actT_d, NK, w, src, dst, ln_g, ln_b, li, si, cres, T):
    P, A = c.P, c.A
    m0 = A.mark()
    TT = 512
    wo = A.alloc([128, NK, D], BF16, "wo")
    wo_b = P.bufs_n("wo", NK)
    w_v = w.rearrange("(j p) d -> p j d", p=128)
    for j in range(NK):
        P.dma(wo[:, j, :], w_v[:, j, :], writes=[wo_b[j]], sembuf=wo_b[j], eng="pool")
    gt, bt, gb_b = load_ln_params(c, ln_g, ln_b, li, si)
    aT = [A.alloc([128, NK, TT], BF16, "oaT") for _ in range(2)]
    aT_b = P.bufs_n("oaT", 2)
    xs = [A.alloc([128, D], F32, "xs") for _ in range(3)]
    xs_b = P.bufs_n("xs", 3)
    lns = alloc_ln_small(c, 2)
    xi = 0
    for t in range(T // TT):
        r0 = t * TT
        a, a_b = aT[t % 2], aT_b[t % 2]
        P.dma(a[:], actT_d[:, :, r0:r0 + TT].rearrange("k p t -> p k t"), writes=[a_b], sembuf=a_b)
        for s in range(4):
            banks = (4, 5) if s % 2 == 0 else (6, 7)
            for hh, bk in enumerate(banks):
                def mmo(e, hh=hh, bk=bk, s=s, a=a):
                    ins = None
                    for j in range(NK):
                        ins = e.matmul(c.ps[bk][:, :], a[:, j, s * 128:(s + 1) * 128],
                                       wo[:, j, hh * 512:(hh + 1) * 512], start=(j == 0), stop=(j == NK - 1))
                    return ins
                P.add("pe", mmo, reads=[a_b] + list(wo_b), writes=[c.psb[bk]])
            x, xb = xs[xi % 3], xs_b[xi % 3]
            xi += 1
            resid_ln(c, src, dst, r0 + s * 128, banks, cres, x, xb, gt, bt, gb_b, lns[s % 2])
    P.barrier()
    A.release(m0)


GQ, GK, GV, GZ, GB_, GA_ = 0, 1024, 2048, 4096, 6144, 6160
GPROJ = 6176


def psbf(c, bk):
    return c.ps[bk][:].bitcast(BF16)


def stage_gdn_proj(c, src, w_in, conv_w, a_log, dt_bias, qT_d, kT_d, ktok_d, vtok_d, z_d, g_d, beta_d, T):
    P, A = c.P, c.A
    m0 = A.mark()
    TT = 512
    win = A.alloc([128, 8, GPROJ], BF16, "gwin")
    NG = 13
    win_b = P.bufs_n("gwin", NG)
    w_in_v = w_in.rearrange("(kc p) n -> p kc n", p=128)
    for g in range(NG):
        lo, hi = g * 512, min(GPROJ, (g + 1) * 512)
        P.dma(win[:, :, lo:hi], w_in_v[:, :, lo:hi], writes=[win_b[g]], sembuf=win_b[g], eng="pool")
    cw = A.alloc([128, 32, 4], F32, "cw")
    cw_b = P.buf("cw")
    cwl = A.alloc([32, 4, 128], F32, "cwl")
    cwl_b = P.buf("cwl")
    P.dma(cwl[:], conv_w.rearrange("i (cc p) -> cc i p", p=128), writes=[cwl_b], sembuf=cwl_b)

    def trcw(e):
        ins = None
        for i in range(4):
            ins = e.transpose(c.ps[7][:, i * 32:(i + 1) * 32], cwl[:, i, :], c.cst[0:32, C_IDENT:C_IDENT + 32])
        return ins
    P.add("pe", trcw, reads=[cwl_b, c.cstb], writes=[c.psb[7]])
    P.add("dve", lambda e: e.tensor_copy(cw[:].rearrange("p cc i -> p i cc"),
                                         c.ps[7][:, 0:128].rearrange("p (i cc) -> p i cc", i=4)),
          reads=[c.psb[7]], writes=[cw_b])
    Wd = A.alloc([128, 32, 4, 128], BF16, "Wd")
    Wd_b = P.buf("Wd")
    for cc in range(32):
        for i in range(4):
            P.add("dve", lambda e, cc=cc, i=i: e.tensor_scalar(Wd[:, cc, i, :], c.ident, cw[:, cc, i:i + 1], None, ALU.mult),
                  reads=[cw_b, c.cstb], writes=[Wd_b])
    negA = A.alloc([128, 16], F32, "negA")
    dtb = A.alloc([128, 16], F32, "dtb")
    gc_b = P.buf("gconst")
    b1, b2 = P.buf("alog"), P.buf("dtb")
    P.dma(negA[:], a_log.partition_broadcast(128), writes=[b1], sembuf=b1)
    P.dma(dtb[:], dt_bias.partition_broadcast(128), writes=[b2], sembuf=b2)
    P.add("act", lambda e: e.activation(negA[:], negA[:], AF.Exp), reads=[b1], writes=[gc_b])
    P.add("dve", lambda e: e.tensor_scalar(negA[:], negA[:], -1.0, None, ALU.mult), reads=[gc_b, b2], writes=[gc_b])
    epsq = A.alloc([128, 2], F32, "epsq")
    P.add("dve", lambda e: e.memset(epsq[:, 0:1], 128.0 * RMS_EPS), writes=[gc_b])
    P.add("dve", lambda e: e.memset(epsq[:, 1:2], RMS_EPS), reads=[gc_b], writes=[gc_b])
    halo = A.alloc([128, 32, 4], BF16, "halo")
    halo_b = P.bufs_n("halo", 32)
    P.add("dve", lambda e: e.memset(halo[:], 0.0), writes=list(halo_b))
    hT = A.alloc([128, 8, TT], BF16, "hT")
    hT_b = P.buf("hT")
    xs = [A.alloc([128, D], F32, "xs") for _ in range(3)]
    xs_b = P.bufs_n("xs", 3)
    xpre = [A.alloc([128, TT + 4], BF16, "xpre") for _ in range(2)]
    xpre_b = P.bufs_n("xpre", 2)
    qs = [A.alloc([128, TT], F32, "qs") for _ in range(2)]
    qs_b = P.bufs_n("qs", 2)
    sq = [A.alloc([128, TT], BF16, "sq") for _ in range(2)]
    sq_b = P.bufs_n("sq", 2)
    rn = [A.alloc([128, TT], F32, "rn") for _ in range(2)]
    rn_b = P.bufs_n("rn", 2)
    ob = [A.alloc([128, TT], BF16, "gob") for _ in range(3)]
    ob_b = P.bufs_n("gob", 3)
    tk = [A.alloc([128, 4, 128], BF16, "tk") for _ in range(3)]
    tk_b = P.bufs_n("tk", 3)
    zt = [A.alloc([128, 512], F32, "zt") for _ in range(3)]
    zt_b = P.bufs_n("zt", 3)
    zi = 0
    sm = [A.alloc([128, 4, 16], F32, "gsm") for _ in range(2)]
    sm_b = P.bufs_n("gsm", 2)
    onesb = c.cstbf[:, CB_ONES:CB_ONES + 128]
    xi = oi = ti = 0
    for t in range(T // TT):
        r0 = t * TT
        for s in range(4):
            x, xb = xs[xi % 3], xs_b[xi % 3]
            xi += 1
            P.dma(x[:], src[r0 + s * 128:r0 + (s + 1) * 128, :], writes=[xb], sembuf=xb)
            transpose_in(c, x, xb, hT, hT_b, s * 128, (0, 1))
        for cc in range(32):
            col = cc * 128
            bk = 2 + (cc % 2)
            bkc = 4 + (cc % 2)

            def mmp(e, col=col, bk=bk):
                ins = None
                for kc in range(8):
                    ins = e.matmul(c.ps[bk][:, :], win[:, kc, col:col + 128], hT[:, kc, :],
                                   start=(kc == 0), stop=(kc == 7))
                return ins
            P.add("pe", mmp, reads=[hT_b, win_b[col // 512]], writes=[c.psb[bk]])
            xp, xp_b = xpre[cc % 2], xpre_b[cc % 2]
            P.add("dve", lambda e, xp=xp, cc=cc: e.tensor_copy(xp[:, 0:4], halo[:, cc, :]), reads=[halo_b[cc]], writes=[xp_b])
            P.add("act", lambda e, xp=xp, bk=bk: e.activation(xp[:, 4:TT + 4], c.ps[bk][:, :], AF.Copy),
                  reads=[c.psb[bk]], writes=[xp_b])
            P.add("dve", lambda e, xp=xp, cc=cc: e.tensor_copy(halo[:, cc, :], xp[:, TT:TT + 4]), reads=[xp_b], writes=[halo_b[cc]])

            def mmc(e, xp=xp, cc=cc, bkc=bkc):
                ins = None
                for i in range(4):
                    ins = e.matmul(c.ps[bkc][:, :], Wd[:, cc, i, :], xp[:, 1 + i:1 + i + TT], start=(i == 0), stop=(i == 3))
                return ins
            P.add("pe", mmc, reads=[xp_b, Wd_b], writes=[c.psb[bkc]])
            o, o_b = ob[oi % 3], ob_b[oi % 3]
            oi += 1
            if cc < 16:
                q_, q_b = qs[cc % 2], qs_b[cc % 2]
                s_, s_b = sq[cc % 2], sq_b[cc % 2]
                r_, r_b = rn[cc % 2], rn_b[cc % 2]
                P.add("act", lambda e, q_=q_, bkc=bkc: e.activation(q_[:], c.ps[bkc][:, :], AF.Silu), reads=[c.psb[bkc]], writes=[q_b])
                P.add("act", lambda e, q_=q_, s_=s_: e.activation(s_[:], q_[:], AF.Square), reads=[q_b], writes=[s_b])
                P.add("pe", lambda e, s_=s_: e.matmul(c.ps[6][:, :], onesb, s_[:], start=True, stop=True),
                      reads=[s_b, c.cstbf_b], writes=[c.psb[6]])
                if cc < 8:
                    P.add("act", lambda e, r_=r_: e.activation(r_[:], c.ps[6][:, :], AF.Sqrt, bias=epsq[:, 0:1], scale=128.0),
                          reads=[c.psb[6], gc_b], writes=[r_b])
                else:
                    P.add("act", lambda e, r_=r_: e.activation(r_[:], c.ps[6][:, :], AF.Sqrt, bias=epsq[:, 1:2], scale=1.0),
                          reads=[c.psb[6], gc_b], writes=[r_b])
                P.add("dve", lambda e, r_=r_: e.reciprocal(r_[:], r_[:]), reads=[r_b], writes=[r_b])
                P.add("dve", lambda e, r_=r_, q_=q_, o=o: e.tensor_tensor(o[:], q_[:], r_[:], ALU.mult),
                      reads=[r_b, q_b], writes=[o_b])
                hd = cc % 8
                P.dma((qT_d if cc < 8 else kT_d)[hd, :, r0:r0 + TT], o[:], reads=[o_b], sembuf=o_b)
            else:
                P.add("act", lambda e, o=o, bkc=bkc: e.activation(o[:], c.ps[bkc][:, :], AF.Silu), reads=[c.psb[bkc]], writes=[o_b])
            if cc >= 8:
                pb = psbf(c, 7)

                def trk(e, o=o, pb=pb):
                    ins = None
                    for s in range(4):
                        ins = e.transpose(pb[:, s * 128:(s + 1) * 128], o[:, s * 128:(s + 1) * 128], c.identb)
                    return ins
                P.add("pe", trk, reads=[o_b, c.cstbf_b], writes=[c.psb[7]])
                k_, k_b = tk[ti % 3], tk_b[ti % 3]
                ti += 1
                P.add("dve", lambda e, k_=k_, pb=pb: e.tensor_copy(k_[:], pb[:, 0:512].rearrange("p (s d) -> p s d", s=4)),
                      reads=[c.psb[7]], writes=[k_b])
                if cc < 16:
                    dd = ktok_d[r0:r0 + TT, (cc - 8) * 128:(cc - 7) * 128]
                else:
                    dd = vtok_d[r0:r0 + TT, (cc - 16) * 128:(cc - 15) * 128]
                P.dma(dd.rearrange("(s p) d -> p s d", p=128), k_[:], reads=[k_b], sembuf=k_b)
        for s in range(4):
            for zq in range(4):
                z_, z_b = zt[zi % 3], zt_b[zi % 3]
                zi += 1
                bk = zq

                def mmz(e, s=s, zq=zq, bk=bk):
                    ins = None
                    for kc in range(8):
                        ins = e.matmul(c.ps[bk][:, :], hT[:, kc, s * 128:(s + 1) * 128],
                                       win[:, kc, GZ + zq * 512:GZ + (zq + 1) * 512], start=(kc == 0), stop=(kc == 7))
                    return ins
                P.add("pe", mmz, reads=[hT_b] + [win_b[(GZ + zq * 512) // 512]], writes=[c.psb[bk]])
                P.add("act", lambda e, z_=z_, bk=bk: e.activation(z_[:], c.ps[bk][:, :], AF.Silu),
                      reads=[c.psb[bk]], writes=[z_b])
                P.dma(z_d[r0 + s * 128:r0 + (s + 1) * 128, zq * 512:(zq + 1) * 512], z_[:], reads=[z_b], sembuf=z_b)

            def mmba(e, s=s):
                ins = None
                for kc in range(8):
                    ins = e.matmul(c.ps[6][:, 0:32], hT[:, kc, s * 128:(s + 1) * 128], win[:, kc, GB_:GB_ + 32],
                                   start=(kc == 0), stop=(kc == 7))
                return ins
            P.add("pe", mmba, reads=[hT_b, win_b[12]], writes=[c.psb[6]])
            m_, m_b = sm[s % 2], sm_b[s % 2]
            P.add("act", lambda e, m_=m_: e.activation(m_[:, 0, :], c.ps[6][:, 0:16], AF.Sigmoid), reads=[c.psb[6]], writes=[m_b])
            P.add("dve", lambda e, m_=m_: e.tensor_tensor(m_[:, 2, :], c.ps[6][:, 16:32], dtb[:], ALU.add),
                  reads=[c.psb[6], b2, gc_b], writes=[m_b])
            P.add("act", lambda e, m_=m_: e.activation(m_[:, 2, :], m_[:, 2, :], AF.Exp), reads=[m_b], writes=[m_b])
            P.add("act", lambda e, m_=m_: e.activation(m_[:, 3, :], m_[:, 2, :], AF.Ln, bias=c.one_col[:], scale=1.0),
                  reads=[m_b, c.cst_b], writes=[m_b])
            P.add("dve", lambda e, m_=m_: e.tensor_tensor(m_[:, 1, :], m_[:, 3, :], negA[:], ALU.mult),
                  reads=[m_b, gc_b], writes=[m_b])
            P.dma(beta_d[r0 + s * 128:r0 + (s + 1) * 128, :], m_[:, 0, :], reads=[m_b], sembuf=m_b)
            P.dma(g_d[r0 + s * 128:r0 + (s + 1) * 128, :], m_[:, 1, :], reads=[m_b], sembuf=m_b)
    P.barrier()
    A.release(m0)


def stage_gdn_scan(c, qT_d, kT_d, ktok_d, vtok_d, g_d, beta_d, o_d, T):
    P, A = c.P, c.A
    m0 = A.mark()
    NCH = T // 128
    H = 16
    UT = c.cst[:, C_UT:C_UT + 128]
    ones32 = c.cst[:, C_ONES:C_ONES + 128]
    SM = c.cst[:, C_SMASK:C_SMASK + 128]
    IMT = c.cst[:, C_IMASKT:C_IMASKT + 128]

    def f32t(name):
        return A.alloc([128, H, 128], F32, name), P.buf(name)

    def bf16t(name):
        return A.alloc([128, H, 128], BF16, name), P.buf(name)
    qTc = [A.alloc([128, 8, 128], BF16, "qTc") for _ in range(2)]
    kTc = [A.alloc([128, 8, 128], BF16, "kTc") for _ in range(2)]
    ktk = [A.alloc([128, 8, 128], BF16, "ktk") for _ in range(2)]
    vtk = [A.alloc([128, H, 128], BF16, "vtk") for _ in range(2)]
    gb = [A.alloc([128, 2, H], F32, "gb") for _ in range(2)]
    qTc_b, kTc_b, ktk_b, vtk_b = P.bufs_n("qTc", 2), P.bufs_n("kTc", 2), P.bufs_n("ktk", 2), P.bufs_n("vtk", 2)
    g_b, be_b = P.bufs_n("gld", 2), P.bufs_n("bld", 2)
    F1, F1b = f32t("F1")
    F2, F2b = f32t("F2")
    F3, F3b = f32t("F3")
    Rp = [f32t("R%d" % i) for i in range(2)]
    RTp = [f32t("RT%d" % i) for i in range(2)]
    Y, Yb = f32t("Y")
    attnT, attnT_b = bf16t("attnT")
    TTb, TTb_b = bf16t("TTb")
    vbt, vbt_b = bf16t("vbt")
    kw, kw_b = bf16t("kw")
    wT, wT_b = bf16t("wT")
    qdT, qdT_b = bf16t("qdT")
    kdec, kdec_b = bf16t("kdec")
    vnew, vnew_b = bf16t("vnew")
    osb, osb_b = f32t("osb")
    S, S_b = f32t("S")
    Sbf, Sbf_b = bf16t("Sbf")
    sm = A.alloc([128, 8, H], F32, "ssm")
    sm_b = P.buf("ssm")
    P.add("dve", lambda e: e.memset(S[:], 0.0), writes=[S_b])
    P.add("dve", lambda e: e.memset(Sbf[:], 0.0), writes=[Sbf_b])

    def flat(t):
        return t[:].rearrange("p h d -> p (h d)")

    def bank4(q):
        return c.ps[q][:].rearrange("p (h d) -> p h d", h=4)

    for ch in range(NCH):
        c0 = ch * 128
        pb = ch % 2
        P.dma(qTc[pb][:], qT_d[:, :, c0:c0 + 128].rearrange("h p t -> p h t"), writes=[qTc_b[pb]], sembuf=qTc_b[pb])
        P.dma(kTc[pb][:], kT_d[:, :, c0:c0 + 128].rearrange("h p t -> p h t"), writes=[kTc_b[pb]], sembuf=kTc_b[pb])
        P.dma(ktk[pb][:], ktok_d[c0:c0 + 128, :].rearrange("p (h d) -> p h d", h=8), writes=[ktk_b[pb]], sembuf=ktk_b[pb])
        P.dma(vtk[pb][:], vtok_d[c0:c0 + 128, :].rearrange("p (h d) -> p h d", h=H), writes=[vtk_b[pb]], sembuf=vtk_b[pb])
        P.dma(gb[pb][:, 0, :], g_d[c0:c0 + 128, :], writes=[g_b[pb]], sembuf=g_b[pb])
        P.dma(gb[pb][:, 1, :], beta_d[c0:c0 + 128, :], writes=[be_b[pb]], sembuf=be_b[pb])
        g = gb[pb][:, 0, :]
        beta = gb[pb][:, 1, :]
        def mm1(e, g=g):
            e.matmul(c.ps[0][:, 0:16], UT, g, start=True, stop=True)
            return e.matmul(c.ps[0][:, 16:32], ones32, g, start=True, stop=True)
        P.add("pe", mm1, reads=[g_b[pb], c.cstb], writes=[c.psb[0]])
        P.add("act", lambda e: e.activation(sm[:, 0, :], c.ps[0][:, 0:16], AF.Copy), reads=[c.psb[0]], writes=[sm_b])
        P.add("act", lambda e: e.activation(sm[:, 1, :], c.ps[0][:, 0:16], AF.Identity, scale=-1.0), reads=[c.psb[0]], writes=[sm_b])
        P.add("act", lambda e: e.activation(sm[:, 2, :], c.ps[0][:, 0:16], AF.Exp), reads=[c.psb[0]], writes=[sm_b])
        P.add("act", lambda e: e.activation(sm[:, 3, :], c.ps[0][:, 16:32], AF.Exp), reads=[c.psb[0]], writes=[sm_b])
        P.add("dve", lambda e: e.tensor_tensor(sm[:, 7, :], c.ps[0][:, 16:32], sm[:, 0, :], ALU.subtract),
              reads=[c.psb[0], sm_b], writes=[sm_b])
        P.add("act", lambda e: e.activation(sm[:, 4, :], sm[:, 7, :], AF.Exp), reads=[sm_b], writes=[sm_b])
        P.add("dve", lambda e, beta=beta: e.tensor_tensor(sm[:, 5, :], beta, sm[:, 2, :], ALU.mult),
              reads=[sm_b, be_b[pb]], writes=[sm_b])
        P.add("dve", lambda e, beta=beta: e.tensor_scalar(sm[:, 6, :], beta, -1.0, None, ALU.mult),
              reads=[sm_b, be_b[pb]], writes=[sm_b])
        P.add("dve", lambda e, g=g: e.tensor_tensor(
            F1[:], UT.rearrange("p (o i) -> p o i", o=1).to_broadcast([128, H, 128]),
            g.rearrange("p (h o) -> p h o", o=1).to_broadcast([128, H, 128]), ALU.mult),
            reads=[g_b[pb], c.cstb], writes=[F1b])
        for q in range(4):
            P.add("pe", lambda e, q=q: e.matmul(c.ps[4 + q][:, :], ones32, F1[:, 4 * q:4 * q + 4, :].rearrange("p h d -> p (h d)"),
                                                start=True, stop=True), reads=[F1b, c.cstb], writes=[c.psb[4 + q]])
        gcrow_b = [c.psb[4 + q] for q in range(4)]
        for q in range(4):
            P.add("act", lambda e, q=q: e.activation(F3[:, 4 * q:4 * q + 4, :], bank4(4 + q), AF.Exp),
                  reads=[gcrow_b[q]], writes=[F3b])
        for par in range(2):
            P.add("dve", lambda e, par=par, pb=pb: e.tensor_tensor(qdT[:, par::2, :], qTc[pb][:], F3[:, par::2, :], ALU.mult),
                  reads=[F3b, qTc_b[pb]], writes=[qdT_b])
        for q in range(4):
            P.add("dve", lambda e, q=q: e.tensor_tensor(
                F2[:, 4 * q:4 * q + 4, :], bank4(4 + q),
                IMT.rearrange("p (o i) -> p o i", o=1).to_broadcast([128, 4, 128]), ALU.add),
                reads=[gcrow_b[q], c.cstb], writes=[F2b])
        for h in range(H):
            P.add("act", lambda e, h=h: e.activation(F2[:, h, :], F2[:, h, :], AF.Exp, bias=sm[:, 1, h:h + 1], scale=1.0),
                  reads=[F2b, sm_b], writes=[F2b])
        def mmkk(e, pb=pb):
            ins = None
            for hk in range(8):
                ins = e.matmul(c.ps[hk // 4][:, (hk % 4) * 128:(hk % 4 + 1) * 128], kTc[pb][:, hk, :], kTc[pb][:, hk, :],
                               start=True, stop=True)
            return ins
        P.add("pe", mmkk, reads=[kTc_b[pb]], writes=[c.psb[0], c.psb[1]])

        def mmqk(e, pb=pb):
            ins = None
            for hk in range(8):
                ins = e.matmul(c.ps[2 + hk // 4][:, (hk % 4) * 128:(hk % 4 + 1) * 128], kTc[pb][:, hk, :], qTc[pb][:, hk, :],
                               start=True, stop=True)
            return ins
        P.add("pe", mmqk, reads=[kTc_b[pb], qTc_b[pb]], writes=[c.psb[2], c.psb[3]])
        for par in range(2):
            for hf in range(2):
                P.add("dve", lambda e, par=par, hf=hf: e.tensor_tensor(
                    attnT[:, 8 * hf + par:8 * hf + 8:2, :], bank4(2 + hf), F2[:, 8 * hf + par:8 * hf + 8:2, :], ALU.mult),
                    reads=[c.psb[2 + hf], F2b], writes=[attnT_b])
        for q in range(4):
            P.add("dve", lambda e, q=q: e.tensor_tensor(
                F2[:, 4 * q:4 * q + 4, :], SM.rearrange("p (o i) -> p o i", o=1).to_broadcast([128, 4, 128]),
                bank4(4 + q), ALU.subtract),
                reads=[gcrow_b[q], c.cstb, attnT_b], writes=[F2b])
        for h in range(H):
            P.add("act", lambda e, h=h: e.activation(F2[:, h, :], F2[:, h, :], AF.Exp, bias=sm[:, 0, h:h + 1], scale=1.0),
                  reads=[F2b, sm_b], writes=[F2b])
        for h in range(H):
            hk = h // 2
            P.add("dve", lambda e, h=h, hk=hk: e.scalar_tensor_tensor(
                F1[:, h, :], c.ps[hk // 4][:, (hk % 4) * 128:(hk % 4 + 1) * 128], sm[:, 6, h:h + 1], F2[:, h, :],
                ALU.mult, ALU.mult), reads=[c.psb[hk // 4], sm_b, F2b], writes=[F1b])
        R, Rb = Rp[0]
        for hf in range(4):
            def trm(e, hf=hf):
                ins = None
                for hq in range(4):
                    h = hf * 4 + hq
                    ins = e.transpose(c.ps[hf][:, hq * 128:(hq + 1) * 128], F1[:, h, :], c.ident)
                return ins
            P.add("pe", trm, reads=[F1b, c.cstb], writes=[c.psb[hf]])
            P.add("act", lambda e, hf=hf, R=R: e.activation(R[:, 4 * hf:4 * hf + 4, :], bank4(hf), AF.Copy),
                  reads=[c.psb[hf]], writes=[Rb])
            P.add("dve", lambda e, hf=hf: e.tensor_tensor(
                Y[:, 4 * hf:4 * hf + 4, :], bank4(hf),
                c.ident.rearrange("p (o i) -> p o i", o=1).to_broadcast([128, 4, 128]), ALU.add),
                reads=[c.psb[hf], c.cstb], writes=[Yb])
        RT, RTb = F1, F1b
        NLEV = 6
        for lev in range(1, NLEV + 1):
            Rn, Rnb = Rp[lev % 2]
            RTn, RTnb = RTp[lev % 2]
            last = (lev == NLEV)
            for hf in range(4):
                hs = slice(4 * hf, 4 * hf + 4)
                def mmrt(e, hf=hf, R=R, RT=RT):
                    ins = None
                    for hq in range(4):
                        h = hf * 4 + hq
                        ins = e.matmul(c.ps[0 + (hf % 2)][:, hq * 128:(hq + 1) * 128], R[:, h, :], RT[:, h, :], start=True, stop=True)
                    return ins
                P.add("pe", mmrt, reads=[Rb, RTb], writes=[c.psb[0 + (hf % 2)]])
                P.add("act", lambda e, hf=hf, RTn=RTn, hs=hs: e.activation(RTn[:, hs, :], bank4(0 + (hf % 2)), AF.Copy),
                      reads=[c.psb[0 + (hf % 2)]], writes=[RTnb])
                if not last:
                    def mmr(e, hf=hf, R=R, RT=RT):
                        ins = None
                        for hq in range(4):
                            h = hf * 4 + hq
                            ins = e.matmul(c.ps[2 + (hf % 2)][:, hq * 128:(hq + 1) * 128], RT[:, h, :], R[:, h, :], start=True, stop=True)
                        return ins
                    P.add("pe", mmr, reads=[Rb, RTb], writes=[c.psb[2 + (hf % 2)]])
                    P.add("dve", lambda e, hf=hf, Rn=Rn, hs=hs: e.tensor_copy(Rn[:, hs, :], bank4(2 + (hf % 2))),
                          reads=[c.psb[2 + (hf % 2)]], writes=[Rnb])
            for hf in range(4):
                hs = slice(4 * hf, 4 * hf + 4)
                def mmy(e, hf=hf, RTn=RTn):
                    ins = None
                    for hq in range(4):
                        h = hf * 4 + hq
                        ins = e.matmul(c.ps[4 + (hf % 2)][:, hq * 128:(hq + 1) * 128], RTn[:, h, :], Y[:, h, :], start=True, stop=True)
                    return ins
                P.add("pe", mmy, reads=[RTnb, Yb], writes=[c.psb[4 + (hf % 2)]])
                P.add("dve", lambda e, hf=hf, hs=hs: e.tensor_tensor(Y[:, hs, :], bank4(4 + (hf % 2)), Y[:, hs, :], ALU.add),
                      reads=[c.psb[4 + (hf % 2)], Yb], writes=[Yb])
            R, Rb = Rn, Rnb
            RT, RTb = RTn, RTnb
        P.add("act", lambda e: e.activation(flat(TTb), flat(Y), AF.Copy), reads=[Yb], writes=[TTb_b])
        P.add("dve", lambda e, pb=pb, beta=beta: e.tensor_tensor(
            vbt[:], vtk[pb][:], beta.rearrange("p (h o) -> p h o", o=1).to_broadcast([128, H, 128]), ALU.mult),
            reads=[vtk_b[pb], be_b[pb]], writes=[vbt_b])
        for par in range(2):
            P.add("dve", lambda e, par=par, pb=pb: e.tensor_tensor(
                kw[:, par::2, :], ktk[pb][:],
                sm[:, 5, par::2].rearrange("p (h o) -> p h o", o=1).to_broadcast([128, 8, 128]), ALU.mult),
                reads=[ktk_b[pb], sm_b], writes=[kw_b])
            P.add("dve", lambda e, par=par, pb=pb: e.tensor_tensor(
                kdec[:, par::2, :], ktk[pb][:],
                sm[:, 4, par::2].rearrange("p (h o) -> p h o", o=1).to_broadcast([128, 8, 128]), ALU.mult),
                reads=[ktk_b[pb], sm_b], writes=[kdec_b])
        for hf in range(4):
            def mmu(e, hf=hf):
                ins = None
                for hq in range(4):
                    h = hf * 4 + hq
                    ins = e.matmul(c.ps[hf][:, hq * 128:(hq + 1) * 128], TTb[:, h, :], vbt[:, h, :], start=True, stop=True)
                return ins
            P.add("pe", mmu, reads=[TTb_b, vbt_b], writes=[c.psb[hf]])
            P.add("act", lambda e, hf=hf: e.activation(F3[:, 4 * hf:4 * hf + 4, :], bank4(hf), AF.Copy),
                  reads=[c.psb[hf], qdT_b], writes=[F3b])

            def mmw(e, hf=hf):
                ins = None
                for hq in range(4):
                    h = hf * 4 + hq
                    ins = e.matmul(c.ps[4 + hf][:, hq * 128:(hq + 1) * 128], kw[:, h, :], TTb[:, h, :], start=True, stop=True)
                return ins
            P.add("pe", mmw, reads=[TTb_b, kw_b], writes=[c.psb[4 + hf]])
            P.add("dve", lambda e, hf=hf: e.tensor_copy(wT[:, 4 * hf:4 * hf + 4, :], bank4(4 + hf)),
                  reads=[c.psb[4 + hf]], writes=[wT_b])
        for hf in range(4):
            def mmws(e, hf=hf):
                ins = None
                for hq in range(4):
                    h = hf * 4 + hq
                    ins = e.matmul(c.ps[hf][:, hq * 128:(hq + 1) * 128], wT[:, h, :], Sbf[:, h, :], start=True, stop=True)
                return ins
            P.add("pe", mmws, reads=[wT_b, Sbf_b], writes=[c.psb[hf]])
            P.add("dve", lambda e, hf=hf: e.tensor_tensor(vnew[:, 4 * hf:4 * hf + 4, :], F3[:, 4 * hf:4 * hf + 4, :], bank4(hf), ALU.subtract),
                  reads=[c.psb[hf], F3b], writes=[vnew_b])
        for hf in range(4):
            def mmo(e, hf=hf):
                ins = None
                for hq in range(4):
                    h = hf * 4 + hq
                    e.matmul(c.ps[4 + hf][:, hq * 128:(hq + 1) * 128], qdT[:, h, :], Sbf[:, h, :], start=True, stop=False)
                    ins = e.matmul(c.ps[4 + hf][:, hq * 128:(hq + 1) * 128], attnT[:, h, :], vnew[:, h, :], start=False, stop=True)
                return ins
            P.add("pe", mmo, reads=[qdT_b, Sbf_b, attnT_b, vnew_b], writes=[c.psb[4 + hf]])
            P.add("act", lambda e, hf=hf: e.activation(osb[:, 4 * hf:4 * hf + 4, :], bank4(4 + hf), AF.Copy),
                  reads=[c.psb[4 + hf]], writes=[osb_b])
        P.dma(o_d[c0:c0 + 128, :], flat(osb), reads=[osb_b], sembuf=osb_b)
        for hf in range(4):
            def mmds(e, hf=hf):
                ins = None
                for hq in range(4):
                    h = hf * 4 + hq
                    ins = e.matmul(c.ps[hf][:, hq * 128:(hq + 1) * 128], kdec[:, h, :], vnew[:, h, :], start=True, stop=True)
                return ins
            P.add("pe", mmds, reads=[kdec_b, vnew_b], writes=[c.psb[hf]])
            for hq in range(4):
                h = hf * 4 + hq
                P.add("dve", lambda e, h=h, hf=hf, hq=hq: e.scalar_tensor_tensor(
                    S[:, h, :], S[:, h, :], sm[:, 3, h:h + 1], c.ps[hf][:, hq * 128:(hq + 1) * 128], ALU.mult, ALU.add),
                    reads=[c.psb[hf], sm_b, S_b], writes=[S_b])
        P.add("act", lambda e: e.activation(flat(Sbf), flat(S), AF.Copy), reads=[S_b], writes=[Sbf_b])
    P.barrier()
    A.release(m0)


def stage_gdn_out(c, o_d, z_d, norm_w, w_out, src, dst, ln_g, ln_b, li, si, T):
    P, A = c.P, c.A
    m0 = A.mark()
    H = 16
    wo = A.alloc([128, H, D], BF16, "gwo")
    wo_b = P.bufs_n("gwo", H)
    w_v = w_out.rearrange("(j p) d -> p j d", p=128)
    for j in range(H):
        P.dma(wo[:, j, :], w_v[:, j, :], writes=[wo_b[j]], sembuf=wo_b[j], eng="pool")
    gt, bt, gb_b = load_ln_params(c, ln_g, ln_b, li, si)
    nw = A.alloc([128, 128], F32, "nw")
    nw_b = P.buf("nw")
    P.dma(nw[:], norm_w.partition_broadcast(128), writes=[nw_b], sembuf=nw_b)
    epsr = A.alloc([128, 1], F32, "epsr")
    P.add("dve", lambda e: e.memset(epsr[:], RMS_EPS), writes=[nw_b], reads=[nw_b])
    ot = [A.alloc([128, H, 128], F32, "ot") for _ in range(2)]
    zt = [A.alloc([128, H, 128], F32, "zt2") for _ in range(2)]
    ot_b, zt_b = P.bufs_n("ot", 2), P.bufs_n("zt2", 2)
    sqt = A.alloc([128, H, 128], F32, "sqt")
    sqt_b = P.buf("sqt")
    ssm = [A.alloc([128, 2, H], F32, "gsm2") for _ in range(2)]
    ssm_b = P.bufs_n("gsm2", 2)
    onb = [A.alloc([128, H, 128], BF16, "onb") for _ in range(2)]
    onb_b = P.bufs_n("onb", 2)
    onT = [A.alloc([128, H, 128], BF16, "onT") for _ in range(2)]
    onT_b = P.bufs_n("onT", 2)
    xs = [A.alloc([128, D], F32, "xs") for _ in range(3)]
    xs_b = P.bufs_n("xs", 3)
    lns = alloc_ln_small(c, 2)
    for s in range(T // 128):
        r0 = s * 128
        p2 = s % 2
        o, o_b, z, z_b = ot[p2], ot_b[p2], zt[p2], zt_b[p2]
        P.dma(o[:].rearrange("p h d -> p (h d)"), o_d[r0:r0 + 128, :], writes=[o_b], sembuf=o_b)
        P.dma(z[:].rearrange("p h d -> p (h d)"), z_d[r0:r0 + 128, :], writes=[z_b], sembuf=z_b)
        sm_, sm_b = ssm[p2], ssm_b[p2]
        P.add("act", lambda e, o=o: e.activation(sqt[:], o[:], AF.Square), reads=[o_b], writes=[sqt_b])
        P.add("dve", lambda e, sm_=sm_: e.tensor_reduce(sm_[:, 0, :], sqt[:], AX.X, ALU.add), reads=[sqt_b], writes=[sm_b])
        P.add("act", lambda e, sm_=sm_: e.activation(sm_[:, 1, :], sm_[:, 0, :], AF.Sqrt, bias=epsr[:], scale=1.0 / 128),
              reads=[sm_b, nw_b], writes=[sm_b])
        P.add("dve", lambda e, sm_=sm_: e.reciprocal(sm_[:, 1, :], sm_[:, 1, :]), reads=[sm_b], writes=[sm_b])
        P.add("dve", lambda e, z=z: e.tensor_tensor(
            z[:], z[:], nw[:].rearrange("p (o d) -> p o d", o=1).to_broadcast([128, H, 128]), ALU.mult),
            reads=[z_b, nw_b], writes=[z_b])
        on, on_b = onb[p2], onb_b[p2]
        for h in range(H):
            P.add("dve", lambda e, h=h, o=o, z=z, on=on, sm_=sm_: e.scalar_tensor_tensor(
                on[:, h, :], o[:, h, :], sm_[:, 1, h:h + 1], z[:, h, :], ALU.mult, ALU.mult),
                reads=[o_b, z_b, sm_b], writes=[on_b])
        oT, oT_b = onT[p2], onT_b[p2]
        for hf in range(2):
            pbv = psbf(c, hf)

            def tr(e, hf=hf, on=on, pbv=pbv):
                ins = None
                for hq in range(8):
                    ins = e.transpose(pbv[:, hq * 128:(hq + 1) * 128], on[:, hf * 8 + hq, :], c.identb)
                return ins
            P.add("pe", tr, reads=[on_b, c.cstbf_b], writes=[c.psb[hf]])
            if hf == 0:
                P.add("act", lambda e, oT=oT, pbv=pbv: e.activation(
                    oT[:, 0:8, :], pbv.rearrange("p (h d) -> p h d", h=8), AF.Copy), reads=[c.psb[0]], writes=[oT_b])
            else:
                P.add("dve", lambda e, oT=oT, pbv=pbv: e.tensor_copy(
                    oT[:, 8:16, :], pbv.rearrange("p (h d) -> p h d", h=8)), reads=[c.psb[1]], writes=[oT_b])
        banks = (4, 5) if s % 2 == 0 else (6, 7)
        for hh, bk in enumerate(banks):
            def mmo(e, hh=hh, bk=bk, oT=oT):
                ins = None
                for j in range(H):
                    ins = e.matmul(c.ps[bk][:, :], oT[:, j, :], wo[:, j, hh * 512:(hh + 1) * 512],
                                   start=(j == 0), stop=(j == H - 1))
                return ins
            P.add("pe", mmo, reads=[oT_b] + list(wo_b), writes=[c.psb[bk]])
        x, xb = xs[s % 3], xs_b[s % 3]
        resid_ln(c, src, dst, r0, banks, 1.0 / ALPHA, x, xb, gt, bt, gb_b, lns[s % 2])
    P.barrier()
    A.release(m0)


def build_program(T=SEQ):
    nc = bass.Bass("TRN2", target_bir_lowering=False)
    din = lambda n, s, d=F32: nc.dram_tensor(n, s, d, kind="ExternalInput").ap()
    dsc = lambda n, s, d=F32: nc.dram_tensor(n, s, d, kind="Internal").ap()
    x = din("x", [T, D])
    ln_g = din("ln_g", [2, 3, D])
    ln_b = din("ln_b", [2, 3, D])
    fpre_in = din("ffn_pre_w_in", [2, D, 2 * DFF])
    fpre_out = din("ffn_pre_w_out", [2, DFF, D])
    fpost_in = din("ffn_post_w_in", [2, D, 2 * DFF])
    fpost_out = din("ffn_post_w_out", [2, DFF, D])
    m_in = din("moba_w_in", [1, D, 3 * D])
    m_out = din("moba_w_out", [1, D, D])
    g_in = din("gdn_w_in", [1, D, GPROJ])
    g_conv = din("gdn_conv_w", [1, 4, 4096])
    g_alog = din("gdn_a_log", [1, 16])
    g_dtb = din("gdn_dt_bias", [1, 16])
    g_nw = din("gdn_norm_w", [1, 128])
    g_out = din("gdn_w_out", [1, 2048, D])
    consts = din("consts", [128, NCONST])
    rope = din("rope", [2, 128, T])
    y = nc.dram_tensor("y", [T, D], F32, kind="ExternalOutput").ap()
    hA = dsc("hA", [T, D])
    hB = dsc("hB", [T, D])
    qT_d = dsc("qT_d", [8, 128, T], BF16)
    kT_d = dsc("kT_d", [8, 128, T], BF16)
    v_d = dsc("v_d", [T, D], BF16)
    bT_d = dsc("bT_d", [8, 16, T], BF16)
    oT_d = dsc("oT_d", [8, 128, T], BF16)
    ktok_d = dsc("ktok_d", [T, 1024], BF16)
    vtok_d = dsc("vtok_d", [T, 2048], BF16)
    z_d = dsc("z_d", [T, 2048])
    gg_d = dsc("gg_d", [T, 16])
    beta_d = dsc("beta_d", [T, 16])
    o_d = dsc("o_d", [T, 2048])
    c = make_ctx(nc)
    load_small_consts(c)
    load_consts_full(c, consts)
    c.P.barrier()
    stage_ffn(c, x, hA, fpre_in[0], fpre_out[0], ln_g, ln_b, 0, 0, T)
    stage_moba_proj(c, hA, m_in[0], rope, qT_d, kT_d, v_d, bT_d, T)
    stage_moba_attn(c, qT_d, kT_d, v_d, bT_d, oT_d, T)
    stage_outproj_ln(c, oT_d, 8, m_out[0], hA, hB, ln_g, ln_b, 0, 1, 1.0 / ALPHA, T)
    stage_ffn(c, hB, hA, fpost_in[0], fpost_out[0], ln_g, ln_b, 0, 2, T)
    stage_ffn(c, hA, hB, fpre_in[1], fpre_out[1], ln_g, ln_b, 1, 0, T)
    stage_gdn_proj(c, hB, g_in[0], g_conv[0], g_alog, g_dtb, qT_d, kT_d, ktok_d, vtok_d, z_d, gg_d, beta_d, T)
    stage_gdn_scan(c, qT_d, kT_d, ktok_d, vtok_d, gg_d, beta_d, o_d, T)
    stage_gdn_out(c, o_d, z_d, g_nw, g_out[0], hB, hA, ln_g, ln_b, 1, 1, T)
    stage_ffn(c, hA, y, fpost_in[1], fpost_out[1], ln_g, ln_b, 1, 2, T)
    c.P.emit()
    return nc


_CACHE = {}


def kernel(x, ln_g, ln_b, ffn_pre_w_in, ffn_pre_w_out, ffn_post_w_in, ffn_post_w_out,
           moba_w_in, moba_w_out, gdn_w_in, gdn_conv_w, gdn_a_log, gdn_dt_bias, gdn_norm_w, gdn_w_out):
    B, T, _ = x.shape
    if "nc" not in _CACHE:
        _CACHE["nc"] = build_program(T)
    nc = _CACHE["nc"]
    f = lambda a: np.ascontiguousarray(np.asarray(a, dtype=np.float32))
    shared = dict(ln_g=f(ln_g), ln_b=f(ln_b), ffn_pre_w_in=f(ffn_pre_w_in), ffn_pre_w_out=f(ffn_pre_w_out),
                  ffn_post_w_in=f(ffn_post_w_in), ffn_post_w_out=f(ffn_post_w_out), moba_w_in=f(moba_w_in),
                  moba_w_out=f(moba_w_out), gdn_w_in=f(gdn_w_in), gdn_conv_w=f(gdn_conv_w), gdn_a_log=f(gdn_a_log),
                  gdn_dt_bias=f(gdn_dt_bias), gdn_norm_w=f(gdn_norm_w), gdn_w_out=f(gdn_w_out),
                  consts=make_consts(), rope=make_rope(T))
    xs = f(x)
    in_maps = [dict(shared, x=xs[b]) for b in range(B)]
    res = run_bass_kernel_spmd(nc, in_maps, core_ids=list(range(B)))
    return np.stack([np.asarray(r["y"], dtype=np.float32) for r in res.results], axis=0)
```

```python
import math
import numpy as np
import concourse.bass as bass
import concourse.mybir as mybir
from concourse.bass_utils import run_bass_kernel_spmd

F32 = mybir.dt.float32
BF16 = mybir.dt.bfloat16
AF = mybir.ActivationFunctionType
ALU = mybir.AluOpType
AX = mybir.AxisListType

D = 1024
DFF = 2816
SEQ = 4096
NB = 8
ALPHA = (2 * 2) ** 0.25
LN_EPS = 1e-5
RMS_EPS = 1e-6


class Buf:
    __slots__ = ("name", "last_w", "readers", "dsem", "dcount", "excl")

    def __init__(self, name):
        self.name = name
        self.excl = False
        self.last_w = None
        self.readers = []
        self.dsem = None
        self.dcount = 0


class Op:
    __slots__ = ("eng", "fn", "deps", "sig", "need_sig", "is_dma", "dbuf", "dval", "idx", "dslot")

    def __init__(self, eng, fn, is_dma=False):
        self.eng = eng
        self.fn = fn
        self.deps = []
        self.sig = None
        self.need_sig = False
        self.is_dma = is_dma
        self.dbuf = None
        self.dval = 0
        self.idx = 0
        self.dslot = None


ENGS = ("pe", "act", "dve", "pool", "sp")


class Prog:
    def __init__(self, nc):
        self.nc = nc
        self.ops = {e: [] for e in ENGS}
        self.bufs = []
        self.dma_bufs = []
        self.nops = 0
        self.slots = []
        self.free_slots = []
        self.free_slots_sw = []

    def buf(self, name):
        b = Buf(name)
        self.bufs.append(b)
        return b

    def bufs_n(self, name, n):
        return [self.buf("%s%d" % (name, i)) for i in range(n)]

    def _track(self, op, reads, writes):
        seen = set()
        for b in reads:
            w = b.last_w
            if w is not None and id(w) not in seen:
                seen.add(id(w))
                op.deps.append((w, True))
            if b.excl:
                for r in b.readers:
                    if id(r) not in seen:
                        seen.add(id(r))
                        op.deps.append((r, False))
                b.readers = []
        for b in writes:
            w = b.last_w
            if w is not None and id(w) not in seen:
                seen.add(id(w))
                op.deps.append((w, False))
            for r in b.readers:
                if id(r) not in seen:
                    seen.add(id(r))
                    op.deps.append((r, False))
        for b in reads:
            b.readers.append(op)
        for b in writes:
            b.last_w = op
            b.readers = []

    def add(self, eng, fn, reads=(), writes=()):
        op = Op(eng, fn)
        self._track(op, reads, writes)
        op.idx = self.nops
        self.nops += 1
        self.ops[eng].append(op)
        return op

    def dma(self, out_ap, in_ap, reads=(), writes=(), sembuf=None, eng="sp"):
        op = Op(eng, None, is_dma=True)
        op.fn = (out_ap, in_ap)
        self._track(op, reads, writes)
        kind = 1 if eng == "pool" else 0
        if sembuf.dsem is None:
            fl = self.free_slots_sw if kind else self.free_slots
            if fl:
                sembuf.dsem = fl.pop()
            else:
                sembuf.dsem = [0, None, kind]
                self.slots.append(sembuf.dsem)
            self.dma_bufs.append(sembuf)
        assert sembuf.dsem[2] == kind, "mixing SW/HW DGE on one semaphore"
        sembuf.dsem[0] += 16
        op.dbuf = sembuf
        op.dslot = sembuf.dsem
        op.dval = sembuf.dsem[0]
        op.idx = self.nops
        self.nops += 1
        self.ops[eng].append(op)
        return op

    def barrier(self):
        lasts = []
        for e in ENGS:
            for o in reversed(self.ops[e]):
                if not o.is_dma and o.fn is not None:
                    lasts.append(o)
                    break
        dmas = [(b.dsem, b.dsem[0]) for b in self.dma_bufs]
        for b in self.dma_bufs:
            (self.free_slots_sw if b.dsem[2] else self.free_slots).append(b.dsem)
            b.dsem = None
        self.dma_bufs = []
        for e in ENGS:
            op = Op(e, None)
            op.deps = [(o, True) for o in lasts]
            op.dval = dmas
            op.idx = self.nops
            self.nops += 1
            self.ops[e].append(op)
        for b in self.bufs:
            b.last_w = None
            b.readers = []

    def emit(self):
        nc = self.nc
        for e in ENGS:
            for op in self.ops[e]:
                for d, raw in op.deps:
                    if d.is_dma:
                        continue
                    if d.eng != op.eng or (raw and op.eng != "pe") or op.fn is None or op.is_dma:
                        d.need_sig = True
        for e in ENGS:
            c = 0
            for op in self.ops[e]:
                if op.need_sig:
                    c += 1
                    op.sig = c
        sems = {e: nc.alloc_semaphore("s_" + e) for e in ENGS}
        for i, sl in enumerate(self.slots):
            sl[1] = nc.alloc_semaphore("dsem%d" % i)
        engobj = {"pe": nc.tensor, "act": nc.scalar, "dve": nc.vector, "pool": nc.gpsimd, "sp": nc.sync}
        self.nwaits = 0
        with nc.Block() as block:
            def run(e, eng):
                seen = {}
                for op in self.ops[e]:
                    waits = {}
                    for d, raw in op.deps:
                        if d.is_dma:
                            key = ("d", id(d.dslot))
                            if waits.get(key, (None, 0))[1] < d.dval:
                                waits[key] = (d.dslot[1], d.dval)
                        else:
                            if (d.eng == op.eng and not (raw and op.eng != "pe")
                                    and op.fn is not None and not op.is_dma):
                                continue
                            key = ("e", d.eng)
                            if waits.get(key, (None, 0))[1] < d.sig:
                                waits[key] = (sems[d.eng], d.sig)
                    if op.fn is None and not op.is_dma:
                        for sl, v in op.dval:
                            waits[("d", id(sl))] = (sl[1], v)
                    for key, (s, v) in waits.items():
                        if seen.get(key, 0) >= v:
                            continue
                        seen[key] = v
                        eng.wait_ge(s, v)
                        self.nwaits += 1
                    if op.is_dma:
                        o, i = op.fn
                        eng.dma_start(out=o, in_=i).then_inc(op.dslot[1], 16)
                    elif op.fn is not None:
                        ins = op.fn(eng)
                        if op.need_sig:
                            ins.then_inc(sems[e], 1)

            @block.tensor
            def _(eng):
                run("pe", eng)

            @block.scalar
            def _(eng):
                run("act", eng)

            @block.vector
            def _(eng):
                run("dve", eng)

            @block.gpsimd
            def _(eng):
                run("pool", eng)

            @block.sync
            def _(eng):
                run("sp", eng)


class Arena:
    def __init__(self, nc, base=16512, limit=229344):
        self.nc = nc
        self.top = base
        self.limit = limit
        self.n = 0

    def mark(self):
        return self.top

    def release(self, m):
        self.top = m

    def alloc(self, shape, dtype, name=None):
        nbytes = int(np.prod(shape[1:])) * (4 if dtype == F32 else 2)
        off = (self.top + 63) // 64 * 64
        assert off + nbytes <= self.limit, ("SBUF arena overflow", name, off, nbytes)
        self.top = off + nbytes
        self.n += 1
        t = self.nc.alloc_sbuf_tensor_at("%s_%d" % (name or "t", self.n), list(shape), dtype, offset=off)
        return t


class Ctx:
    pass


def make_ctx(nc):
    c = Ctx()
    c.nc = nc
    c.P = Prog(nc)
    c.A = Arena(nc)
    c.ps = [nc.alloc_psum_tensor("psb%d" % i, [128, 512], F32) for i in range(8)]
    c.psb = c.P.bufs_n("psb", 8)
    for b in c.psb:
        b.excl = True
    return c


def load_consts(c, consts_ap):
    P, A = c.P, c.A
    c.ident = A.alloc([128, 128], F32, "ident")
    c.ident_b = P.buf("ident")
    P.dma(c.ident[:], consts_ap[:, 0:128], writes=[c.ident_b], sembuf=c.ident_b)
    c.identb = A.alloc([128, 128], BF16, "identb")
    c.identb_b = P.buf("identb")
    P.add("dve", lambda e: e.tensor_copy(c.identb[:], c.ident[:]), reads=[c.ident_b], writes=[c.identb_b])


def bcast_rows(ap2d, nrows_part=128):
    return ap2d.partition_broadcast(nrows_part)


def ln_tile(c, xr, xr_b, gt, bt, gb_b, dst_ap, st):
    P = c.P
    eps = LN_EPS / (ALPHA * ALPHA)
    stats, mv, rstd, nmr = st["stats"], st["mv"], st["rstd"], st["nmr"]
    sb = st["b"]
    P.add("dve", lambda e: e.bn_stats(stats[:, 0, :], xr[:, 0:512]), reads=[xr_b], writes=[sb])
    P.add("dve", lambda e: e.bn_stats(stats[:, 1, :], xr[:, 512:1024]), reads=[xr_b], writes=[sb])
    P.add("dve", lambda e: e.bn_aggr(mv[:], stats[:].rearrange("p a b -> p (a b)")), reads=[sb], writes=[sb])
    P.add("act", lambda e: e.activation(rstd[:], mv[:, 1:2], AF.Sqrt, bias=c.epsln[:], scale=1.0),
          reads=[sb, c.cst_b], writes=[sb])
    P.add("dve", lambda e: e.reciprocal(rstd[:], rstd[:]), reads=[sb], writes=[sb])
    P.add("dve", lambda e: e.tensor_scalar(nmr[:], mv[:, 0:1], -1.0, rstd[:], ALU.mult, ALU.mult),
          reads=[sb], writes=[sb])
    P.add("act", lambda e: e.activation(xr[:], xr[:], AF.Identity, bias=nmr[:], scale=rstd[:]),
          reads=[sb, xr_b], writes=[xr_b])
    P.add("dve", lambda e: e.tensor_tensor(xr[:], xr[:], gt[:], ALU.mult), reads=[xr_b, gb_b], writes=[xr_b])
    P.add("dve", lambda e: e.tensor_tensor(xr[:], xr[:], bt[:], ALU.add), reads=[xr_b, gb_b], writes=[xr_b])
    P.dma(dst_ap, xr[:], reads=[xr_b], sembuf=xr_b)


def alloc_ln_small(c, n=2):
    out = []
    for i in range(n):
        st = {
            "stats": c.A.alloc([128, 2, 6], F32, "stats"),
            "mv": c.A.alloc([128, 2], F32, "mv"),
            "rstd": c.A.alloc([128, 1], F32, "rstd"),
            "nmr": c.A.alloc([128, 1], F32, "nmr"),
            "b": c.P.buf("lnsmall%d" % i),
        }
        out.append(st)
    return out


def load_ln_params(c, ln_g_ap, ln_b_ap, li, si):
    P, A = c.P, c.A
    gt = A.alloc([128, 1024], F32, "lng")
    bt = A.alloc([128, 1024], F32, "lnb")
    gb_b = P.buf("lngb")
    b2 = P.buf("lngb2")
    P.dma(gt[:], ln_g_ap[li, si:si + 1, :].partition_broadcast(128), writes=[gb_b], sembuf=gb_b)
    P.dma(bt[:], ln_b_ap[li, si:si + 1, :].partition_broadcast(128), writes=[b2], sembuf=b2)
    P.add("dve", lambda e: e.tensor_copy(bt[:, 0:1], bt[:, 0:1]), reads=[b2, gb_b], writes=[gb_b])
    return gt, bt, gb_b


def transpose_in(c, xs, xs_b, hT, hT_b, col0, banks):
    P = c.P
    for half in range(2):
        bk = banks[half]
        ps, psb = c.ps[bk], c.psb[bk]

        def f(e, half=half, ps=ps):
            ins = None
            for q in range(4):
                kc = half * 4 + q
                ins = e.transpose(ps[:, q * 128:(q + 1) * 128], xs[:, kc * 128:(kc + 1) * 128], c.ident[:])
            return ins
        P.add("pe", f, reads=[xs_b, c.ident_b], writes=[psb])
        eng = "act" if half == 0 else "dve"
        if eng == "act":
            P.add("act", lambda e, half=half, ps=ps: e.activation(
                hT[:, half * 4:half * 4 + 4, col0:col0 + 128],
                ps[:].rearrange("p (a b) -> p a b", a=4), AF.Copy),
                reads=[psb], writes=[hT_b])
        else:
            P.add("dve", lambda e, half=half, ps=ps: e.tensor_copy(
                hT[:, half * 4:half * 4 + 4, col0:col0 + 128],
                ps[:].rearrange("p (a b) -> p a b", a=4)),
                reads=[psb], writes=[hT_b])


def stage_ffn(c, src, dst, w_in, w_out, ln_g, ln_b, li, si, T):
    P, A, nc = c.P, c.A, c.nc
    m0 = A.mark()
    TT = 512
    NJ = DFF // 128
    win = A.alloc([128, 8, 2 * DFF], BF16, "win")
    wout = A.alloc([128, NJ, D], BF16, "wout")
    NWG = 11
    win_b = P.bufs_n("win", NWG)
    wout_b = P.bufs_n("wout", NJ)
    w_in_v = w_in.rearrange("(kc p) n -> p kc n", p=128)
    w_out_v = w_out.rearrange("(j p) d -> p j d", p=128)
    CW = 2 * DFF // NWG
    order = []
    for g in range(NWG // 2 + 1):
        for gg in (g, g + (NWG + 1) // 2):
            if gg < NWG and gg not in order:
                order.append(gg)
    for g in order:
        P.dma(win[:, :, g * CW:(g + 1) * CW], w_in_v[:, :, g * CW:(g + 1) * CW],
              writes=[win_b[g]], sembuf=win_b[g], eng="pool")
    for j in range(NJ):
        P.dma(wout[:, j, :], w_out_v[:, j, :], writes=[wout_b[j]], sembuf=wout_b[j], eng="pool")
    gt, bt, gb_b = load_ln_params(c, ln_g, ln_b, li, si)
    hT = A.alloc([128, 8, TT], BF16, "hT")
    hT_b = P.buf("hT")
    aT = A.alloc([128, NJ, TT], BF16, "aT")
    aT_b = P.bufs_n("aT", NJ)
    NX = 3
    xs = [A.alloc([128, D], F32, "xs") for _ in range(NX)]
    xs_b = P.bufs_n("xs", NX)
    sg = [A.alloc([128, TT], F32, "sg") for _ in range(2)]
    sg_b = P.bufs_n("sg", 2)
    lns = alloc_ln_small(c, 2)
    xi = 0
    cres = 0.5 / ALPHA
    for t in range(T // TT):
        r0 = t * TT
        for s in range(TT // 128):
            x, xb = xs[xi % NX], xs_b[xi % NX]
            xi += 1
            P.dma(x[:], src[r0 + s * 128:r0 + (s + 1) * 128, :], writes=[xb], sembuf=xb)
            transpose_in(c, x, xb, hT, hT_b, s * 128, (0, 1))
        if getattr(c, "cut", 9) <= 1:
            continue
        for j in range(NJ):
            gcol = j * 128
            ucol = DFF + j * 128
            bg, bu = (0, 1) if j % 2 == 0 else (2, 3)

            def mmg(e, col=gcol, bk=bg):
                ins = None
                for kc in range(8):
                    ins = e.matmul(c.ps[bk][:, 0:TT], win[:, kc, col:col + 128], hT[:, kc, :],
                                   start=(kc == 0), stop=(kc == 7))
                return ins
            P.add("pe", mmg, reads=[hT_b, win_b[gcol // CW]], writes=[c.psb[bg]])
            P.add("pe", lambda e, col=ucol, bk=bu: mmg(e, col, bk), reads=[hT_b, win_b[ucol // CW]],
                  writes=[c.psb[bu]])
            s_, s_b = sg[j % 2], sg_b[j % 2]
            P.add("act", lambda e, bk=bg, s_=s_: e.activation(s_[:], c.ps[bk][:, 0:TT], AF.Silu),
                  reads=[c.psb[bg]], writes=[s_b])
            P.add("dve", lambda e, bk=bu, s_=s_, j=j: e.tensor_tensor(aT[:, j, :], c.ps[bk][:, 0:TT], s_[:], ALU.mult),
                  reads=[c.psb[bu], s_b], writes=[aT_b[j]])
        if getattr(c, "cut", 9) <= 2:
            continue
        for s in range(TT // 128):
            bk0, bk1 = (4, 5) if s % 2 == 0 else (6, 7)
            for hh, bk in enumerate((bk0, bk1)):
                def mmo(e, hh=hh, bk=bk, s=s):
                    ins = None
                    for j in range(NJ):
                        ins = e.matmul(c.ps[bk][:, :], aT[:, j, s * 128:(s + 1) * 128],
                                       wout[:, j, hh * 512:(hh + 1) * 512], start=(j == 0), stop=(j == NJ - 1))
                    return ins
                P.add("pe", mmo, reads=list(aT_b) + list(wout_b), writes=[c.psb[bk]])
            if getattr(c, "cut", 9) <= 3:
                continue
            x, xb = xs[xi % NX], xs_b[xi % NX]
            xi += 1
            P.dma(x[:], src[r0 + s * 128:r0 + (s + 1) * 128, :], writes=[xb], sembuf=xb)
            for hh, bk in enumerate((bk0, bk1)):
                P.add("dve", lambda e, hh=hh, bk=bk, x=x: e.scalar_tensor_tensor(
                    x[:, hh * 512:(hh + 1) * 512], c.ps[bk][:, :], cres, x[:, hh * 512:(hh + 1) * 512],
                    ALU.mult, ALU.add), reads=[c.psb[bk], xb], writes=[xb])
            if getattr(c, "cut", 9) <= 4:
                P.dma(dst[r0 + s * 128:r0 + (s + 1) * 128, :], x[:], reads=[xb], sembuf=xb)
                continue
            ln_tile(c, x, xb, gt, bt, gb_b, dst[r0 + s * 128:r0 + (s + 1) * 128, :], lns[s % 2])
    P.barrier()
    A.release(m0)


def load_small_consts(c):
    P, A = c.P, c.A
    c.epsln = A.alloc([128, 1], F32, "epsln")
    c.cst_b = P.buf("cst")
    P.add("dve", lambda e: e.memset(c.epsln[:], LN_EPS / (ALPHA * ALPHA)), writes=[c.cst_b])
    c.one_col = A.alloc([128, 1], F32, "onecol")
    P.add("dve", lambda e: e.memset(c.one_col[:], 1.0), writes=[c.cst_b])


C_IDENT = 0
C_ROT = 128
C_PASTM = 256
C_PASTS = 512
C_ONES = 768
C_UT = 896
C_SMASK = 1024
C_IMASKT = 1152
NF32 = 1280
CB_IDENT = 0
CB_CM = 128
CB_EN = 640
CB_ONES = 2688
NBF = 2816
NCONST = NF32 + NBF
NEG = -30000.0


def make_consts():
    c = np.zeros((128, NCONST), np.float32)
    c[:, C_IDENT:C_IDENT + 128] = np.eye(128)
    rot = np.zeros((128, 128), np.float32)
    for p in range(64):
        rot[p, p + 64] = 1.0
        rot[p + 64, p] = -1.0
    c[:, C_ROT:C_ROT + 128] = rot
    k = np.arange(128)[:, None]
    q = np.arange(256)[None, :]
    c[:, NF32 + CB_CM:NF32 + CB_CM + 256] = np.where(q >= k, 0.0, NEG)
    c[:, NF32 + CB_CM + 256:NF32 + CB_CM + 512] = np.where(q >= k + 128, 0.0, NEG)
    en = np.zeros((128, 16, 128), np.float32)
    for n in range(16):
        en[n, n, :] = 1.0
    c[:, NF32 + CB_EN:NF32 + CB_EN + 2048] = en.reshape(128, 2048)
    j = np.arange(16)[:, None]
    n = np.arange(16)[None, :]
    c[:, C_PASTM:C_PASTM + 256] = np.where(n < j, 0.0, -1e30).reshape(1, 256)
    c[:, C_PASTS:C_PASTS + 256] = np.where(n < j, 1.0, 0.0).reshape(1, 256)
    c[:, C_ONES:C_ONES + 128] = 1.0
    c[:, NF32 + CB_ONES:NF32 + CB_ONES + 128] = 1.0
    c[:, NF32 + CB_IDENT:NF32 + CB_IDENT + 128] = np.eye(128)
    a = np.arange(128)
    c[:, C_UT:C_UT + 128] = (a[:, None] <= a[None, :]).astype(np.float32)
    c[:, C_SMASK:C_SMASK + 128] = np.where(a[:, None] > a[None, :], 0.0, NEG)
    c[:, C_IMASKT:C_IMASKT + 128] = np.where(a[None, :] >= a[:, None], 0.0, NEG)
    return c


def make_rope(T):
    half = 64
    inv = (10000.0 ** (-np.arange(half, dtype=np.float32) / np.float32(half))).astype(np.float32)
    ang = (np.arange(T, dtype=np.float32)[None, :] * inv[:, None]).astype(np.float32)
    cs = np.cos(ang).astype(np.float32)
    sn = np.sin(ang).astype(np.float32)
    out = np.zeros((2, 128, T), np.float32)
    out[0, :64] = cs
    out[0, 64:] = cs
    out[1, :64] = sn
    out[1, 64:] = sn
    return out


def load_consts_full(c, consts_ap):
    P, A = c.P, c.A
    c.cst = A.alloc([128, NF32], F32, "cst")
    c.cstb = P.buf("cstf")
    P.dma(c.cst[:], consts_ap[:, 0:NF32], writes=[c.cstb], sembuf=c.cstb)
    c.ident = c.cst[:, C_IDENT:C_IDENT + 128]
    c.ident_b = c.cstb
    c.cstbf = A.alloc([128, NBF], BF16, "cstbf")
    c.cstbf_b = P.buf("cstbf")
    m = A.mark()
    tmp = A.alloc([128, NBF], F32, "csttmp")
    tb = P.buf("csttmp")
    P.dma(tmp[:], consts_ap[:, NF32:NCONST], writes=[tb], sembuf=tb)
    P.add("dve", lambda e: e.tensor_copy(c.cstbf[:], tmp[:]), reads=[tb], writes=[c.cstbf_b])
    A.release(m)
    c.identb = c.cstbf[:, CB_IDENT:CB_IDENT + 128]
    c.identb_b = c.cstbf_b


def pipeline(n, steps, skew=1):
    for it in range(n + (len(steps) - 1) * skew):
        for k, f in enumerate(steps):
            i = it - k * skew
            if 0 <= i < n:
                f(i)


class _V:
    def __init__(self, ap):
        self.ap = ap

    def __getitem__(self, k):
        return self.ap[k]


def resid_ln(c, src, dst, r0, banks, cres, x, xb, gt, bt, gb_b, st):
    P = c.P
    P.dma(x[:], src[r0:r0 + 128, :], writes=[xb], sembuf=xb)
    for hh, bk in enumerate(banks):
        P.add("dve", lambda e, hh=hh, bk=bk: e.scalar_tensor_tensor(
            x[:, hh * 512:(hh + 1) * 512], c.ps[bk][:, :], cres, x[:, hh * 512:(hh + 1) * 512],
            ALU.mult, ALU.add), reads=[c.psb[bk], xb], writes=[xb])
    ln_tile(c, x, xb, gt, bt, gb_b, dst[r0:r0 + 128, :], st)


def stage_moba_proj(c, src, w_in, rope, qT_d, kT_d, v_d, biasT_d, T):
    P, A = c.P, c.A
    m0 = A.mark()
    TT = 512
    win = A.alloc([128, 8, 3 * D], BF16, "mwin")
    win_b = P.bufs_n("mwin", 6)
    w_in_v = w_in.rearrange("(kc p) n -> p kc n", p=128)
    for g in (2, 3, 0, 1, 4, 5):
        P.dma(win[:, :, g * 512:(g + 1) * 512], w_in_v[:, :, g * 512:(g + 1) * 512],
              writes=[win_b[g]], sembuf=win_b[g], eng="pool")
    hT = A.alloc([128, 8, TT], BF16, "hT")
    hT_b = P.buf("hT")
    NX = 3
    xs = [A.alloc([128, D], F32, "xs") for _ in range(NX)]
    xs_b = P.bufs_n("xs", NX)
    cs = [A.alloc([128, 2, TT], F32, "cs") for _ in range(2)]
    cs_b = P.bufs_n("cs", 2)
    qf = [A.alloc([128, TT], F32, "qf") for _ in range(2)]
    qf_b = P.bufs_n("qf", 2)
    t1 = [A.alloc([128, TT], F32, "t1") for _ in range(2)]
    t1_b = P.bufs_n("t1", 2)
    kr32 = [A.alloc([128, TT], F32, "kr32") for _ in range(2)]
    kr32_b = P.bufs_n("kr32", 2)
    q32 = A.alloc([128, 8, TT], F32, "q32")
    q32_b = P.bufs_n("q32", 8)
    ob = [A.alloc([128, TT], BF16, "ob") for _ in range(3)]
    ob_b = P.bufs_n("ob", 3)
    vb = [A.alloc([128, D], BF16, "vb") for _ in range(2)]
    vb_b = P.bufs_n("vb", 2)
    kmean = A.alloc([128, 8, 16], F32, "kmean")
    kmean_b = P.buf("kmean")
    P.add("dve", lambda e: e.memset(kmean[:], 0.0), writes=[kmean_b])
    gm = [A.alloc([128, 8, 16], F32, "gm") for _ in range(2)]
    gm_b = P.bufs_n("gm", 2)
    top8 = [A.alloc([128, 8, 8], F32, "top8") for _ in range(2)]
    selt = [A.alloc([128, 8, 16], F32, "selt") for _ in range(2)]
    bT = [A.alloc([16, 8, 128], BF16, "bT") for _ in range(2)]
    bT_b = P.bufs_n("bT", 2)
    xi = 0
    oi = 0
    for t in range(T // TT):
        r0 = t * TT
        for s in range(4):
            x, xb = xs[xi % NX], xs_b[xi % NX]
            xi += 1
            P.dma(x[:], src[r0 + s * 128:r0 + (s + 1) * 128, :], writes=[xb], sembuf=xb)
            transpose_in(c, x, xb, hT, hT_b, s * 128, (0, 1))
        ct, ctb = cs[t % 2], cs_b[t % 2]
        P.dma(ct[:], rope[:, :, r0:r0 + TT].rearrange("a p t -> p a t"), writes=[ctb], sembuf=ctb)
        it = 0
        for qk in (1, 0):
            for h in range(8):
                col = qk * D + h * 128
                bk = 2 + (it % 2)
                bkr = 4 + (it % 2)

                def mmp(e, col=col, bk=bk):
                    ins = None
                    for kc in range(8):
                        ins = e.matmul(c.ps[bk][:, :], win[:, kc, col:col + 128], hT[:, kc, :],
                                       start=(kc == 0), stop=(kc == 7))
                    return ins
                P.add("pe", mmp, reads=[hT_b, win_b[col // 512]], writes=[c.psb[bk]])
                f, fb = qf[it % 2], qf_b[it % 2]
                P.add("act", lambda e, f=f, bk=bk: e.activation(f[:], c.ps[bk][:, :], AF.Copy),
                      reads=[c.psb[bk]], writes=[fb])
                P.add("pe", lambda e, f=f, bkr=bkr: e.matmul(c.ps[bkr][:, :], c.cst[:, C_ROT:C_ROT + 128], f[:],
                                                            start=True, stop=True),
                      reads=[fb, c.cstb], writes=[c.psb[bkr]])
                tt, ttb = t1[it % 2], t1_b[it % 2]
                P.add("dve", lambda e, tt=tt, f=f, ct=ct: e.tensor_tensor(tt[:], f[:], ct[:, 0, :], ALU.mult),
                      reads=[fb, ctb], writes=[ttb])
                if qk == 1:
                    dst32, dst32_b = kr32[it % 2], kr32_b[it % 2]
                    d32 = dst32[:]
                else:
                    d32, dst32_b = q32[:, h, :], q32_b[h]
                P.add("dve", lambda e, d32=d32, bkr=bkr, ct=ct: e.tensor_tensor(d32, c.ps[bkr][:, :], ct[:, 1, :], ALU.mult),
                      reads=[c.psb[bkr], ctb], writes=[dst32_b])
                P.add("dve", lambda e, d32=d32, tt=tt: e.tensor_tensor(d32, d32, tt[:], ALU.add),
                      reads=[ttb, dst32_b], writes=[dst32_b])
                o, o_b = ob[oi % 3], ob_b[oi % 3]
                oi += 1
                P.add("act", lambda e, o=o, d32=d32: e.activation(o[:], d32, AF.Copy), reads=[dst32_b], writes=[o_b])
                if qk == 1:
                    P.add("dve", lambda e, d32=d32, h=h, t=t: e.tensor_reduce(
                        kmean[:, h, 2 * t:2 * t + 2], d32.rearrange("p (a b) -> p a b", a=2), AX.X, ALU.add),
                        reads=[dst32_b], writes=[kmean_b])
                    P.dma(kT_d[h, :, r0:r0 + TT], o[:], reads=[o_b], sembuf=o_b)
                else:
                    P.dma(qT_d[h, :, r0:r0 + TT], o[:], reads=[o_b], sembuf=o_b)
                it += 1
        for s in range(4):
            for hh in range(2):
                def mmv(e, s=s, hh=hh):
                    ins = None
                    for kc in range(8):
                        ins = e.matmul(c.ps[hh][:, :], hT[:, kc, s * 128:(s + 1) * 128],
                                       win[:, kc, 2 * D + hh * 512:2 * D + (hh + 1) * 512],
                                       start=(kc == 0), stop=(kc == 7))
                    return ins
                P.add("pe", mmv, reads=[hT_b, win_b[4 + hh]], writes=[c.psb[hh]])
            v, v_b = vb[s % 2], vb_b[s % 2]
            P.add("act", lambda e, v=v: e.activation(v[:, 0:512], c.ps[0][:, :], AF.Copy),
                  reads=[c.psb[0]], writes=[v_b])
            P.add("dve", lambda e, v=v: e.tensor_copy(v[:, 512:1024], c.ps[1][:, :]),
                  reads=[c.psb[1]], writes=[v_b])
            P.dma(v_d[r0 + s * 128:r0 + (s + 1) * 128, :], v[:], reads=[v_b], sembuf=v_b)
        for s in range(4):
            jb = (r0 + s * 128) // 256
            g, g_b = gm[s % 2], gm_b[s % 2]
            t8, sl = top8[s % 2], selt[s % 2]

            def mmgate(e, s=s):
                ins = None
                for h in range(8):
                    ins = e.matmul(c.ps[6][:, h * 16:(h + 1) * 16], q32[:, h, s * 128:(s + 1) * 128],
                                   kmean[:, h, :], start=True, stop=True)
                return ins
            P.add("pe", mmgate, reads=list(q32_b) + [kmean_b], writes=[c.psb[6]])
            pm = c.cst[:, C_PASTM + jb * 16:C_PASTM + jb * 16 + 16]
            psel = c.cst[:, C_PASTS + jb * 16:C_PASTS + jb * 16 + 16]
            P.add("dve", lambda e, g=g, pm=pm: e.tensor_tensor(
                g[:], c.ps[6][:, 0:128].rearrange("p (h n) -> p h n", h=8),
                pm.rearrange("p (o n) -> p o n", o=1).to_broadcast([128, 8, 16]), ALU.add),
                reads=[c.psb[6], c.cstb], writes=[g_b])
            for h in range(8):
                P.add("dve", lambda e, g=g, t8=t8, h=h: e.max(t8[:, h, :], g[:, h, :]), reads=[g_b], writes=[g_b])
            P.add("dve", lambda e, g=g, t8=t8, sl=sl: e.tensor_tensor(
                sl[:], g[:], t8[:, :, 2:3].to_broadcast([128, 8, 16]), ALU.is_ge), reads=[g_b], writes=[g_b])
            P.add("dve", lambda e, sl=sl, psel=psel: e.tensor_tensor(
                sl[:], sl[:], psel.rearrange("p (o n) -> p o n", o=1).to_broadcast([128, 8, 16]), ALU.mult),
                reads=[g_b, c.cstb], writes=[g_b])
            P.add("dve", lambda e, sl=sl: e.tensor_scalar(sl[:], sl[:], -1.0, -NEG, ALU.add, ALU.mult),
                  reads=[g_b], writes=[g_b])
            for half in range(2):
                def tr(e, half=half, sl=sl):
                    ins = None
                    for hq in range(4):
                        h = half * 4 + hq
                        ins = e.transpose(c.ps[7][0:16, hq * 128:(hq + 1) * 128], sl[:, h, :], c.ident)
                    return ins
                P.add("pe", tr, reads=[g_b, c.ident_b], writes=[c.psb[7]])
                b, b_b = bT[s % 2], bT_b[s % 2]
                P.add("act", lambda e, b=b, half=half: e.activation(
                    b[:, half * 4:half * 4 + 4, :], c.ps[7][0:16, :].rearrange("p (a q) -> p a q", a=4), AF.Copy),
                    reads=[c.psb[7]], writes=[b_b])
            b, b_b = bT[s % 2], bT_b[s % 2]
            P.dma(biasT_d[:, :, r0 + s * 128:r0 + (s + 1) * 128].rearrange("h n q -> n h q"), b[:],
                  reads=[b_b], sembuf=b_b)
    P.barrier()
    A.release(m0)


def stage_moba_attn(c, qT_d, kT_d, v_d, biasT_d, oT_d, T):
    P, A = c.P, c.A
    m0 = A.mark()
    NBLK = T // 256
    scale = 128 ** -0.5
    kT = [A.alloc([128, T], BF16, "kT") for _ in range(2)]
    qT = [A.alloc([128, T], BF16, "qT") for _ in range(2)]
    vh = [A.alloc([128, T // 128, 128], BF16, "vh") for _ in range(2)]
    bT = [A.alloc([128, T], BF16, "bTh") for _ in range(2)]
    kT_b, qT_b, vh_b, bT_b = (P.bufs_n("kT", 2), P.bufs_n("qT", 2), P.bufs_n("vh", 2), P.bufs_n("bTh", 2))
    for i in range(2):
        P.add("dve", lambda e, i=i: e.memset(bT[i][:], 0.0), writes=[bT_b[i]])
    pT = [A.alloc([128, 256], BF16, "pT") for _ in range(3)]
    pT_b = P.bufs_n("pT", 3)
    rd = [A.alloc([128, 256], F32, "rd") for _ in range(2)]
    rd_b = P.bufs_n("rd", 2)
    oo = [A.alloc([128, 256], BF16, "oo") for _ in range(2)]
    oo_b = P.bufs_n("oo", 2)
    onesb = c.cstbf[:, CB_ONES:CB_ONES + 128]
    for h in range(8):
        hb = h % 2
        P.dma(kT[hb][:], kT_d[h, :, :], writes=[kT_b[hb]], sembuf=kT_b[hb])
        P.dma(qT[hb][:], qT_d[h, :, :], writes=[qT_b[hb]], sembuf=qT_b[hb])
        P.dma(vh[hb][:], v_d[:, h * 128:(h + 1) * 128].rearrange("(c p) d -> p c d", p=128),
              writes=[vh_b[hb]], sembuf=vh_b[hb])
        P.dma(bT[hb][0:16, :], biasT_d[h, :, :], writes=[bT_b[hb]], sembuf=bT_b[hb])
        pairs = [(j, kt) for j in range(NBLK) for kt in range(2 * j + 2)]

        def step_s(idx, hb=hb):
            j, kt = pairs[idx]
            q0 = j * 256
            sb_ = idx % 3
            n = kt // 2
            own = (n == j)

            def mms(e):
                e.matmul(c.ps[sb_][:, 0:256], kT[hb][:, kt * 128:(kt + 1) * 128], qT[hb][:, q0:q0 + 256],
                         start=True, stop=False)
                if own:
                    return e.matmul(c.ps[sb_][:, 0:256], c.identb,
                                    c.cstbf[:, CB_CM + (kt % 2) * 256:CB_CM + (kt % 2) * 256 + 256],
                                    start=False, stop=True)
                return e.matmul(c.ps[sb_][:, 0:256], c.cstbf[:, CB_EN + n * 128:CB_EN + (n + 1) * 128],
                                bT[hb][:, q0:q0 + 256], start=False, stop=True)
            P.add("pe", mms, reads=[kT_b[hb], qT_b[hb], bT_b[hb], c.cstbf_b], writes=[c.psb[sb_]])
            p, p_b = pT[idx % 3], pT_b[idx % 3]
            P.add("act", lambda e: e.activation(p[:], c.ps[sb_][:, 0:256], AF.Exp, scale=scale),
                  reads=[c.psb[sb_]], writes=[p_b])

        def step_pv(idx, hb=hb, h=h):
            j, kt = pairs[idx]
            q0 = j * 256
            nkt = 2 * j + 2
            ob_, db_ = 3 + (j % 2), 5 + (j % 2)
            p, p_b = pT[idx % 3], pT_b[idx % 3]

            def mmpv(e):
                e.matmul(c.ps[ob_][:, 0:256], vh[hb][:, kt, :], p[:], start=(kt == 0), stop=(kt == nkt - 1))
                return e.matmul(c.ps[db_][:, 0:256], onesb, p[:], start=(kt == 0), stop=(kt == nkt - 1))
            P.add("pe", mmpv, reads=[vh_b[hb], p_b, c.cstbf_b], writes=[c.psb[ob_], c.psb[db_]])
            if kt == nkt - 1:
                r, r_b = rd[j % 2], rd_b[j % 2]
                o, o_b = oo[j % 2], oo_b[j % 2]
                P.add("dve", lambda e: e.reciprocal(r[:], c.ps[db_][:, 0:256]), reads=[c.psb[db_]], writes=[r_b])
                P.add("dve", lambda e: e.tensor_tensor(o[:], c.ps[ob_][:, 0:256], r[:], ALU.mult),
                      reads=[c.psb[ob_], r_b], writes=[o_b])
                P.dma(oT_d[h, :, q0:q0 + 256], o[:], reads=[o_b], sembuf=o_b)
        pipeline(len(pairs), [step_s, step_pv])
    P.barrier()
    A.release(m0)


def stage_outproj_ln(c, actT_d, NK, w, src, dst, ln_g, ln_b, li, si, cres, T):
    P, A = c.P, c.A
    m0 = A.mark()
    TT = 512
    wo = A.alloc([128, NK, D], BF16, "wo")
    wo_b = P.bufs_n("wo", NK)
    w_v = w.rearrange("(j p) d -> p j d", p=128)
    for j in range(NK):
        P.dma(wo[:, j, :], w_v[:, j, :], writes=[wo_b[j]], sembuf=wo_b[j], eng="pool")
    gt, bt, gb_b = load_ln_params(c, ln_g, ln_b, li, si)
    aT = [A.alloc([128, NK, TT], BF16, "oaT") for _ in range(2)]
    aT_b = P.bufs_n("oaT", 2)
    xs = [A.alloc([128, D], F32, "xs") for _ in range(3)]
    xs_b = P.bufs_n("xs", 3)
    lns = alloc_ln_small(c, 2)
    xi = 0
    for t in range(T // TT):
        r0 = t * TT
        a, a_b = aT[t % 2], aT_b[t % 2]
        P.dma(a[:], actT_d[:, :, r0:r0 + TT].rearrange("k p t -> p k t"), writes=[a_b], sembuf=a_b)
        for s in range(4):
            banks = (4, 5) if s % 2 == 0 else (6, 7)
            for hh, bk in enumerate(banks):
                def mmo(e, hh=hh, bk=bk, s=s, a=a):
                    ins = None
                    for j in range(NK):
                        ins = e.matmul(c.ps[bk][:, :], a[:, j, s * 128:(s + 1) * 128],
                                       wo[:, j, hh * 512:(hh + 1) * 512], start=(j == 0), stop=(j == NK - 1))
                    return ins
                P.add("pe", mmo, reads=[a_b] + list(wo_b), writes=[c.psb[bk]])
            x, xb = xs[xi % 3], xs_b[xi % 3]
            xi += 1
            resid_ln(c, src, dst, r0 + s * 128, banks, cres, x, xb, gt, bt, gb_b, lns[s % 2])
    P.barrier()
    A.release(m0)


GQ, GK, GV, GZ, GB_, GA_ = 0, 1024, 2048, 4096, 6144, 6160
GPROJ = 6176


def psbf(c, bk):
    return c.ps[bk][:].bitcast(BF16)


def stage_gdn_proj(c, src, w_in, conv_w, a_log, dt_bias, qT_d, kT_d, ktok_d, vtok_d, z_d, g_d, beta_d, T):
    P, A = c.P, c.A
    m0 = A.mark()
    TT = 512
    win = A.alloc([128, 8, GPROJ], BF16, "gwin")
    NG = 13
    win_b = P.bufs_n("gwin", NG)
    w_in_v = w_in.rearrange("(kc p) n -> p kc n", p=128)
    for g in range(NG):
        lo, hi = g * 512, min(GPROJ, (g + 1) * 512)
        P.dma(win[:, :, lo:hi], w_in_v[:, :, lo:hi], writes=[win_b[g]], sembuf=win_b[g], eng="pool")
    cw = A.alloc([128, 32, 4], F32, "cw")
    cw_b = P.buf("cw")
    cwl = A.alloc([32, 4, 128], F32, "cwl")
    cwl_b = P.buf("cwl")
    P.dma(cwl[:], conv_w.rearrange("i (cc p) -> cc i p", p=128), writes=[cwl_b], sembuf=cwl_b)

    def trcw(e):
        ins = None
        for i in range(4):
            ins = e.transpose(c.ps[7][:, i * 32:(i + 1) * 32], cwl[:, i, :], c.cst[0:32, C_IDENT:C_IDENT + 32])
        return ins
    P.add("pe", trcw, reads=[cwl_b, c.cstb], writes=[c.psb[7]])
    P.add("dve", lambda e: e.tensor_copy(cw[:].rearrange("p cc i -> p i cc"),
                                         c.ps[7][:, 0:128].rearrange("p (i cc) -> p i cc", i=4)),
          reads=[c.psb[7]], writes=[cw_b])
    Wd = A.alloc([128, 32, 4, 128], BF16, "Wd")
    Wd_b = P.buf("Wd")
    for cc in range(32):
        for i in range(4):
            P.add("dve", lambda e, cc=cc, i=i: e.tensor_scalar(Wd[:, cc, i, :], c.ident, cw[:, cc, i:i + 1], None, ALU.mult),
                  reads=[cw_b, c.cstb], writes=[Wd_b])
    negA = A.alloc([128, 16], F32, "negA")
    dtb = A.alloc([128, 16], F32, "dtb")
    gc_b = P.buf("gconst")
    b1, b2 = P.buf("alog"), P.buf("dtb")
    P.dma(negA[:], a_log.partition_broadcast(128), writes=[b1], sembuf=b1)
    P.dma(dtb[:], dt_bias.partition_broadcast(128), writes=[b2], sembuf=b2)
    P.add("act", lambda e: e.activation(negA[:], negA[:], AF.Exp), reads=[b1], writes=[gc_b])
    P.add("dve", lambda e: e.tensor_scalar(negA[:], negA[:], -1.0, None, ALU.mult), reads=[gc_b, b2], writes=[gc_b])
    epsq = A.alloc([128, 2], F32, "epsq")
    P.add("dve", lambda e: e.memset(epsq[:, 0:1], 128.0 * RMS_EPS), writes=[gc_b])
    P.add("dve", lambda e: e.memset(epsq[:, 1:2], RMS_EPS), reads=[gc_b], writes=[gc_b])
    halo = A.alloc([128, 32, 4], BF16, "halo")
    halo_b = P.bufs_n("halo", 32)
    P.add("dve", lambda e: e.memset(halo[:], 0.0), writes=list(halo_b))
    hT = A.alloc([128, 8, TT], BF16, "hT")
    hT_b = P.buf("hT")
    xs = [A.alloc([128, D], F32, "xs") for _ in range(3)]
    xs_b = P.bufs_n("xs", 3)
    NXP, NQS, NOB = 3, 3, 5
    xpre = [A.alloc([128, TT + 4], BF16, "xpre") for _ in range(NXP)]
    xpre_b = P.bufs_n("xpre", NXP)
    qs = [A.alloc([128, TT], F32, "qs") for _ in range(NQS)]
    qs_b = P.bufs_n("qs", NQS)
    sq = [A.alloc([128, TT], BF16, "sq") for _ in range(NQS)]
    sq_b = P.bufs_n("sq", NQS)
    rn = [A.alloc([128, TT], F32, "rn") for _ in range(2)]
    rn_b = P.bufs_n("rn", 2)
    ob = [A.alloc([128, TT], BF16, "gob") for _ in range(NOB)]
    ob_b = P.bufs_n("gob", NOB)
    tk = [A.alloc([128, 4, 128], BF16, "tk") for _ in range(3)]
    tk_b = P.bufs_n("tk", 3)
    zt = [A.alloc([128, 512], F32, "zt") for _ in range(3)]
    zt_b = P.bufs_n("zt", 3)
    zi = 0
    sm = [A.alloc([128, 4, 16], F32, "gsm") for _ in range(2)]
    sm_b = P.bufs_n("gsm", 2)
    onesb = c.cstbf[:, CB_ONES:CB_ONES + 128]
    xi = oi = ti = 0
    for t in range(T // TT):
        r0 = t * TT
        for s in range(4):
            x, xb = xs[xi % 3], xs_b[xi % 3]
            xi += 1
            P.dma(x[:], src[r0 + s * 128:r0 + (s + 1) * 128, :], writes=[xb], sembuf=xb)
            transpose_in(c, x, xb, hT, hT_b, s * 128, (0, 1))
        def s0(cc, r0=r0):
            col = cc * 128
            bk = 2 + (cc % 2)

            def mmp(e):
                ins = None
                for kc in range(8):
                    ins = e.matmul(c.ps[bk][:, :], win[:, kc, col:col + 128], hT[:, kc, :],
                                   start=(kc == 0), stop=(kc == 7))
                return ins
            P.add("pe", mmp, reads=[hT_b, win_b[col // 512]], writes=[c.psb[bk]])
            xp, xp_b = xpre[cc % NXP], xpre_b[cc % NXP]
            P.add("dve", lambda e: e.tensor_copy(xp[:, 0:4], halo[:, cc, :]), reads=[halo_b[cc]], writes=[xp_b])
            P.add("act", lambda e: e.activation(xp[:, 4:TT + 4], c.ps[bk][:, :], AF.Copy),
                  reads=[c.psb[bk]], writes=[xp_b])
            P.add("dve", lambda e: e.tensor_copy(halo[:, cc, :], xp[:, TT:TT + 4]), reads=[xp_b], writes=[halo_b[cc]])

        def s1(cc, r0=r0):
            bkc = 4 + (cc % 2)
            xp, xp_b = xpre[cc % NXP], xpre_b[cc % NXP]

            def mmc(e):
                ins = None
                for i in range(4):
                    ins = e.matmul(c.ps[bkc][:, :], Wd[:, cc, i, :], xp[:, 1 + i:1 + i + TT], start=(i == 0), stop=(i == 3))
                return ins
            P.add("pe", mmc, reads=[xp_b, Wd_b], writes=[c.psb[bkc]])
            o, o_b = ob[cc % NOB], ob_b[cc % NOB]
            if cc < 16:
                q_, q_b = qs[cc % NQS], qs_b[cc % NQS]
                s_, s_b = sq[cc % NQS], sq_b[cc % NQS]
                P.add("act", lambda e: e.activation(q_[:], c.ps[bkc][:, :], AF.Silu), reads=[c.psb[bkc]], writes=[q_b])
                P.add("act", lambda e: e.activation(s_[:], q_[:], AF.Square), reads=[q_b], writes=[s_b])
            else:
                P.add("act", lambda e: e.activation(o[:], c.ps[bkc][:, :], AF.Silu), reads=[c.psb[bkc]], writes=[o_b])

        def s2(cc, r0=r0):
            if cc >= 16:
                return
            o, o_b = ob[cc % NOB], ob_b[cc % NOB]
            q_, q_b = qs[cc % NQS], qs_b[cc % NQS]
            s_, s_b = sq[cc % NQS], sq_b[cc % NQS]
            r_, r_b = rn[cc % 2], rn_b[cc % 2]
            P.add("pe", lambda e: e.matmul(c.ps[6][:, :], onesb, s_[:], start=True, stop=True),
                  reads=[s_b, c.cstbf_b], writes=[c.psb[6]])
            if cc < 8:
                P.add("act", lambda e: e.activation(r_[:], c.ps[6][:, :], AF.Sqrt, bias=epsq[:, 0:1], scale=128.0),
                      reads=[c.psb[6], gc_b], writes=[r_b])
            else:
                P.add("act", lambda e: e.activation(r_[:], c.ps[6][:, :], AF.Sqrt, bias=epsq[:, 1:2], scale=1.0),
                      reads=[c.psb[6], gc_b], writes=[r_b])
            P.add("dve", lambda e: e.reciprocal(r_[:], r_[:]), reads=[r_b], writes=[r_b])
            P.add("dve", lambda e: e.tensor_tensor(o[:], q_[:], r_[:], ALU.mult),
                  reads=[r_b, q_b], writes=[o_b])
            hd = cc % 8
            P.dma((qT_d if cc < 8 else kT_d)[hd, :, r0:r0 + TT], o[:], reads=[o_b], sembuf=o_b)

        def s3(cc, r0=r0):
            if cc < 8:
                return
            o, o_b = ob[cc % NOB], ob_b[cc % NOB]
            pb = psbf(c, 7)

            def trk(e):
                ins = None
                for s in range(4):
                    ins = e.transpose(pb[:, s * 128:(s + 1) * 128], o[:, s * 128:(s + 1) * 128], c.identb)
                return ins
            P.add("pe", trk, reads=[o_b, c.cstbf_b], writes=[c.psb[7]])
            k_, k_b = tk[cc % 3], tk_b[cc % 3]
            P.add("dve", lambda e: e.tensor_copy(k_[:], pb[:, 0:512].rearrange("p (s d) -> p s d", s=4)),
                  reads=[c.psb[7]], writes=[k_b])
            if cc < 16:
                dd = ktok_d[r0:r0 + TT, (cc - 8) * 128:(cc - 7) * 128]
            else:
                dd = vtok_d[r0:r0 + TT, (cc - 16) * 128:(cc - 15) * 128]
            P.dma(dd.rearrange("(s p) d -> p s d", p=128), k_[:], reads=[k_b], sembuf=k_b)
        pipeline(32, [s0, s1, s2, s3])
        for s in range(4):
            for zq in range(4):
                z_, z_b = zt[zi % 3], zt_b[zi % 3]
                zi += 1
                bk = zq

                def mmz(e, s=s, zq=zq, bk=bk):
                    ins = None
                    for kc in range(8):
                        ins = e.matmul(c.ps[bk][:, :], hT[:, kc, s * 128:(s + 1) * 128],
                                       win[:, kc, GZ + zq * 512:GZ + (zq + 1) * 512], start=(kc == 0), stop=(kc == 7))
                    return ins
                P.add("pe", mmz, reads=[hT_b] + [win_b[(GZ + zq * 512) // 512]], writes=[c.psb[bk]])
                P.add("act", lambda e, z_=z_, bk=bk: e.activation(z_[:], c.ps[bk][:, :], AF.Silu),
                      reads=[c.psb[bk]], writes=[z_b])
                P.dma(z_d[r0 + s * 128:r0 + (s + 1) * 128, zq * 512:(zq + 1) * 512], z_[:], reads=[z_b], sembuf=z_b)

            def mmba(e, s=s):
                ins = None
                for kc in range(8):
                    ins = e.matmul(c.ps[6][:, 0:32], hT[:, kc, s * 128:(s + 1) * 128], win[:, kc, GB_:GB_ + 32],
                                   start=(kc == 0), stop=(kc == 7))
                return ins
            P.add("pe", mmba, reads=[hT_b, win_b[12]], writes=[c.psb[6]])
            m_, m_b = sm[s % 2], sm_b[s % 2]
            P.add("act", lambda e, m_=m_: e.activation(m_[:, 0, :], c.ps[6][:, 0:16], AF.Sigmoid), reads=[c.psb[6]], writes=[m_b])
            P.add("dve", lambda e, m_=m_: e.tensor_tensor(m_[:, 2, :], c.ps[6][:, 16:32], dtb[:], ALU.add),
                  reads=[c.psb[6], b2, gc_b], writes=[m_b])
            P.add("act", lambda e, m_=m_: e.activation(m_[:, 2, :], m_[:, 2, :], AF.Exp), reads=[m_b], writes=[m_b])
            P.add("act", lambda e, m_=m_: e.activation(m_[:, 3, :], m_[:, 2, :], AF.Ln, bias=c.one_col[:], scale=1.0),
                  reads=[m_b, c.cst_b], writes=[m_b])
            P.add("dve", lambda e, m_=m_: e.tensor_tensor(m_[:, 1, :], m_[:, 3, :], negA[:], ALU.mult),
                  reads=[m_b, gc_b], writes=[m_b])
            P.dma(beta_d[r0 + s * 128:r0 + (s + 1) * 128, :], m_[:, 0, :], reads=[m_b], sembuf=m_b)
            P.dma(g_d[r0 + s * 128:r0 + (s + 1) * 128, :], m_[:, 1, :], reads=[m_b], sembuf=m_b)
    P.barrier()
    A.release(m0)


def stage_gdn_scan(c, qT_d, kT_d, ktok_d, vtok_d, g_d, beta_d, o_d, T):
    P, A = c.P, c.A
    m0 = A.mark()
    NCH = T // 128
    H = 16
    UT = c.cst[:, C_UT:C_UT + 128]
    ones32 = c.cst[:, C_ONES:C_ONES + 128]
    SM = c.cst[:, C_SMASK:C_SMASK + 128]
    IMT = c.cst[:, C_IMASKT:C_IMASKT + 128]

    def f32t(name):
        return A.alloc([128, H, 128], F32, name), P.bufs_n(name, 4)

    def bf16t(name):
        return A.alloc([128, H, 128], BF16, name), P.bufs_n(name, 4)
    qTc = [A.alloc([128, 8, 128], BF16, "qTc") for _ in range(2)]
    kTc = [A.alloc([128, 8, 128], BF16, "kTc") for _ in range(2)]
    ktk = [A.alloc([128, 8, 128], BF16, "ktk") for _ in range(2)]
    vtk = [A.alloc([128, H, 128], BF16, "vtk") for _ in range(2)]
    gb = [A.alloc([128, 2, H], F32, "gb") for _ in range(2)]
    qTc_b, kTc_b, ktk_b, vtk_b = P.bufs_n("qTc", 2), P.bufs_n("kTc", 2), P.bufs_n("ktk", 2), P.bufs_n("vtk", 2)
    g_b, be_b = P.bufs_n("gld", 2), P.bufs_n("bld", 2)
    F1, F1b = f32t("F1")
    F2, F2b = f32t("F2")
    F3, F3b = f32t("F3")
    Rp = [f32t("R%d" % i) for i in range(2)]
    RTp = [f32t("RT%d" % i) for i in range(2)]
    Y, Yb = f32t("Y")
    attnT, attnT_b = bf16t("attnT")
    TTb, TTb_b = bf16t("TTb")
    vbt, vbt_b = bf16t("vbt")
    kw, kw_b = bf16t("kw")
    wT, wT_b = bf16t("wT")
    qdT, qdT_b = bf16t("qdT")
    kdec, kdec_b = bf16t("kdec")
    vnew, vnew_b = bf16t("vnew")
    osb, osb_b = f32t("osb")
    S, S_b = f32t("S")
    Sbf, Sbf_b = bf16t("Sbf")
    sm = A.alloc([128, 8, H], F32, "ssm")
    sm_b = P.buf("ssm")
    P.add("dve", lambda e: e.memset(S[:], 0.0), writes=S_b)
    P.add("dve", lambda e: e.memset(Sbf[:], 0.0), writes=Sbf_b)

    def flat(t):
        return t[:].rearrange("p h d -> p (h d)")

    def bank4(q):
        return c.ps[q][:].rearrange("p (h d) -> p h d", h=4)

    def hs4(q):
        return slice(4 * q, 4 * q + 4)

    for ch in range(NCH):
        c0 = ch * 128
        pb = ch % 2
        P.dma(qTc[pb][:], qT_d[:, :, c0:c0 + 128].rearrange("h p t -> p h t"), writes=[qTc_b[pb]], sembuf=qTc_b[pb])
        P.dma(kTc[pb][:], kT_d[:, :, c0:c0 + 128].rearrange("h p t -> p h t"), writes=[kTc_b[pb]], sembuf=kTc_b[pb])
        P.dma(ktk[pb][:], ktok_d[c0:c0 + 128, :].rearrange("p (h d) -> p h d", h=8), writes=[ktk_b[pb]], sembuf=ktk_b[pb])
        P.dma(vtk[pb][:], vtok_d[c0:c0 + 128, :].rearrange("p (h d) -> p h d", h=H), writes=[vtk_b[pb]], sembuf=vtk_b[pb])
        P.dma(gb[pb][:, 0, :], g_d[c0:c0 + 128, :], writes=[g_b[pb]], sembuf=g_b[pb])
        P.dma(gb[pb][:, 1, :], beta_d[c0:c0 + 128, :], writes=[be_b[pb]], sembuf=be_b[pb])
        g = gb[pb][:, 0, :]
        beta = gb[pb][:, 1, :]
        def mm1(e, g=g):
            e.matmul(c.ps[0][:, 0:16], UT, g, start=True, stop=True)
            return e.matmul(c.ps[0][:, 16:32], ones32, g, start=True, stop=True)
        P.add("pe", mm1, reads=[g_b[pb], c.cstb], writes=[c.psb[0]])
        P.add("act", lambda e: e.activation(sm[:, 0, :], c.ps[0][:, 0:16], AF.Copy), reads=[c.psb[0]], writes=[sm_b])
        P.add("act", lambda e: e.activation(sm[:, 1, :], c.ps[0][:, 0:16], AF.Identity, scale=-1.0), reads=[c.psb[0]], writes=[sm_b])
        P.add("act", lambda e: e.activation(sm[:, 2, :], c.ps[0][:, 0:16], AF.Exp), reads=[c.psb[0]], writes=[sm_b])
        P.add("act", lambda e: e.activation(sm[:, 3, :], c.ps[0][:, 16:32], AF.Exp), reads=[c.psb[0]], writes=[sm_b])
        P.add("act", lambda e: e.activation(sm[:, 7, :], c.ps[0][:, 16:32], AF.Identity, bias=0.0, scale=1.0),
              reads=[c.psb[0]], writes=[sm_b])
        P.add("dve", lambda e: e.tensor_tensor(sm[:, 7, :], sm[:, 7, :], sm[:, 0, :], ALU.subtract),
              reads=[sm_b], writes=[sm_b])
        P.add("act", lambda e: e.activation(sm[:, 4, :], sm[:, 7, :], AF.Exp), reads=[sm_b], writes=[sm_b])
        P.add("dve", lambda e, beta=beta: e.tensor_tensor(sm[:, 5, :], beta, sm[:, 2, :], ALU.mult),
              reads=[sm_b, be_b[pb]], writes=[sm_b])
        P.add("dve", lambda e, beta=beta: e.tensor_scalar(sm[:, 6, :], beta, -1.0, None, ALU.mult),
              reads=[sm_b, be_b[pb]], writes=[sm_b])
        for q in range(4):
            P.add("dve", lambda e, g=g, q=q: e.tensor_tensor(
                F1[:, hs4(q), :], UT.rearrange("p (o i) -> p o i", o=1).to_broadcast([128, 4, 128]),
                g[:, 4 * q:4 * q + 4].rearrange("p (h o) -> p h o", o=1).to_broadcast([128, 4, 128]), ALU.mult),
                reads=[g_b[pb], c.cstb], writes=[F1b[q]])
            P.add("pe", lambda e, q=q: e.matmul(c.ps[4 + q][:, :], ones32, F1[:, hs4(q), :].rearrange("p h d -> p (h d)"),
                                                start=True, stop=True), reads=[F1b[q], c.cstb], writes=[c.psb[4 + q]])
        gcrow_b = [c.psb[4 + q] for q in range(4)]
        for q in range(4):
            P.add("act", lambda e, q=q: e.activation(F3[:, hs4(q), :], bank4(4 + q), AF.Exp),
                  reads=[gcrow_b[q]], writes=[F3b[q]])
            P.add("dve", lambda e, q=q: e.tensor_tensor(
                F2[:, hs4(q), :], bank4(4 + q),
                IMT.rearrange("p (o i) -> p o i", o=1).to_broadcast([128, 4, 128]), ALU.add),
                reads=[gcrow_b[q], c.cstb], writes=[F2b[q]])
        for par in range(2):
            P.add("dve", lambda e, par=par, pb=pb: e.tensor_tensor(qdT[:, par::2, :], qTc[pb][:], F3[:, par::2, :], ALU.mult),
                  reads=list(F3b) + [qTc_b[pb]], writes=qdT_b)
        for h in range(H):
            P.add("act", lambda e, h=h: e.activation(F2[:, h, :], F2[:, h, :], AF.Exp, bias=sm[:, 1, h:h + 1], scale=1.0),
                  reads=[F2b[h // 4], sm_b], writes=[F2b[h // 4]])
        def mmkk(e, pb=pb):
            ins = None
            for hk in range(8):
                ins = e.matmul(c.ps[hk // 4][:, (hk % 4) * 128:(hk % 4 + 1) * 128], kTc[pb][:, hk, :], kTc[pb][:, hk, :],
                               start=True, stop=True)
            return ins
        P.add("pe", mmkk, reads=[kTc_b[pb]], writes=[c.psb[0], c.psb[1]])

        def mmqk(e, pb=pb):
            ins = None
            for hk in range(8):
                ins = e.matmul(c.ps[2 + hk // 4][:, (hk % 4) * 128:(hk % 4 + 1) * 128], kTc[pb][:, hk, :], qTc[pb][:, hk, :],
                               start=True, stop=True)
            return ins
        P.add("pe", mmqk, reads=[kTc_b[pb], qTc_b[pb]], writes=[c.psb[2], c.psb[3]])
        for hf in range(2):
            for par in range(2):
                P.add("dve", lambda e, par=par, hf=hf: e.tensor_tensor(
                    attnT[:, 8 * hf + par:8 * hf + 8:2, :], bank4(2 + hf), F2[:, 8 * hf + par:8 * hf + 8:2, :], ALU.mult),
                    reads=[c.psb[2 + hf], F2b[2 * hf], F2b[2 * hf + 1]], writes=[attnT_b[2 * hf], attnT_b[2 * hf + 1]])
        for q in range(4):
            P.add("dve", lambda e, q=q: e.tensor_tensor(
                F2[:, hs4(q), :], SM.rearrange("p (o i) -> p o i", o=1).to_broadcast([128, 4, 128]),
                bank4(4 + q), ALU.subtract),
                reads=[gcrow_b[q], c.cstb], writes=[F2b[q]])
        for h in range(H):
            P.add("act", lambda e, h=h: e.activation(F2[:, h, :], F2[:, h, :], AF.Exp, bias=sm[:, 0, h:h + 1], scale=1.0),
                  reads=[F2b[h // 4], sm_b], writes=[F2b[h // 4]])
        for h in range(H):
            hk = h // 2
            P.add("dve", lambda e, h=h, hk=hk: e.scalar_tensor_tensor(
                F1[:, h, :], c.ps[hk // 4][:, (hk % 4) * 128:(hk % 4 + 1) * 128], sm[:, 6, h:h + 1], F2[:, h, :],
                ALU.mult, ALU.mult), reads=[c.psb[hk // 4], sm_b, F2b[h // 4]], writes=[F1b[h // 4]])
        R, Rb = Rp[0]
        for hf in range(4):
            bk = 2 + (hf % 2)

            def trm(e, hf=hf, bk=bk):
                ins = None
                for hq in range(4):
                    h = hf * 4 + hq
                    ins = e.transpose(c.ps[bk][:, hq * 128:(hq + 1) * 128], F1[:, h, :], c.ident)
                return ins
            P.add("pe", trm, reads=[F1b[hf], c.cstb], writes=[c.psb[bk]])
            P.add("act", lambda e, hf=hf, R=R, bk=bk: e.activation(R[:, hs4(hf), :], bank4(bk), AF.Copy),
                  reads=[c.psb[bk]], writes=[Rb[hf]])
            P.add("dve", lambda e, hf=hf, bk=bk: e.tensor_tensor(
                Y[:, hs4(hf), :], bank4(bk),
                c.ident.rearrange("p (o i) -> p o i", o=1).to_broadcast([128, 4, 128]), ALU.add),
                reads=[c.psb[bk], c.cstb], writes=[Yb[hf]])
        RT, RTb = F1, F1b
        NLEV = 6
        for lev in range(1, NLEV + 1):
            Rn, Rnb = Rp[lev % 2]
            RTn, RTnb = RTp[lev % 2]
            last = (lev == NLEV)
            for hf in range(4):
                b_rt, b_r, b_y = (hf % 2), 2 + (hf % 2), 4 + (hf % 2)

                def mmrt(e, hf=hf, R=R, RT=RT, bk=b_rt):
                    ins = None
                    for hq in range(4):
                        h = hf * 4 + hq
                        ins = e.matmul(c.ps[bk][:, hq * 128:(hq + 1) * 128], R[:, h, :], RT[:, h, :], start=True, stop=True)
                    return ins
                P.add("pe", mmrt, reads=[Rb[hf], RTb[hf]], writes=[c.psb[b_rt]])
                P.add("act", lambda e, hf=hf, RTn=RTn, bk=b_rt: e.activation(RTn[:, hs4(hf), :], bank4(bk), AF.Copy),
                      reads=[c.psb[b_rt]], writes=[RTnb[hf]])
                if not last:
                    def mmr(e, hf=hf, R=R, RT=RT, bk=b_r):
                        ins = None
                        for hq in range(4):
                            h = hf * 4 + hq
                            ins = e.matmul(c.ps[bk][:, hq * 128:(hq + 1) * 128], RT[:, h, :], R[:, h, :], start=True, stop=True)
                        return ins
                    P.add("pe", mmr, reads=[Rb[hf], RTb[hf]], writes=[c.psb[b_r]])
                    P.add("dve", lambda e, hf=hf, Rn=Rn, bk=b_r: e.tensor_copy(Rn[:, hs4(hf), :], bank4(bk)),
                          reads=[c.psb[b_r]], writes=[Rnb[hf]])
                if hf >= 1:
                    hy = hf - 1
                    b_yy = 4 + (hy % 2)

                    def mmy(e, hf=hy, RTn=RTn, bk=b_yy):
                        ins = None
                        for hq in range(4):
                            h = hf * 4 + hq
                            ins = e.matmul(c.ps[bk][:, hq * 128:(hq + 1) * 128], RTn[:, h, :], Y[:, h, :], start=True, stop=True)
                        return ins
                    P.add("pe", mmy, reads=[RTnb[hy], Yb[hy]], writes=[c.psb[b_yy]])
                    P.add("dve", lambda e, hf=hy, bk=b_yy: e.tensor_tensor(Y[:, hs4(hf), :], bank4(bk), Y[:, hs4(hf), :], ALU.add),
                          reads=[c.psb[b_yy], Yb[hy]], writes=[Yb[hy]])
            hy = 3
            b_yy = 4 + (hy % 2)

            def mmy3(e, hf=hy, RTn=RTn, bk=b_yy):
                ins = None
                for hq in range(4):
                    h = hf * 4 + hq
                    ins = e.matmul(c.ps[bk][:, hq * 128:(hq + 1) * 128], RTn[:, h, :], Y[:, h, :], start=True, stop=True)
                return ins
            P.add("pe", mmy3, reads=[RTnb[hy], Yb[hy]], writes=[c.psb[b_yy]])
            P.add("dve", lambda e, hf=hy, bk=b_yy: e.tensor_tensor(Y[:, hs4(hf), :], bank4(bk), Y[:, hs4(hf), :], ALU.add),
                  reads=[c.psb[b_yy], Yb[hy]], writes=[Yb[hy]])
            R, Rb = Rn, Rnb
            RT, RTb = RTn, RTnb
        P.add("dve", lambda e, pb=pb, beta=beta: e.tensor_tensor(
            vbt[:], vtk[pb][:], beta.rearrange("p (h o) -> p h o", o=1).to_broadcast([128, H, 128]), ALU.mult),
            reads=[vtk_b[pb], be_b[pb]], writes=vbt_b)
        for par in range(2):
            P.add("dve", lambda e, par=par, pb=pb: e.tensor_tensor(
                kw[:, par::2, :], ktk[pb][:],
                sm[:, 5, par::2].rearrange("p (h o) -> p h o", o=1).to_broadcast([128, 8, 128]), ALU.mult),
                reads=[ktk_b[pb], sm_b], writes=kw_b)
            P.add("dve", lambda e, par=par, pb=pb: e.tensor_tensor(
                kdec[:, par::2, :], ktk[pb][:],
                sm[:, 4, par::2].rearrange("p (h o) -> p h o", o=1).to_broadcast([128, 8, 128]), ALU.mult),
                reads=[ktk_b[pb], sm_b], writes=kdec_b)
        for hf in range(4):
            P.add("act", lambda e, hf=hf: e.activation(TTb[:, hs4(hf), :], Y[:, hs4(hf), :], AF.Copy),
                  reads=[Yb[hf]], writes=[TTb_b[hf]])
        for hf in range(4):
            bu, bw_ = (hf % 2), 2 + (hf % 2)

            def mmu(e, hf=hf, bk=bu):
                ins = None
                for hq in range(4):
                    h = hf * 4 + hq
                    ins = e.matmul(c.ps[bk][:, hq * 128:(hq + 1) * 128], TTb[:, h, :], vbt[:, h, :], start=True, stop=True)
                return ins
            P.add("pe", mmu, reads=[TTb_b[hf]] + list(vbt_b), writes=[c.psb[bu]])
            P.add("act", lambda e, hf=hf, bk=bu: e.activation(F3[:, hs4(hf), :], bank4(bk), AF.Copy),
                  reads=[c.psb[bu]], writes=[F3b[hf]])

            def mmw(e, hf=hf, bk=bw_):
                ins = None
                for hq in range(4):
                    h = hf * 4 + hq
                    ins = e.matmul(c.ps[bk][:, hq * 128:(hq + 1) * 128], kw[:, h, :], TTb[:, h, :], start=True, stop=True)
                return ins
            P.add("pe", mmw, reads=[TTb_b[hf]] + list(kw_b), writes=[c.psb[bw_]])
            P.add("dve", lambda e, hf=hf, bk=bw_: e.tensor_copy(wT[:, hs4(hf), :], bank4(bk)),
                  reads=[c.psb[bw_]], writes=[wT_b[hf]])
        for hf in range(4):
            bk = 4 + (hf % 2)

            def mmws(e, hf=hf, bk=bk):
                ins = None
                for hq in range(4):
                    h = hf * 4 + hq
                    ins = e.matmul(c.ps[bk][:, hq * 128:(hq + 1) * 128], wT[:, h, :], Sbf[:, h, :], start=True, stop=True)
                return ins
            P.add("pe", mmws, reads=[wT_b[hf]] + list(Sbf_b), writes=[c.psb[bk]])
            P.add("dve", lambda e, hf=hf, bk=bk: e.tensor_tensor(vnew[:, hs4(hf), :], F3[:, hs4(hf), :], bank4(bk), ALU.subtract),
                  reads=[c.psb[bk], F3b[hf]], writes=[vnew_b[hf]])
        for hf in range(4):
            bk = 6 + (hf % 2)

            def mmo(e, hf=hf, bk=bk):
                ins = None
                for hq in range(4):
                    h = hf * 4 + hq
                    e.matmul(c.ps[bk][:, hq * 128:(hq + 1) * 128], qdT[:, h, :], Sbf[:, h, :], start=True, stop=False)
                    ins = e.matmul(c.ps[bk][:, hq * 128:(hq + 1) * 128], attnT[:, h, :], vnew[:, h, :], start=False, stop=True)
                return ins
            P.add("pe", mmo, reads=list(qdT_b) + list(Sbf_b) + [attnT_b[hf], vnew_b[hf]], writes=[c.psb[bk]])
            P.add("act", lambda e, hf=hf, bk=bk: e.activation(osb[:, hs4(hf), :], bank4(bk), AF.Copy),
                  reads=[c.psb[bk]], writes=[osb_b[hf]])
        P.dma(o_d[c0:c0 + 128, :], flat(osb), reads=list(osb_b), sembuf=osb_b[0])
        for hf in range(4):
            bk = (hf % 2)

            def mmds(e, hf=hf, bk=bk):
                ins = None
                for hq in range(4):
                    h = hf * 4 + hq
                    ins = e.matmul(c.ps[bk][:, hq * 128:(hq + 1) * 128], kdec[:, h, :], vnew[:, h, :], start=True, stop=True)
                return ins
            P.add("pe", mmds, reads=list(kdec_b) + [vnew_b[hf]], writes=[c.psb[bk]])
            for hq in range(4):
                h = hf * 4 + hq
                P.add("dve", lambda e, h=h, bk=bk, hq=hq: e.scalar_tensor_tensor(
                    S[:, h, :], S[:, h, :], sm[:, 3, h:h + 1], c.ps[bk][:, hq * 128:(hq + 1) * 128], ALU.mult, ALU.add),
                    reads=[c.psb[bk], sm_b, S_b[hf]], writes=[S_b[hf]])
            P.add("act", lambda e, hf=hf: e.activation(Sbf[:, hs4(hf), :], S[:, hs4(hf), :], AF.Copy),
                  reads=[S_b[hf]], writes=[Sbf_b[hf]])
    P.barrier()
    A.release(m0)


def stage_gdn_out(c, o_d, z_d, norm_w, w_out, src, dst, ln_g, ln_b, li, si, T):
    P, A = c.P, c.A
    m0 = A.mark()
    H = 16
    wo = A.alloc([128, H, D], BF16, "gwo")
    wo_b = P.bufs_n("gwo", H)
    w_v = w_out.rearrange("(j p) d -> p j d", p=128)
    for j in range(H):
        P.dma(wo[:, j, :], w_v[:, j, :], writes=[wo_b[j]], sembuf=wo_b[j], eng="pool")
    gt, bt, gb_b = load_ln_params(c, ln_g, ln_b, li, si)
    nw = A.alloc([128, 128], F32, "nw")
    nw_b = P.buf("nw")
    P.dma(nw[:], norm_w.partition_broadcast(128), writes=[nw_b], sembuf=nw_b)
    epsr = A.alloc([128, 1], F32, "epsr")
    P.add("dve", lambda e: e.memset(epsr[:], RMS_EPS), writes=[nw_b], reads=[nw_b])
    ot = [A.alloc([128, H, 128], F32, "ot") for _ in range(2)]
    zt = [A.alloc([128, H, 128], F32, "zt2") for _ in range(2)]
    ot_b, zt_b = P.bufs_n("ot", 2), P.bufs_n("zt2", 2)
    sqt = A.alloc([128, H, 128], F32, "sqt")
    sqt_b = P.buf("sqt")
    ssm = [A.alloc([128, 2, H], F32, "gsm2") for _ in range(2)]
    ssm_b = P.bufs_n("gsm2", 2)
    onb = [A.alloc([128, H, 128], BF16, "onb") for _ in range(3)]
    onb_b = P.bufs_n("onb", 3)
    onT = [A.alloc([128, H, 128], BF16, "onT") for _ in range(3)]
    onT_b = P.bufs_n("onT", 3)
    xs = [A.alloc([128, D], F32, "xs") for _ in range(3)]
    xs_b = P.bufs_n("xs", 3)
    lns = alloc_ln_small(c, 2)
    def g0(s):
        r0 = s * 128
        p2 = s % 2
        o, o_b, z, z_b = ot[p2], ot_b[p2], zt[p2], zt_b[p2]
        P.dma(o[:].rearrange("p h d -> p (h d)"), o_d[r0:r0 + 128, :], writes=[o_b], sembuf=o_b)
        P.dma(z[:].rearrange("p h d -> p (h d)"), z_d[r0:r0 + 128, :], writes=[z_b], sembuf=z_b)
        sm_, sm_b = ssm[p2], ssm_b[p2]
        P.add("act", lambda e: e.activation(sqt[:], o[:], AF.Square), reads=[o_b], writes=[sqt_b])
        P.add("dve", lambda e: e.tensor_reduce(sm_[:, 0, :], sqt[:], AX.X, ALU.add), reads=[sqt_b], writes=[sm_b])
        P.add("act", lambda e: e.activation(sm_[:, 1, :], sm_[:, 0, :], AF.Sqrt, bias=epsr[:], scale=1.0 / 128),
              reads=[sm_b, nw_b], writes=[sm_b])
        P.add("dve", lambda e: e.reciprocal(sm_[:, 1, :], sm_[:, 1, :]), reads=[sm_b], writes=[sm_b])
        P.add("dve", lambda e: e.tensor_tensor(
            z[:], z[:], nw[:].rearrange("p (o d) -> p o d", o=1).to_broadcast([128, H, 128]), ALU.mult),
            reads=[z_b, nw_b], writes=[z_b])
        on, on_b = onb[s % 3], onb_b[s % 3]
        for h in range(H):
            P.add("dve", lambda e, h=h: e.scalar_tensor_tensor(
                on[:, h, :], o[:, h, :], sm_[:, 1, h:h + 1], z[:, h, :], ALU.mult, ALU.mult),
                reads=[o_b, z_b, sm_b], writes=[on_b])

    def g1(s):
        on, on_b = onb[s % 3], onb_b[s % 3]
        oT, oT_b = onT[s % 3], onT_b[s % 3]
        for hf in range(2):
            pbv = psbf(c, hf)

            def tr(e, hf=hf, pbv=pbv):
                ins = None
                for hq in range(8):
                    ins = e.transpose(pbv[:, hq * 128:(hq + 1) * 128], on[:, hf * 8 + hq, :], c.identb)
                return ins
            P.add("pe", tr, reads=[on_b, c.cstbf_b], writes=[c.psb[hf]])
            if hf == 0:
                P.add("act", lambda e, pbv=pbv: e.activation(
                    oT[:, 0:8, :], pbv.rearrange("p (h d) -> p h d", h=8), AF.Copy), reads=[c.psb[0]], writes=[oT_b])
            else:
                P.add("dve", lambda e, pbv=pbv: e.tensor_copy(
                    oT[:, 8:16, :], pbv.rearrange("p (h d) -> p h d", h=8)), reads=[c.psb[1]], writes=[oT_b])

    def g2(s):
        r0 = s * 128
        oT, oT_b = onT[s % 3], onT_b[s % 3]
        banks = (4, 5) if s % 2 == 0 else (6, 7)
        for hh, bk in enumerate(banks):
            def mmo(e, hh=hh, bk=bk):
                ins = None
                for j in range(H):
                    ins = e.matmul(c.ps[bk][:, :], oT[:, j, :], wo[:, j, hh * 512:(hh + 1) * 512],
                                   start=(j == 0), stop=(j == H - 1))
                return ins
            P.add("pe", mmo, reads=[oT_b] + list(wo_b), writes=[c.psb[bk]])
        x, xb = xs[s % 3], xs_b[s % 3]
        resid_ln(c, src, dst, r0, banks, 1.0 / ALPHA, x, xb, gt, bt, gb_b, lns[s % 2])
    pipeline(T // 128, [g0, g1, g2])
    P.barrier()
    A.release(m0)


def build_program(T=SEQ):
    nc = bass.Bass("TRN2", target_bir_lowering=False)
    din = lambda n, s, d=F32: nc.dram_tensor(n, s, d, kind="ExternalInput").ap()
    dsc = lambda n, s, d=F32: nc.dram_tensor(n, s, d, kind="Internal").ap()
    x = din("x", [T, D])
    ln_g = din("ln_g", [2, 3, D])
    ln_b = din("ln_b", [2, 3, D])
    fpre_in = din("ffn_pre_w_in", [2, D, 2 * DFF])
    fpre_out = din("ffn_pre_w_out", [2, DFF, D])
    fpost_in = din("ffn_post_w_in", [2, D, 2 * DFF])
    fpost_out = din("ffn_post_w_out", [2, DFF, D])
    m_in = din("moba_w_in", [1, D, 3 * D])
    m_out = din("moba_w_out", [1, D, D])
    g_in = din("gdn_w_in", [1, D, GPROJ])
    g_conv = din("gdn_conv_w", [1, 4, 4096])
    g_alog = din("gdn_a_log", [1, 16])
    g_dtb = din("gdn_dt_bias", [1, 16])
    g_nw = din("gdn_norm_w", [1, 128])
    g_out = din("gdn_w_out", [1, 2048, D])
    consts = din("consts", [128, NCONST])
    rope = din("rope", [2, 128, T])
    y = nc.dram_tensor("y", [T, D], F32, kind="ExternalOutput").ap()
    hA = dsc("hA", [T, D])
    hB = dsc("hB", [T, D])
    qT_d = dsc("qT_d", [8, 128, T], BF16)
    kT_d = dsc("kT_d", [8, 128, T], BF16)
    v_d = dsc("v_d", [T, D], BF16)
    bT_d = dsc("bT_d", [8, 16, T], BF16)
    oT_d = dsc("oT_d", [8, 128, T], BF16)
    ktok_d = dsc("ktok_d", [T, 1024], BF16)
    vtok_d = dsc("vtok_d", [T, 2048], BF16)
    z_d = dsc("z_d", [T, 2048])
    gg_d = dsc("gg_d", [T, 16])
    beta_d = dsc("beta_d", [T, 16])
    o_d = dsc("o_d", [T, 2048])
    c = make_ctx(nc)
    load_small_consts(c)
    load_consts_full(c, consts)
    c.P.barrier()
    stage_ffn(c, x, hA, fpre_in[0], fpre_out[0], ln_g, ln_b, 0, 0, T)
    stage_moba_proj(c, hA, m_in[0], rope, qT_d, kT_d, v_d, bT_d, T)
    stage_moba_attn(c, qT_d, kT_d, v_d, bT_d, oT_d, T)
    stage_outproj_ln(c, oT_d, 8, m_out[0], hA, hB, ln_g, ln_b, 0, 1, 1.0 / ALPHA, T)
    stage_ffn(c, hB, hA, fpost_in[0], fpost_out[0], ln_g, ln_b, 0, 2, T)
    stage_ffn(c, hA, hB, fpre_in[1], fpre_out[1], ln_g, ln_b, 1, 0, T)
    stage_gdn_proj(c, hB, g_in[0], g_conv[0], g_alog, g_dtb, qT_d, kT_d, ktok_d, vtok_d, z_d, gg_d, beta_d, T)
    stage_gdn_scan(c, qT_d, kT_d, ktok_d, vtok_d, gg_d, beta_d, o_d, T)
    stage_gdn_out(c, o_d, z_d, g_nw, g_out[0], hB, hA, ln_g, ln_b, 1, 1, T)
    stage_ffn(c, hA, y, fpost_in[1], fpost_out[1], ln_g, ln_b, 1, 2, T)
    c.P.emit()
    return nc


_CACHE = {}


def kernel(x, ln_g, ln_b, ffn_pre_w_in, ffn_pre_w_out, ffn_post_w_in, ffn_post_w_out,
           moba_w_in, moba_w_out, gdn_w_in, gdn_conv_w, gdn_a_log, gdn_dt_bias, gdn_norm_w, gdn_w_out):
    B, T, _ = x.shape
    if "nc" not in _CACHE:
        _CACHE["nc"] = build_program(T)
    nc = _CACHE["nc"]
    f = lambda a: np.ascontiguousarray(np.asarray(a, dtype=np.float32))
    shared = dict(ln_g=f(ln_g), ln_b=f(ln_b), ffn_pre_w_in=f(ffn_pre_w_in), ffn_pre_w_out=f(ffn_pre_w_out),
                  ffn_post_w_in=f(ffn_post_w_in), ffn_post_w_out=f(ffn_post_w_out), moba_w_in=f(moba_w_in),
                  moba_w_out=f(moba_w_out), gdn_w_in=f(gdn_w_in), gdn_conv_w=f(gdn_conv_w), gdn_a_log=f(gdn_a_log),
                  gdn_dt_bias=f(gdn_dt_bias), gdn_norm_w=f(gdn_norm_w), gdn_w_out=f(gdn_w_out),
                  consts=make_consts(), rope=make_rope(T))
    xs = f(x)
    in_maps = [dict(shared, x=xs[b]) for b in range(B)]
    res = run_bass_kernel_spmd(nc, in_maps, core_ids=list(range(B)))
    return np.stack([np.asarray(r["y"], dtype=np.float32) for r in res.results], axis=0)
```

```python
import math
import numpy as np
import concourse.bass as bass
import concourse.mybir as mybir
from concourse.bass_utils import run_bass_kernel_spmd

F32 = mybir.dt.float32
BF16 = mybir.dt.bfloat16
F32R = mybir.dt.float32r
AF = mybir.ActivationFunctionType
ALU = mybir.AluOpType
AX = mybir.AxisListType

D = 1024
DFF = 2816
SEQ = 4096
NB = 8
ALPHA = (2 * 2) ** 0.25
LN_EPS = 1e-5
RMS_EPS = 1e-6


class Buf:
    __slots__ = ("name", "last_w", "readers", "dsem", "dcount", "excl")

    def __init__(self, name):
        self.name = name
        self.excl = False
        self.last_w = None
        self.readers = []
        self.dsem = None
        self.dcount = 0


class Op:
    __slots__ = ("eng", "fn", "deps", "sig", "need_sig", "is_dma", "dbuf", "dval", "idx", "dslot")

    def __init__(self, eng, fn, is_dma=False):
        self.eng = eng
        self.fn = fn
        self.deps = []
        self.sig = None
        self.need_sig = False
        self.is_dma = is_dma
        self.dbuf = None
        self.dval = 0
        self.idx = 0
        self.dslot = None


ENGS = ("pe", "act", "dve", "pool", "sp")


class Prog:
    def __init__(self, nc):
        self.nc = nc
        self.ops = {e: [] for e in ENGS}
        self.bufs = []
        self.dma_bufs = []
        self.nops = 0
        self.slots = []
        self.free_slots = []
        self.free_slots_sw = []

    def buf(self, name):
        b = Buf(name)
        self.bufs.append(b)
        return b

    def bufs_n(self, name, n):
        return [self.buf("%s%d" % (name, i)) for i in range(n)]

    def _track(self, op, reads, writes):
        seen = set()
        for b in reads:
            w = b.last_w
            if w is not None and id(w) not in seen:
                seen.add(id(w))
                op.deps.append((w, True))
            if b.excl:
                for r in b.readers:
                    if id(r) not in seen:
                        seen.add(id(r))
                        op.deps.append((r, False))
                b.readers = []
        for b in writes:
            w = b.last_w
            if w is not None and id(w) not in seen:
                seen.add(id(w))
                op.deps.append((w, False))
            for r in b.readers:
                if id(r) not in seen:
                    seen.add(id(r))
                    op.deps.append((r, False))
        for b in reads:
            b.readers.append(op)
        for b in writes:
            b.last_w = op
            b.readers = []

    def add(self, eng, fn, reads=(), writes=()):
        op = Op(eng, fn)
        self._track(op, reads, writes)
        op.idx = self.nops
        self.nops += 1
        self.ops[eng].append(op)
        return op

    def dma(self, out_ap, in_ap, reads=(), writes=(), sembuf=None, eng="sp"):
        op = Op(eng, None, is_dma=True)
        op.fn = (out_ap, in_ap)
        self._track(op, reads, writes)
        kind = 1 if eng == "pool" else 0
        if sembuf.dsem is None:
            fl = self.free_slots_sw if kind else self.free_slots
            if fl:
                sembuf.dsem = fl.pop()
            else:
                sembuf.dsem = [0, None, kind]
                self.slots.append(sembuf.dsem)
            self.dma_bufs.append(sembuf)
        assert sembuf.dsem[2] == kind, "mixing SW/HW DGE on one semaphore"
        sembuf.dsem[0] += 16
        op.dbuf = sembuf
        op.dslot = sembuf.dsem
        op.dval = sembuf.dsem[0]
        op.idx = self.nops
        self.nops += 1
        self.ops[eng].append(op)
        return op

    def barrier(self):
        lasts = []
        for e in ENGS:
            for o in reversed(self.ops[e]):
                if not o.is_dma and o.fn is not None:
                    lasts.append(o)
                    break
        dmas = [(b.dsem, b.dsem[0]) for b in self.dma_bufs]
        for b in self.dma_bufs:
            (self.free_slots_sw if b.dsem[2] else self.free_slots).append(b.dsem)
            b.dsem = None
        self.dma_bufs = []
        for e in ENGS:
            op = Op(e, None)
            op.deps = [(o, True) for o in lasts]
            op.dval = dmas
            op.idx = self.nops
            self.nops += 1
            self.ops[e].append(op)
        for b in self.bufs:
            b.last_w = None
            b.readers = []

    @staticmethod
    def _needs_wait(op, d, raw):
        if d.eng != op.eng or op.fn is None or op.is_dma:
            return True
        if op.eng == "pe":
            return False
        return True

    def emit(self):
        nc = self.nc
        for e in ENGS:
            for k, op in enumerate(self.ops[e]):
                op.idx = k
        for e in ENGS:
            for op in self.ops[e]:
                for d, raw in op.deps:
                    if d.is_dma:
                        continue
                    if self._needs_wait(op, d, raw):
                        d.need_sig = True
        for e in ENGS:
            c = 0
            for op in self.ops[e]:
                if op.need_sig:
                    c += 1
                    op.sig = c
        sems = {e: nc.alloc_semaphore("s_" + e) for e in ENGS}
        for i, sl in enumerate(self.slots):
            sl[1] = nc.alloc_semaphore("dsem%d" % i)
        engobj = {"pe": nc.tensor, "act": nc.scalar, "dve": nc.vector, "pool": nc.gpsimd, "sp": nc.sync}
        self.nwaits = 0
        with nc.Block() as block:
            def run(e, eng):
                seen = {}
                for op in self.ops[e]:
                    waits = {}
                    for d, raw in op.deps:
                        if d.is_dma:
                            key = ("d", id(d.dslot))
                            if waits.get(key, (None, 0))[1] < d.dval:
                                waits[key] = (d.dslot[1], d.dval)
                        else:
                            if not self._needs_wait(op, d, raw):
                                continue
                            key = ("e", d.eng)
                            if waits.get(key, (None, 0))[1] < d.sig:
                                waits[key] = (sems[d.eng], d.sig)
                    if op.fn is None and not op.is_dma:
                        for sl, v in op.dval:
                            waits[("d", id(sl))] = (sl[1], v)
                    for key, (s, v) in waits.items():
                        if seen.get(key, 0) >= v:
                            continue
                        seen[key] = v
                        eng.wait_ge(s, v)
                        self.nwaits += 1
                    if op.is_dma:
                        o, i = op.fn
                        eng.dma_start(out=o, in_=i).then_inc(op.dslot[1], 16)
                    elif op.fn is not None:
                        ins = op.fn(eng)
                        if op.need_sig:
                            ins.then_inc(sems[e], 1)

            @block.tensor
            def _(eng):
                run("pe", eng)

            @block.scalar
            def _(eng):
                run("act", eng)

            @block.vector
            def _(eng):
                run("dve", eng)

            @block.gpsimd
            def _(eng):
                run("pool", eng)

            @block.sync
            def _(eng):
                run("sp", eng)


class Arena:
    def __init__(self, nc, base=16512, limit=229344):
        self.nc = nc
        self.top = base
        self.limit = limit
        self.n = 0

    def mark(self):
        return self.top

    def release(self, m):
        self.top = m

    def alloc(self, shape, dtype, name=None):
        nbytes = int(np.prod(shape[1:])) * (2 if dtype == BF16 else 4)
        off = (self.top + 63) // 64 * 64
        assert off + nbytes <= self.limit, ("SBUF arena overflow", name, off, nbytes)
        self.top = off + nbytes
        self.n += 1
        t = self.nc.alloc_sbuf_tensor_at("%s_%d" % (name or "t", self.n), list(shape), dtype, offset=off)
        return t


class Ctx:
    pass


def make_ctx(nc):
    c = Ctx()
    c.nc = nc
    c.P = Prog(nc)
    c.A = Arena(nc)
    c.ps = [nc.alloc_psum_tensor("psb%d" % i, [128, 512], F32) for i in range(8)]
    c.psb = c.P.bufs_n("psb", 8)
    for b in c.psb:
        b.excl = True
    return c


def load_consts(c, consts_ap):
    P, A = c.P, c.A
    c.ident = A.alloc([128, 128], F32, "ident")
    c.ident_b = P.buf("ident")
    P.dma(c.ident[:], consts_ap[:, 0:128], writes=[c.ident_b], sembuf=c.ident_b)
    c.identb = A.alloc([128, 128], BF16, "identb")
    c.identb_b = P.buf("identb")
    P.add("dve", lambda e: e.tensor_copy(c.identb[:], c.ident[:]), reads=[c.ident_b], writes=[c.identb_b])


def bcast_rows(ap2d, nrows_part=128):
    return ap2d.partition_broadcast(nrows_part)


def ln_tile(c, xr, xr_b, gt, bt, gb_b, dst_ap, st):
    P = c.P
    eps = LN_EPS / (ALPHA * ALPHA)
    stats, mv, rstd, nmr = st["stats"], st["mv"], st["rstd"], st["nmr"]
    sb = st["b"]
    P.add("dve", lambda e: e.bn_stats(stats[:, 0, :], xr[:, 0:512]), reads=[xr_b], writes=[sb])
    P.add("dve", lambda e: e.bn_stats(stats[:, 1, :], xr[:, 512:1024]), reads=[xr_b], writes=[sb])
    P.add("dve", lambda e: e.bn_aggr(mv[:], stats[:].rearrange("p a b -> p (a b)")), reads=[sb], writes=[sb])
    P.add("act", lambda e: e.activation(rstd[:], mv[:, 1:2], AF.Sqrt, bias=c.epsln[:], scale=1.0),
          reads=[sb, c.cst_b], writes=[sb])
    P.add("dve", lambda e: e.reciprocal(rstd[:], rstd[:]), reads=[sb], writes=[sb])
    P.add("dve", lambda e: e.tensor_scalar(nmr[:], mv[:, 0:1], -1.0, rstd[:], ALU.mult, ALU.mult),
          reads=[sb], writes=[sb])
    P.add("act", lambda e: e.activation(xr[:], xr[:], AF.Identity, bias=nmr[:], scale=rstd[:]),
          reads=[sb, xr_b], writes=[xr_b])
    P.add("dve", lambda e: e.tensor_tensor(xr[:], xr[:], gt[:], ALU.mult), reads=[xr_b, gb_b], writes=[xr_b])
    P.add("dve", lambda e: e.tensor_tensor(xr[:], xr[:], bt[:], ALU.add), reads=[xr_b, gb_b], writes=[xr_b])
    P.dma(dst_ap, xr[:], reads=[xr_b], sembuf=xr_b)


def alloc_ln_small(c, n=2):
    out = []
    for i in range(n):
        st = {
            "stats": c.A.alloc([128, 2, 6], F32, "stats"),
            "mv": c.A.alloc([128, 2], F32, "mv"),
            "rstd": c.A.alloc([128, 1], F32, "rstd"),
            "nmr": c.A.alloc([128, 1], F32, "nmr"),
            "b": c.P.buf("lnsmall%d" % i),
        }
        out.append(st)
    return out


def load_ln_params(c, ln_g_ap, ln_b_ap, li, si):
    P, A = c.P, c.A
    gt = A.alloc([128, 1024], F32, "lng")
    bt = A.alloc([128, 1024], F32, "lnb")
    gb_b = P.buf("lngb")
    b2 = P.buf("lngb2")
    P.dma(gt[:], ln_g_ap[li, si:si + 1, :].partition_broadcast(128), writes=[gb_b], sembuf=gb_b)
    P.dma(bt[:], ln_b_ap[li, si:si + 1, :].partition_broadcast(128), writes=[b2], sembuf=b2)
    P.add("dve", lambda e: e.tensor_copy(bt[:, 0:1], bt[:, 0:1]), reads=[b2, gb_b], writes=[gb_b])
    return gt, bt, gb_b


def transpose_in(c, xs, xs_b, hT, hT_b, col0, banks):
    P = c.P
    for half in range(2):
        bk = banks[half]
        ps, psb = c.ps[bk], c.psb[bk]

        def f(e, half=half, ps=ps):
            ins = None
            for q in range(4):
                kc = half * 4 + q
                ins = e.transpose(ps[:, q * 128:(q + 1) * 128], xs[:, kc * 128:(kc + 1) * 128], c.ident[:])
            return ins
        P.add("pe", f, reads=[xs_b, c.ident_b], writes=[psb])
        eng = "act" if half == 0 else "dve"
        if eng == "act":
            P.add("act", lambda e, half=half, ps=ps: e.activation(
                hT[:, half * 4:half * 4 + 4, col0:col0 + 128],
                ps[:].rearrange("p (a b) -> p a b", a=4), AF.Copy),
                reads=[psb], writes=[hT_b])
        else:
            P.add("dve", lambda e, half=half, ps=ps: e.tensor_copy(
                hT[:, half * 4:half * 4 + 4, col0:col0 + 128],
                ps[:].rearrange("p (a b) -> p a b", a=4)),
                reads=[psb], writes=[hT_b])


def stage_ffn(c, src, dst, w_in, w_out, ln_g, ln_b, li, si, T):
    P, A, nc = c.P, c.A, c.nc
    m0 = A.mark()
    TT = 512
    NJ = DFF // 128
    win = A.alloc([128, 8, 2 * DFF], BF16, "win")
    wout = A.alloc([128, NJ, D], BF16, "wout")
    NWG = 11
    win_b = P.bufs_n("win", NWG)
    wout_b = P.bufs_n("wout", NJ)
    w_in_v = w_in.rearrange("(kc p) n -> p kc n", p=128)
    w_out_v = w_out.rearrange("(j p) d -> p j d", p=128)
    CW = 2 * DFF // NWG
    order = []
    for g in range(NWG // 2 + 1):
        for gg in (g, g + (NWG + 1) // 2):
            if gg < NWG and gg not in order:
                order.append(gg)
    for g in order:
        P.dma(win[:, :, g * CW:(g + 1) * CW], w_in_v[:, :, g * CW:(g + 1) * CW],
              writes=[win_b[g]], sembuf=win_b[g], eng="pool")
    for j in range(NJ):
        P.dma(wout[:, j, :], w_out_v[:, j, :], writes=[wout_b[j]], sembuf=wout_b[j], eng="pool")
    gt, bt, gb_b = load_ln_params(c, ln_g, ln_b, li, si)
    hT = A.alloc([128, 8, TT], BF16, "hT")
    hT_b = P.buf("hT")
    aT = A.alloc([128, NJ, TT], BF16, "aT")
    aT_b = P.bufs_n("aT", NJ)
    NX = 3
    xs = [A.alloc([128, D], F32, "xs") for _ in range(NX)]
    xs_b = P.bufs_n("xs", NX)
    sg = [A.alloc([128, TT], F32, "sg") for _ in range(2)]
    sg_b = P.bufs_n("sg", 2)
    lns = alloc_ln_small(c, 2)
    xi = 0
    cres = 0.5 / ALPHA
    for t in range(T // TT):
        r0 = t * TT
        for s in range(TT // 128):
            x, xb = xs[xi % NX], xs_b[xi % NX]
            xi += 1
            P.dma(x[:], src[r0 + s * 128:r0 + (s + 1) * 128, :], writes=[xb], sembuf=xb)
            transpose_in(c, x, xb, hT, hT_b, s * 128, (0, 1))
        if getattr(c, "cut", 9) <= 1:
            continue
        for j in range(NJ):
            gcol = j * 128
            ucol = DFF + j * 128
            bg, bu = (0, 1) if j % 2 == 0 else (2, 3)

            def mmg(e, col=gcol, bk=bg):
                ins = None
                for kc in range(8):
                    ins = e.matmul(c.ps[bk][:, 0:TT], win[:, kc, col:col + 128], hT[:, kc, :],
                                   start=(kc == 0), stop=(kc == 7))
                return ins
            P.add("pe", mmg, reads=[hT_b, win_b[gcol // CW]], writes=[c.psb[bg]])
            P.add("pe", lambda e, col=ucol, bk=bu: mmg(e, col, bk), reads=[hT_b, win_b[ucol // CW]],
                  writes=[c.psb[bu]])
            s_, s_b = sg[j % 2], sg_b[j % 2]
            P.add("act", lambda e, bk=bg, s_=s_: e.activation(s_[:], c.ps[bk][:, 0:TT], AF.Silu),
                  reads=[c.psb[bg]], writes=[s_b])
            P.add("dve", lambda e, bk=bu, s_=s_, j=j: e.tensor_tensor(aT[:, j, :], c.ps[bk][:, 0:TT], s_[:], ALU.mult),
                  reads=[c.psb[bu], s_b], writes=[aT_b[j]])
        if getattr(c, "cut", 9) <= 2:
            continue
        for s in range(TT // 128):
            bk0, bk1 = (4, 5) if s % 2 == 0 else (6, 7)
            for hh, bk in enumerate((bk0, bk1)):
                def mmo(e, hh=hh, bk=bk, s=s):
                    ins = None
                    for j in range(NJ):
                        ins = e.matmul(c.ps[bk][:, :], aT[:, j, s * 128:(s + 1) * 128],
                                       wout[:, j, hh * 512:(hh + 1) * 512], start=(j == 0), stop=(j == NJ - 1))
                    return ins
                P.add("pe", mmo, reads=list(aT_b) + list(wout_b), writes=[c.psb[bk]])
            if getattr(c, "cut", 9) <= 3:
                continue
            x, xb = xs[xi % NX], xs_b[xi % NX]
            xi += 1
            P.dma(x[:], src[r0 + s * 128:r0 + (s + 1) * 128, :], writes=[xb], sembuf=xb)
            for hh, bk in enumerate((bk0, bk1)):
                P.add("dve", lambda e, hh=hh, bk=bk, x=x: e.scalar_tensor_tensor(
                    x[:, hh * 512:(hh + 1) * 512], c.ps[bk][:, :], cres, x[:, hh * 512:(hh + 1) * 512],
                    ALU.mult, ALU.add), reads=[c.psb[bk], xb], writes=[xb])
            if getattr(c, "cut", 9) <= 4:
                P.dma(dst[r0 + s * 128:r0 + (s + 1) * 128, :], x[:], reads=[xb], sembuf=xb)
                continue
            ln_tile(c, x, xb, gt, bt, gb_b, dst[r0 + s * 128:r0 + (s + 1) * 128, :], lns[s % 2])
    P.barrier()
    A.release(m0)


def load_small_consts(c):
    P, A = c.P, c.A
    c.epsln = A.alloc([128, 1], F32, "epsln")
    c.cst_b = P.buf("cst")
    P.add("dve", lambda e: e.memset(c.epsln[:], LN_EPS / (ALPHA * ALPHA)), writes=[c.cst_b])
    c.one_col = A.alloc([128, 1], F32, "onecol")
    P.add("dve", lambda e: e.memset(c.one_col[:], 1.0), writes=[c.cst_b])


C_IDENT = 0
C_ROT = 128
C_PASTM = 256
C_PASTS = 512
C_ONES = 768
C_UT = 896
C_SMASK = 1024
C_IMASKT = 1152
NF32 = 1280
CB_IDENT = 0
CB_CM = 128
CB_EN = 640
CB_ONES = 2688
NBF = 2816
NCONST = NF32 + NBF
NEG = -30000.0


def make_consts():
    c = np.zeros((128, NCONST), np.float32)
    c[:, C_IDENT:C_IDENT + 128] = np.eye(128)
    rot = np.zeros((128, 128), np.float32)
    for p in range(64):
        rot[p, p + 64] = 1.0
        rot[p + 64, p] = -1.0
    c[:, C_ROT:C_ROT + 128] = rot
    k = np.arange(128)[:, None]
    q = np.arange(256)[None, :]
    c[:, NF32 + CB_CM:NF32 + CB_CM + 256] = np.where(q >= k, 0.0, NEG)
    c[:, NF32 + CB_CM + 256:NF32 + CB_CM + 512] = np.where(q >= k + 128, 0.0, NEG)
    en = np.zeros((128, 16, 128), np.float32)
    for n in range(16):
        en[n, n, :] = 1.0
    c[:, NF32 + CB_EN:NF32 + CB_EN + 2048] = en.reshape(128, 2048)
    j = np.arange(16)[:, None]
    n = np.arange(16)[None, :]
    c[:, C_PASTM:C_PASTM + 256] = np.where(n < j, 0.0, -1e30).reshape(1, 256)
    c[:, C_PASTS:C_PASTS + 256] = np.where(n < j, 1.0, 0.0).reshape(1, 256)
    c[:, C_ONES:C_ONES + 128] = 1.0
    c[:, NF32 + CB_ONES:NF32 + CB_ONES + 128] = 1.0
    c[:, NF32 + CB_IDENT:NF32 + CB_IDENT + 128] = np.eye(128)
    a = np.arange(128)
    c[:, C_UT:C_UT + 128] = (a[:, None] <= a[None, :]).astype(np.float32)
    c[:, C_SMASK:C_SMASK + 128] = np.where(a[:, None] > a[None, :], 0.0, NEG)
    c[:, C_IMASKT:C_IMASKT + 128] = np.where(a[None, :] >= a[:, None], 0.0, NEG)
    return c


def make_rope(T):
    half = 64
    inv = (10000.0 ** (-np.arange(half, dtype=np.float32) / np.float32(half))).astype(np.float32)
    ang = (np.arange(T, dtype=np.float32)[None, :] * inv[:, None]).astype(np.float32)
    cs = np.cos(ang).astype(np.float32)
    sn = np.sin(ang).astype(np.float32)
    out = np.zeros((2, 128, T), np.float32)
    out[0, :64] = cs
    out[0, 64:] = cs
    out[1, :64] = sn
    out[1, 64:] = sn
    return out


def load_consts_full(c, consts_ap):
    P, A = c.P, c.A
    c.cst = A.alloc([128, NF32], F32, "cst")
    c.cstb = P.buf("cstf")
    P.dma(c.cst[:], consts_ap[:, 0:NF32], writes=[c.cstb], sembuf=c.cstb)
    c.ident = c.cst[:, C_IDENT:C_IDENT + 128]
    c.ident_b = c.cstb
    c.cstbf = A.alloc([128, NBF], BF16, "cstbf")
    c.cstbf_b = P.buf("cstbf")
    m = A.mark()
    tmp = A.alloc([128, NBF], F32, "csttmp")
    tb = P.buf("csttmp")
    P.dma(tmp[:], consts_ap[:, NF32:NCONST], writes=[tb], sembuf=tb)
    P.add("dve", lambda e: e.tensor_copy(c.cstbf[:], tmp[:]), reads=[tb], writes=[c.cstbf_b])
    A.release(m)
    c.identb = c.cstbf[:, CB_IDENT:CB_IDENT + 128]
    c.identb_b = c.cstbf_b


def pipeline(n, steps, skew=1):
    for it in range(n + (len(steps) - 1) * skew):
        for k, f in enumerate(steps):
            i = it - k * skew
            if 0 <= i < n:
                f(i)


class _V:
    def __init__(self, ap):
        self.ap = ap

    def __getitem__(self, k):
        return self.ap[k]


def resid_ln(c, src, dst, r0, banks, cres, x, xb, gt, bt, gb_b, st):
    P = c.P
    P.dma(x[:], src[r0:r0 + 128, :], writes=[xb], sembuf=xb)
    for hh, bk in enumerate(banks):
        P.add("dve", lambda e, hh=hh, bk=bk: e.scalar_tensor_tensor(
            x[:, hh * 512:(hh + 1) * 512], c.ps[bk][:, :], cres, x[:, hh * 512:(hh + 1) * 512],
            ALU.mult, ALU.add), reads=[c.psb[bk], xb], writes=[xb])
    ln_tile(c, x, xb, gt, bt, gb_b, dst[r0:r0 + 128, :], st)


def stage_moba_proj(c, src, w_in, rope, qT_d, kT_d, v_d, biasT_d, T):
    P, A = c.P, c.A
    m0 = A.mark()
    TT = 512
    win = A.alloc([128, 8, 3 * D], BF16, "mwin")
    win_b = P.bufs_n("mwin", 6)
    w_in_v = w_in.rearrange("(kc p) n -> p kc n", p=128)
    for g in (2, 3, 0, 1, 4, 5):
        P.dma(win[:, :, g * 512:(g + 1) * 512], w_in_v[:, :, g * 512:(g + 1) * 512],
              writes=[win_b[g]], sembuf=win_b[g], eng="pool")
    hT = A.alloc([128, 8, TT], BF16, "hT")
    hT_b = P.buf("hT")
    NX = 3
    xs = [A.alloc([128, D], F32, "xs") for _ in range(NX)]
    xs_b = P.bufs_n("xs", NX)
    cs = [A.alloc([128, 2, TT], F32, "cs") for _ in range(2)]
    cs_b = P.bufs_n("cs", 2)
    NQF = 4
    qf = [A.alloc([128, TT], F32, "qf") for _ in range(NQF)]
    qf_b = P.bufs_n("qf", NQF)
    t1 = [A.alloc([128, TT], F32, "t1") for _ in range(NQF)]
    t1_b = P.bufs_n("t1", NQF)
    kr32 = [A.alloc([128, TT], F32, "kr32") for _ in range(2)]
    kr32_b = P.bufs_n("kr32", 2)
    q32 = A.alloc([128, 8, TT], F32, "q32")
    q32_b = P.bufs_n("q32", 8)
    ob = [A.alloc([128, TT], BF16, "ob") for _ in range(3)]
    ob_b = P.bufs_n("ob", 3)
    vb = [A.alloc([128, D], BF16, "vb") for _ in range(2)]
    vb_b = P.bufs_n("vb", 2)
    kmean = A.alloc([128, 8, 16], F32, "kmean")
    kmean_b = P.buf("kmean")
    P.add("dve", lambda e: e.memset(kmean[:], 0.0), writes=[kmean_b])
    gm = [A.alloc([128, 8, 16], F32, "gm") for _ in range(2)]
    gm_b = P.bufs_n("gm", 2)
    top8 = [A.alloc([128, 8, 8], F32, "top8") for _ in range(2)]
    selt = [A.alloc([128, 8, 16], F32, "selt") for _ in range(2)]
    bT = [A.alloc([16, 8, 128], BF16, "bT") for _ in range(2)]
    bT_b = P.bufs_n("bT", 2)
    xi = 0
    oi = 0
    for t in range(T // TT):
        r0 = t * TT
        for s in range(4):
            x, xb = xs[xi % NX], xs_b[xi % NX]
            xi += 1
            P.dma(x[:], src[r0 + s * 128:r0 + (s + 1) * 128, :], writes=[xb], sembuf=xb)
            transpose_in(c, x, xb, hT, hT_b, s * 128, (0, 1))
        ct, ctb = cs[t % 2], cs_b[t % 2]
        P.dma(ct[:], rope[:, :, r0:r0 + TT].rearrange("a p t -> p a t"), writes=[ctb], sembuf=ctb)
        items = [(qk, h) for qk in (1, 0) for h in range(8)]

        def pm0(it, r0=r0, ct=ct, ctb=ctb):
            qk, h = items[it]
            col = qk * D + h * 128
            bk = 2 + (it % 2)

            def mmp(e):
                ins = None
                for kc in range(8):
                    ins = e.matmul(c.ps[bk][:, :], win[:, kc, col:col + 128], hT[:, kc, :],
                                   start=(kc == 0), stop=(kc == 7))
                return ins
            P.add("pe", mmp, reads=[hT_b, win_b[col // 512]], writes=[c.psb[bk]])
            f, fb = qf[it % NQF], qf_b[it % NQF]
            P.add("act", lambda e: e.activation(f[:], c.ps[bk][:, :], AF.Copy), reads=[c.psb[bk]], writes=[fb])

        def pm1(it, r0=r0, ct=ct, ctb=ctb):
            qk, h = items[it]
            bkr = 4 + (it % 2)
            f, fb = qf[it % NQF], qf_b[it % NQF]
            P.add("pe", lambda e: e.matmul(c.ps[bkr][:, :], c.cst[:, C_ROT:C_ROT + 128], f[:], start=True, stop=True),
                  reads=[fb, c.cstb], writes=[c.psb[bkr]])
            tt, ttb = t1[it % NQF], t1_b[it % NQF]
            P.add("dve", lambda e: e.tensor_tensor(tt[:], f[:], ct[:, 0, :], ALU.mult), reads=[fb, ctb], writes=[ttb])

        def pm2(it, r0=r0, ct=ct, ctb=ctb, t=t):
            qk, h = items[it]
            bkr = 4 + (it % 2)
            tt, ttb = t1[it % NQF], t1_b[it % NQF]
            if qk == 1:
                dst32, dst32_b = kr32[it % 2], kr32_b[it % 2]
                d32 = dst32[:]
            else:
                d32, dst32_b = q32[:, h, :], q32_b[h]
            P.add("dve", lambda e: e.tensor_tensor(d32, c.ps[bkr][:, :], ct[:, 1, :], ALU.mult),
                  reads=[c.psb[bkr], ctb], writes=[dst32_b])
            P.add("dve", lambda e: e.tensor_tensor(d32, d32, tt[:], ALU.add), reads=[ttb, dst32_b], writes=[dst32_b])
            o, o_b = ob[it % 3], ob_b[it % 3]
            P.add("act", lambda e: e.activation(o[:], d32, AF.Copy), reads=[dst32_b], writes=[o_b])
            if qk == 1:
                P.add("dve", lambda e: e.tensor_reduce(
                    kmean[:, h, 2 * t:2 * t + 2], d32.rearrange("p (a b) -> p a b", a=2), AX.X, ALU.add),
                    reads=[dst32_b], writes=[kmean_b])
                P.dma(kT_d[h, :, r0:r0 + TT], o[:], reads=[o_b], sembuf=o_b)
            else:
                P.dma(qT_d[h, :, r0:r0 + TT], o[:], reads=[o_b], sembuf=o_b)
        pipeline(16, [pm0, pm1, pm2])
        for s in range(4):
            for hh in range(2):
                def mmv(e, s=s, hh=hh):
                    ins = None
                    for kc in range(8):
                        ins = e.matmul(c.ps[hh][:, :], hT[:, kc, s * 128:(s + 1) * 128],
                                       win[:, kc, 2 * D + hh * 512:2 * D + (hh + 1) * 512],
                                       start=(kc == 0), stop=(kc == 7))
                    return ins
                P.add("pe", mmv, reads=[hT_b, win_b[4 + hh]], writes=[c.psb[hh]])
            v, v_b = vb[s % 2], vb_b[s % 2]
            P.add("act", lambda e, v=v: e.activation(v[:, 0:512], c.ps[0][:, :], AF.Copy),
                  reads=[c.psb[0]], writes=[v_b])
            P.add("dve", lambda e, v=v: e.tensor_copy(v[:, 512:1024], c.ps[1][:, :]),
                  reads=[c.psb[1]], writes=[v_b])
            P.dma(v_d[r0 + s * 128:r0 + (s + 1) * 128, :], v[:], reads=[v_b], sembuf=v_b)
        for s in range(4):
            jb = (r0 + s * 128) // 256
            g, g_b = gm[s % 2], gm_b[s % 2]
            t8, sl = top8[s % 2], selt[s % 2]

            def mmgate(e, s=s):
                ins = None
                for h in range(8):
                    ins = e.matmul(c.ps[6][:, h * 16:(h + 1) * 16], q32[:, h, s * 128:(s + 1) * 128],
                                   kmean[:, h, :], start=True, stop=True)
                return ins
            P.add("pe", mmgate, reads=list(q32_b) + [kmean_b], writes=[c.psb[6]])
            pm = c.cst[:, C_PASTM + jb * 16:C_PASTM + jb * 16 + 16]
            psel = c.cst[:, C_PASTS + jb * 16:C_PASTS + jb * 16 + 16]
            P.add("dve", lambda e, g=g, pm=pm: e.tensor_tensor(
                g[:], c.ps[6][:, 0:128].rearrange("p (h n) -> p h n", h=8),
                pm.rearrange("p (o n) -> p o n", o=1).to_broadcast([128, 8, 16]), ALU.add),
                reads=[c.psb[6], c.cstb], writes=[g_b])
            for h in range(8):
                P.add("dve", lambda e, g=g, t8=t8, h=h: e.max(t8[:, h, :], g[:, h, :]), reads=[g_b], writes=[g_b])
            P.add("dve", lambda e, g=g, t8=t8, sl=sl: e.tensor_tensor(
                sl[:], g[:], t8[:, :, 2:3].to_broadcast([128, 8, 16]), ALU.is_ge), reads=[g_b], writes=[g_b])
            P.add("dve", lambda e, sl=sl, psel=psel: e.tensor_tensor(
                sl[:], sl[:], psel.rearrange("p (o n) -> p o n", o=1).to_broadcast([128, 8, 16]), ALU.mult),
                reads=[g_b, c.cstb], writes=[g_b])
            P.add("dve", lambda e, sl=sl: e.tensor_scalar(sl[:], sl[:], -1.0, -NEG, ALU.add, ALU.mult),
                  reads=[g_b], writes=[g_b])
            for half in range(2):
                def tr(e, half=half, sl=sl):
                    ins = None
                    for hq in range(4):
                        h = half * 4 + hq
                        ins = e.transpose(c.ps[7][0:16, hq * 128:(hq + 1) * 128], sl[:, h, :], c.ident)
                    return ins
                P.add("pe", tr, reads=[g_b, c.ident_b], writes=[c.psb[7]])
                b, b_b = bT[s % 2], bT_b[s % 2]
                P.add("act", lambda e, b=b, half=half: e.activation(
                    b[:, half * 4:half * 4 + 4, :], c.ps[7][0:16, :].rearrange("p (a q) -> p a q", a=4), AF.Copy),
                    reads=[c.psb[7]], writes=[b_b])
            b, b_b = bT[s % 2], bT_b[s % 2]
            P.dma(biasT_d[:, :, r0 + s * 128:r0 + (s + 1) * 128].rearrange("h n q -> n h q"), b[:],
                  reads=[b_b], sembuf=b_b)
    P.barrier()
    A.release(m0)


def stage_moba_attn(c, qT_d, kT_d, v_d, biasT_d, oT_d, T):
    P, A = c.P, c.A
    m0 = A.mark()
    NBLK = T // 256
    scale = 128 ** -0.5
    kT = [A.alloc([128, T], BF16, "kT") for _ in range(2)]
    qT = [A.alloc([128, T], BF16, "qT") for _ in range(2)]
    vh = [A.alloc([128, T // 128, 128], BF16, "vh") for _ in range(2)]
    bT = [A.alloc([128, T], BF16, "bTh") for _ in range(2)]
    kT_b, qT_b, vh_b, bT_b = (P.bufs_n("kT", 2), P.bufs_n("qT", 2), P.bufs_n("vh", 2), P.bufs_n("bTh", 2))
    for i in range(2):
        P.add("dve", lambda e, i=i: e.memset(bT[i][:], 0.0), writes=[bT_b[i]])
    pT = [A.alloc([128, 256], BF16, "pT") for _ in range(3)]
    pT_b = P.bufs_n("pT", 3)
    rd = [A.alloc([128, 256], F32, "rd") for _ in range(2)]
    rd_b = P.bufs_n("rd", 2)
    oo = [A.alloc([128, 256], BF16, "oo") for _ in range(2)]
    oo_b = P.bufs_n("oo", 2)
    onesb = c.cstbf[:, CB_ONES:CB_ONES + 128]
    for h in range(8):
        hb = h % 2
        P.dma(kT[hb][:], kT_d[h, :, :], writes=[kT_b[hb]], sembuf=kT_b[hb])
        P.dma(qT[hb][:], qT_d[h, :, :], writes=[qT_b[hb]], sembuf=qT_b[hb])
        P.dma(vh[hb][:], v_d[:, h * 128:(h + 1) * 128].rearrange("(c p) d -> p c d", p=128),
              writes=[vh_b[hb]], sembuf=vh_b[hb])
        P.dma(bT[hb][0:16, :], biasT_d[h, :, :], writes=[bT_b[hb]], sembuf=bT_b[hb])
        pairs = [(j, kt) for j in range(NBLK) for kt in range(2 * j + 2)]

        def step_s(idx, hb=hb):
            j, kt = pairs[idx]
            q0 = j * 256
            sb_ = idx % 3
            n = kt // 2
            own = (n == j)

            def mms(e):
                e.matmul(c.ps[sb_][:, 0:256], kT[hb][:, kt * 128:(kt + 1) * 128], qT[hb][:, q0:q0 + 256],
                         start=True, stop=False)
                if own:
                    return e.matmul(c.ps[sb_][:, 0:256], c.identb,
                                    c.cstbf[:, CB_CM + (kt % 2) * 256:CB_CM + (kt % 2) * 256 + 256],
                                    start=False, stop=True)
                return e.matmul(c.ps[sb_][:, 0:256], c.cstbf[:, CB_EN + n * 128:CB_EN + (n + 1) * 128],
                                bT[hb][:, q0:q0 + 256], start=False, stop=True)
            P.add("pe", mms, reads=[kT_b[hb], qT_b[hb], bT_b[hb], c.cstbf_b], writes=[c.psb[sb_]])
            p, p_b = pT[idx % 3], pT_b[idx % 3]
            P.add("act", lambda e: e.activation(p[:], c.ps[sb_][:, 0:256], AF.Exp, scale=scale),
                  reads=[c.psb[sb_]], writes=[p_b])

        def step_pv(idx, hb=hb, h=h):
            j, kt = pairs[idx]
            q0 = j * 256
            nkt = 2 * j + 2
            ob_, db_ = 3 + (j % 2), 5 + (j % 2)
            p, p_b = pT[idx % 3], pT_b[idx % 3]

            def mmpv(e):
                e.matmul(c.ps[ob_][:, 0:256], vh[hb][:, kt, :], p[:], start=(kt == 0), stop=(kt == nkt - 1))
                return e.matmul(c.ps[db_][:, 0:256], onesb, p[:], start=(kt == 0), stop=(kt == nkt - 1))
            P.add("pe", mmpv, reads=[vh_b[hb], p_b, c.cstbf_b], writes=[c.psb[ob_], c.psb[db_]])
            if kt == nkt - 1:
                r, r_b = rd[j % 2], rd_b[j % 2]
                o, o_b = oo[j % 2], oo_b[j % 2]
                P.add("dve", lambda e: e.reciprocal(r[:], c.ps[db_][:, 0:256]), reads=[c.psb[db_]], writes=[r_b])
                P.add("dve", lambda e: e.tensor_tensor(o[:], c.ps[ob_][:, 0:256], r[:], ALU.mult),
                      reads=[c.psb[ob_], r_b], writes=[o_b])
                P.dma(oT_d[h, :, q0:q0 + 256], o[:], reads=[o_b], sembuf=o_b)
        pipeline(len(pairs), [step_s, step_pv])
    P.barrier()
    A.release(m0)


def stage_outproj_ln(c, actT_d, NK, w, src, dst, ln_g, ln_b, li, si, cres, T):
    P, A = c.P, c.A
    m0 = A.mark()
    TT = 512
    wo = A.alloc([128, NK, D], BF16, "wo")
    wo_b = P.bufs_n("wo", NK)
    w_v = w.rearrange("(j p) d -> p j d", p=128)
    for j in range(NK):
        P.dma(wo[:, j, :], w_v[:, j, :], writes=[wo_b[j]], sembuf=wo_b[j], eng="pool")
    gt, bt, gb_b = load_ln_params(c, ln_g, ln_b, li, si)
    aT = [A.alloc([128, NK, TT], BF16, "oaT") for _ in range(2)]
    aT_b = P.bufs_n("oaT", 2)
    xs = [A.alloc([128, D], F32, "xs") for _ in range(3)]
    xs_b = P.bufs_n("xs", 3)
    lns = alloc_ln_small(c, 2)
    xi = 0
    for t in range(T // TT):
        r0 = t * TT
        a, a_b = aT[t % 2], aT_b[t % 2]
        P.dma(a[:], actT_d[:, :, r0:r0 + TT].rearrange("k p t -> p k t"), writes=[a_b], sembuf=a_b)
        def o0(s, a=a, a_b=a_b):
            banks = (4, 5) if s % 2 == 0 else (6, 7)
            for hh, bk in enumerate(banks):
                def mmo(e, hh=hh, bk=bk):
                    ins = None
                    for j in range(NK):
                        ins = e.matmul(c.ps[bk][:, :], a[:, j, s * 128:(s + 1) * 128],
                                       wo[:, j, hh * 512:(hh + 1) * 512], start=(j == 0), stop=(j == NK - 1))
                    return ins
                P.add("pe", mmo, reads=[a_b] + list(wo_b), writes=[c.psb[bk]])

        def o1(s, r0=r0):
            banks = (4, 5) if s % 2 == 0 else (6, 7)
            k = (r0 // 128 + s)
            x, xb = xs[k % 3], xs_b[k % 3]
            resid_ln(c, src, dst, r0 + s * 128, banks, cres, x, xb, gt, bt, gb_b, lns[s % 2])
        pipeline(4, [o0, o1])
    P.barrier()
    A.release(m0)


GQ, GK, GV, GZ, GB_, GA_ = 0, 1024, 2048, 4096, 6144, 6160
GPROJ = 6176


def psbf(c, bk):
    return c.ps[bk][:].bitcast(BF16)


def stage_gdn_proj(c, src, w_in, conv_w, a_log, dt_bias, qT_d, kT_d, ktok_d, vtok_d, z_d, g_d, beta_d, T):
    P, A = c.P, c.A
    m0 = A.mark()
    TT = 512
    win = A.alloc([128, 8, GPROJ], BF16, "gwin")
    NG = 13
    win_b = P.bufs_n("gwin", NG)
    w_in_v = w_in.rearrange("(kc p) n -> p kc n", p=128)
    for g in range(NG):
        lo, hi = g * 512, min(GPROJ, (g + 1) * 512)
        P.dma(win[:, :, lo:hi], w_in_v[:, :, lo:hi], writes=[win_b[g]], sembuf=win_b[g], eng="pool")
    cw = A.alloc([128, 32, 4], F32, "cw")
    cw_b = P.buf("cw")
    cwl = A.alloc([32, 4, 128], F32, "cwl")
    cwl_b = P.buf("cwl")
    P.dma(cwl[:], conv_w.rearrange("i (cc p) -> cc i p", p=128), writes=[cwl_b], sembuf=cwl_b)

    def trcw(e):
        ins = None
        for i in range(4):
            ins = e.transpose(c.ps[7][:, i * 32:(i + 1) * 32], cwl[:, i, :], c.cst[0:32, C_IDENT:C_IDENT + 32])
        return ins
    P.add("pe", trcw, reads=[cwl_b, c.cstb], writes=[c.psb[7]])
    P.add("dve", lambda e: e.tensor_copy(cw[:].rearrange("p cc i -> p i cc"),
                                         c.ps[7][:, 0:128].rearrange("p (i cc) -> p i cc", i=4)),
          reads=[c.psb[7]], writes=[cw_b])
    Wd = A.alloc([128, 32, 4, 128], BF16, "Wd")
    Wd_b = P.buf("Wd")
    for cc in range(32):
        for i in range(4):
            P.add("dve", lambda e, cc=cc, i=i: e.tensor_scalar(Wd[:, cc, i, :], c.ident, cw[:, cc, i:i + 1], None, ALU.mult),
                  reads=[cw_b, c.cstb], writes=[Wd_b])
    negA = A.alloc([128, 16], F32, "negA")
    dtb = A.alloc([128, 16], F32, "dtb")
    gc_b = P.buf("gconst")
    b1, b2 = P.buf("alog"), P.buf("dtb")
    P.dma(negA[:], a_log.partition_broadcast(128), writes=[b1], sembuf=b1)
    P.dma(dtb[:], dt_bias.partition_broadcast(128), writes=[b2], sembuf=b2)
    P.add("act", lambda e: e.activation(negA[:], negA[:], AF.Exp), reads=[b1], writes=[gc_b])
    P.add("dve", lambda e: e.tensor_scalar(negA[:], negA[:], -1.0, None, ALU.mult), reads=[gc_b, b2], writes=[gc_b])
    epsq = A.alloc([128, 2], F32, "epsq")
    P.add("dve", lambda e: e.memset(epsq[:, 0:1], 128.0 * RMS_EPS), writes=[gc_b])
    P.add("dve", lambda e: e.memset(epsq[:, 1:2], RMS_EPS), reads=[gc_b], writes=[gc_b])
    halo = A.alloc([128, 32, 4], BF16, "halo")
    halo_b = P.bufs_n("halo", 32)
    P.add("dve", lambda e: e.memset(halo[:], 0.0), writes=list(halo_b))
    hT = A.alloc([128, 8, TT], BF16, "hT")
    hT_b = P.buf("hT")
    xs = [A.alloc([128, D], F32, "xs") for _ in range(3)]
    xs_b = P.bufs_n("xs", 3)
    NXP, NQS, NOB = 3, 3, 5
    xpre = [A.alloc([128, TT + 4], BF16, "xpre") for _ in range(NXP)]
    xpre_b = P.bufs_n("xpre", NXP)
    qs = [A.alloc([128, TT], F32, "qs") for _ in range(NQS)]
    qs_b = P.bufs_n("qs", NQS)
    sq = [A.alloc([128, TT], BF16, "sq") for _ in range(NQS)]
    sq_b = P.bufs_n("sq", NQS)
    rn = [A.alloc([128, TT], F32, "rn") for _ in range(2)]
    rn_b = P.bufs_n("rn", 2)
    ob = [A.alloc([128, TT], BF16, "gob") for _ in range(NOB)]
    ob_b = P.bufs_n("gob", NOB)
    tk = [A.alloc([128, 4, 128], BF16, "tk") for _ in range(3)]
    tk_b = P.bufs_n("tk", 3)
    zt = [A.alloc([128, 512], F32, "zt") for _ in range(3)]
    zt_b = P.bufs_n("zt", 3)
    zi = 0
    sm = [A.alloc([128, 4, 16], F32, "gsm") for _ in range(2)]
    sm_b = P.bufs_n("gsm", 2)
    onesb = c.cstbf[:, CB_ONES:CB_ONES + 128]
    xi = oi = ti = 0
    for t in range(T // TT):
        r0 = t * TT
        for s in range(4):
            x, xb = xs[xi % 3], xs_b[xi % 3]
            xi += 1
            P.dma(x[:], src[r0 + s * 128:r0 + (s + 1) * 128, :], writes=[xb], sembuf=xb)
            transpose_in(c, x, xb, hT, hT_b, s * 128, (0, 1))
        def s0(cc, r0=r0):
            col = cc * 128
            bk = 2 + (cc % 2)

            def mmp(e):
                ins = None
                for kc in range(8):
                    ins = e.matmul(c.ps[bk][:, :], win[:, kc, col:col + 128], hT[:, kc, :],
                                   start=(kc == 0), stop=(kc == 7))
                return ins
            P.add("pe", mmp, reads=[hT_b, win_b[col // 512]], writes=[c.psb[bk]])
            xp, xp_b = xpre[cc % NXP], xpre_b[cc % NXP]
            P.add("dve", lambda e: e.tensor_copy(xp[:, 0:4], halo[:, cc, :]), reads=[halo_b[cc]], writes=[xp_b])
            P.add("act", lambda e: e.activation(xp[:, 4:TT + 4], c.ps[bk][:, :], AF.Copy),
                  reads=[c.psb[bk]], writes=[xp_b])
            P.add("dve", lambda e: e.tensor_copy(halo[:, cc, :], xp[:, TT:TT + 4]), reads=[xp_b], writes=[halo_b[cc]])

        def s1(cc, r0=r0):
            bkc = 4 + (cc % 2)
            xp, xp_b = xpre[cc % NXP], xpre_b[cc % NXP]

            def mmc(e):
                ins = None
                for i in range(4):
                    ins = e.matmul(c.ps[bkc][:, :], Wd[:, cc, i, :], xp[:, 1 + i:1 + i + TT], start=(i == 0), stop=(i == 3))
                return ins
            P.add("pe", mmc, reads=[xp_b, Wd_b], writes=[c.psb[bkc]])
            o, o_b = ob[cc % NOB], ob_b[cc % NOB]
            if cc < 16:
                q_, q_b = qs[cc % NQS], qs_b[cc % NQS]
                s_, s_b = sq[cc % NQS], sq_b[cc % NQS]
                P.add("act", lambda e: e.activation(q_[:], c.ps[bkc][:, :], AF.Silu), reads=[c.psb[bkc]], writes=[q_b])
                P.add("act", lambda e: e.activation(s_[:], q_[:], AF.Square), reads=[q_b], writes=[s_b])
            else:
                P.add("act", lambda e: e.activation(o[:], c.ps[bkc][:, :], AF.Silu), reads=[c.psb[bkc]], writes=[o_b])

        def s2(cc, r0=r0):
            if cc >= 16:
                return
            o, o_b = ob[cc % NOB], ob_b[cc % NOB]
            q_, q_b = qs[cc % NQS], qs_b[cc % NQS]
            s_, s_b = sq[cc % NQS], sq_b[cc % NQS]
            r_, r_b = rn[cc % 2], rn_b[cc % 2]
            b6 = 6 if cc % 2 == 0 else 0
            P.add("pe", lambda e: e.matmul(c.ps[b6][:, :], onesb, s_[:], start=True, stop=True),
                  reads=[s_b, c.cstbf_b], writes=[c.psb[b6]])
            if cc < 8:
                P.add("act", lambda e: e.activation(r_[:], c.ps[b6][:, :], AF.Sqrt, bias=epsq[:, 0:1], scale=128.0),
                      reads=[c.psb[b6], gc_b], writes=[r_b])
            else:
                P.add("act", lambda e: e.activation(r_[:], c.ps[b6][:, :], AF.Sqrt, bias=epsq[:, 1:2], scale=1.0),
                      reads=[c.psb[b6], gc_b], writes=[r_b])
            P.add("dve", lambda e: e.reciprocal(r_[:], r_[:]), reads=[r_b], writes=[r_b])
            P.add("dve", lambda e: e.tensor_tensor(o[:], q_[:], r_[:], ALU.mult),
                  reads=[r_b, q_b], writes=[o_b])
            hd = cc % 8
            P.dma((qT_d if cc < 8 else kT_d)[hd, :, r0:r0 + TT], o[:], reads=[o_b], sembuf=o_b)

        def s3(cc, r0=r0):
            if cc < 8:
                return
            o, o_b = ob[cc % NOB], ob_b[cc % NOB]
            b7 = 7 if cc % 2 == 0 else 1
            pb = psbf(c, b7)

            def trk(e):
                ins = None
                for s in range(4):
                    ins = e.transpose(pb[:, s * 128:(s + 1) * 128], o[:, s * 128:(s + 1) * 128], c.identb)
                return ins
            P.add("pe", trk, reads=[o_b, c.cstbf_b], writes=[c.psb[b7]])
            k_, k_b = tk[cc % 3], tk_b[cc % 3]
            P.add("dve", lambda e: e.tensor_copy(k_[:], pb[:, 0:512].rearrange("p (s d) -> p s d", s=4)),
                  reads=[c.psb[b7]], writes=[k_b])
            if cc < 16:
                dd = ktok_d[r0:r0 + TT, (cc - 8) * 128:(cc - 7) * 128]
            else:
                dd = vtok_d[r0:r0 + TT, (cc - 16) * 128:(cc - 15) * 128]
            P.dma(dd.rearrange("(s p) d -> p s d", p=128), k_[:], reads=[k_b], sembuf=k_b)
        pipeline(32, [s0, s1, s2, s3])
        for s in range(4):
            for zq in range(4):
                z_, z_b = zt[zi % 3], zt_b[zi % 3]
                zi += 1
                bk = zq

                def mmz(e, s=s, zq=zq, bk=bk):
                    ins = None
                    for kc in range(8):
                        ins = e.matmul(c.ps[bk][:, :], hT[:, kc, s * 128:(s + 1) * 128],
                                       win[:, kc, GZ + zq * 512:GZ + (zq + 1) * 512], start=(kc == 0), stop=(kc == 7))
                    return ins
                P.add("pe", mmz, reads=[hT_b] + [win_b[(GZ + zq * 512) // 512]], writes=[c.psb[bk]])
                P.add("act", lambda e, z_=z_, bk=bk: e.activation(z_[:], c.ps[bk][:, :], AF.Silu),
                      reads=[c.psb[bk]], writes=[z_b])
                P.dma(z_d[r0 + s * 128:r0 + (s + 1) * 128, zq * 512:(zq + 1) * 512], z_[:], reads=[z_b], sembuf=z_b)

            def mmba(e, s=s):
                ins = None
                for kc in range(8):
                    ins = e.matmul(c.ps[6][:, 0:32], hT[:, kc, s * 128:(s + 1) * 128], win[:, kc, GB_:GB_ + 32],
                                   start=(kc == 0), stop=(kc == 7))
                return ins
            P.add("pe", mmba, reads=[hT_b, win_b[12]], writes=[c.psb[6]])
            m_, m_b = sm[s % 2], sm_b[s % 2]
            P.add("act", lambda e, m_=m_: e.activation(m_[:, 0, :], c.ps[6][:, 0:16], AF.Sigmoid), reads=[c.psb[6]], writes=[m_b])
            P.add("dve", lambda e, m_=m_: e.tensor_tensor(m_[:, 2, :], c.ps[6][:, 16:32], dtb[:], ALU.add),
                  reads=[c.psb[6], b2, gc_b], writes=[m_b])
            P.add("act", lambda e, m_=m_: e.activation(m_[:, 2, :], m_[:, 2, :], AF.Exp), reads=[m_b], writes=[m_b])
            P.add("act", lambda e, m_=m_: e.activation(m_[:, 3, :], m_[:, 2, :], AF.Ln, bias=c.one_col[:], scale=1.0),
                  reads=[m_b, c.cst_b], writes=[m_b])
            P.add("dve", lambda e, m_=m_: e.tensor_tensor(m_[:, 1, :], m_[:, 3, :], negA[:], ALU.mult),
                  reads=[m_b, gc_b], writes=[m_b])
            P.dma(beta_d[r0 + s * 128:r0 + (s + 1) * 128, :], m_[:, 0, :], reads=[m_b], sembuf=m_b)
            P.dma(g_d[r0 + s * 128:r0 + (s + 1) * 128, :], m_[:, 1, :], reads=[m_b], sembuf=m_b)
    P.barrier()
    A.release(m0)


def stage_gdn_scan(c, qT_d, kT_d, ktok_d, vtok_d, g_d, beta_d, o_d, T):
    P, A = c.P, c.A
    m0 = A.mark()
    NCH = T // 128
    H = 16
    UT = c.cst[:, C_UT:C_UT + 128]
    ones32 = c.cst[:, C_ONES:C_ONES + 128]
    SM = c.cst[:, C_SMASK:C_SMASK + 128]
    IMT = c.cst[:, C_IMASKT:C_IMASKT + 128]

    def f32t(name):
        return A.alloc([128, H, 128], F32, name), P.bufs_n(name, 4)

    def bf16t(name):
        return A.alloc([128, H, 128], BF16, name), P.bufs_n(name, 4)
    qTc = [A.alloc([128, 8, 128], BF16, "qTc") for _ in range(2)]
    kTc = [A.alloc([128, 8, 128], BF16, "kTc") for _ in range(2)]
    ktk = [A.alloc([128, 8, 128], BF16, "ktk") for _ in range(2)]
    vtk = [A.alloc([128, H, 128], BF16, "vtk") for _ in range(2)]
    gb = [A.alloc([128, 2, H], F32, "gb") for _ in range(2)]
    qTc_b, kTc_b, ktk_b, vtk_b = P.bufs_n("qTc", 2), P.bufs_n("kTc", 2), P.bufs_n("ktk", 2), P.bufs_n("vtk", 2)
    g_b, be_b = P.bufs_n("gld", 2), P.bufs_n("bld", 2)
    def f32rt(name):
        return A.alloc([128, H, 128], F32R, name), P.bufs_n(name, 4)
    F1, F1b = f32rt("F1")
    F2, F2b = f32t("F2")
    F3, F3b = f32t("F3")
    Rp = [f32rt("R%d" % i) for i in range(2)]
    RTp = [f32rt("RT%d" % i) for i in range(2)]
    Y, Yb = f32rt("Y")
    identr_t = A.alloc([128, 128], F32R, "identr")
    identr_b = P.buf("identr")
    P.add("dve", lambda e: e.tensor_copy(identr_t[:], c.ident), reads=[c.cstb], writes=[identr_b])
    identr = identr_t[:]
    attnT, attnT_b = bf16t("attnT")
    TTb, TTb_b = bf16t("TTb")
    vbt, vbt_b = bf16t("vbt")
    kw, kw_b = bf16t("kw")
    wT, wT_b = bf16t("wT")
    qdT, qdT_b = bf16t("qdT")
    kdec, kdec_b = bf16t("kdec")
    vnew, vnew_b = bf16t("vnew")
    osb, osb_b = f32t("osb")
    S, S_b = f32t("S")
    Sbf, Sbf_b = bf16t("Sbf")
    sm = A.alloc([128, 8, H], F32, "ssm")
    sm_b = P.buf("ssm")
    P.add("dve", lambda e: e.memset(S[:], 0.0), writes=S_b)
    P.add("dve", lambda e: e.memset(Sbf[:], 0.0), writes=Sbf_b)

    def flat(t):
        return t[:].rearrange("p h d -> p (h d)")

    def bank4(q):
        return c.ps[q][:].rearrange("p (h d) -> p h d", h=4)

    def hs4(q):
        return slice(4 * q, 4 * q + 4)

    for ch in range(NCH):
        c0 = ch * 128
        pb = ch % 2
        P.dma(qTc[pb][:], qT_d[:, :, c0:c0 + 128].rearrange("h p t -> p h t"), writes=[qTc_b[pb]], sembuf=qTc_b[pb])
        P.dma(kTc[pb][:], kT_d[:, :, c0:c0 + 128].rearrange("h p t -> p h t"), writes=[kTc_b[pb]], sembuf=kTc_b[pb])
        P.dma(ktk[pb][:], ktok_d[c0:c0 + 128, :].rearrange("p (h d) -> p h d", h=8), writes=[ktk_b[pb]], sembuf=ktk_b[pb])
        P.dma(vtk[pb][:], vtok_d[c0:c0 + 128, :].rearrange("p (h d) -> p h d", h=H), writes=[vtk_b[pb]], sembuf=vtk_b[pb])
        P.dma(gb[pb][:, 0, :], g_d[c0:c0 + 128, :], writes=[g_b[pb]], sembuf=g_b[pb])
        P.dma(gb[pb][:, 1, :], beta_d[c0:c0 + 128, :], writes=[be_b[pb]], sembuf=be_b[pb])
        g = gb[pb][:, 0, :]
        beta = gb[pb][:, 1, :]
        def mm1(e, g=g):
            e.matmul(c.ps[0][:, 0:16], UT, g, start=True, stop=True)
            return e.matmul(c.ps[0][:, 16:32], ones32, g, start=True, stop=True)
        P.add("pe", mm1, reads=[g_b[pb], c.cstb], writes=[c.psb[0]])
        P.add("act", lambda e: e.activation(sm[:, 0, :], c.ps[0][:, 0:16], AF.Copy), reads=[c.psb[0]], writes=[sm_b])
        P.add("act", lambda e: e.activation(sm[:, 1, :], c.ps[0][:, 0:16], AF.Identity, scale=-1.0), reads=[c.psb[0]], writes=[sm_b])
        P.add("act", lambda e: e.activation(sm[:, 2, :], c.ps[0][:, 0:16], AF.Exp), reads=[c.psb[0]], writes=[sm_b])
        P.add("act", lambda e: e.activation(sm[:, 3, :], c.ps[0][:, 16:32], AF.Exp), reads=[c.psb[0]], writes=[sm_b])
        P.add("act", lambda e: e.activation(sm[:, 7, :], c.ps[0][:, 16:32], AF.Identity, bias=0.0, scale=1.0),
              reads=[c.psb[0]], writes=[sm_b])
        P.add("dve", lambda e: e.tensor_tensor(sm[:, 7, :], sm[:, 7, :], sm[:, 0, :], ALU.subtract),
              reads=[sm_b], writes=[sm_b])
        P.add("act", lambda e: e.activation(sm[:, 4, :], sm[:, 7, :], AF.Exp), reads=[sm_b], writes=[sm_b])
        P.add("dve", lambda e, beta=beta: e.tensor_tensor(sm[:, 5, :], beta, sm[:, 2, :], ALU.mult),
              reads=[sm_b, be_b[pb]], writes=[sm_b])
        P.add("dve", lambda e, beta=beta: e.tensor_scalar(sm[:, 6, :], beta, -1.0, None, ALU.mult),
              reads=[sm_b, be_b[pb]], writes=[sm_b])
        for q in range(4):
            P.add("dve", lambda e, g=g, q=q: e.tensor_tensor(
                F3[:, hs4(q), :], UT.rearrange("p (o i) -> p o i", o=1).to_broadcast([128, 4, 128]),
                g[:, 4 * q:4 * q + 4].rearrange("p (h o) -> p h o", o=1).to_broadcast([128, 4, 128]), ALU.mult),
                reads=[g_b[pb], c.cstb], writes=[F3b[q]])
            P.add("pe", lambda e, q=q: e.matmul(c.ps[4 + q][:, :], ones32, F3[:, hs4(q), :].rearrange("p h d -> p (h d)"),
                                                start=True, stop=True), reads=[F3b[q], c.cstb], writes=[c.psb[4 + q]])
        gcrow_b = [c.psb[4 + q] for q in range(4)]
        for q in range(4):
            P.add("act", lambda e, q=q: e.activation(F3[:, hs4(q), :], bank4(4 + q), AF.Exp),
                  reads=[gcrow_b[q]], writes=[F3b[q]])
            P.add("dve", lambda e, q=q: e.tensor_tensor(
                F2[:, hs4(q), :], bank4(4 + q),
                IMT.rearrange("p (o i) -> p o i", o=1).to_broadcast([128, 4, 128]), ALU.add),
                reads=[gcrow_b[q], c.cstb], writes=[F2b[q]])
        for par in range(2):
            P.add("pool", lambda e, par=par, pb=pb: e.tensor_tensor(qdT[:, par::2, :], qTc[pb][:], F3[:, par::2, :], ALU.mult),
                  reads=list(F3b) + [qTc_b[pb]], writes=qdT_b)
        for h in range(H):
            P.add("act", lambda e, h=h: e.activation(F2[:, h, :], F2[:, h, :], AF.Exp, bias=sm[:, 1, h:h + 1], scale=1.0),
                  reads=[F2b[h // 4], sm_b], writes=[F2b[h // 4]])
        def mmkk(e, pb=pb):
            ins = None
            for hk in range(8):
                ins = e.matmul(c.ps[hk // 4][:, (hk % 4) * 128:(hk % 4 + 1) * 128], kTc[pb][:, hk, :], kTc[pb][:, hk, :],
                               start=True, stop=True)
            return ins
        P.add("pe", mmkk, reads=[kTc_b[pb]], writes=[c.psb[0], c.psb[1]])

        def mmqk(e, pb=pb):
            ins = None
            for hk in range(8):
                ins = e.matmul(c.ps[2 + hk // 4][:, (hk % 4) * 128:(hk % 4 + 1) * 128], kTc[pb][:, hk, :], qTc[pb][:, hk, :],
                               start=True, stop=True)
            return ins
        P.add("pe", mmqk, reads=[kTc_b[pb], qTc_b[pb]], writes=[c.psb[2], c.psb[3]])
        for hf in range(2):
            for par in range(2):
                P.add("dve", lambda e, par=par, hf=hf: e.tensor_tensor(
                    attnT[:, 8 * hf + par:8 * hf + 8:2, :], bank4(2 + hf), F2[:, 8 * hf + par:8 * hf + 8:2, :], ALU.mult),
                    reads=[c.psb[2 + hf], F2b[2 * hf], F2b[2 * hf + 1]], writes=[attnT_b[2 * hf], attnT_b[2 * hf + 1]])
        for q in range(4):
            P.add("dve", lambda e, q=q: e.tensor_tensor(
                F2[:, hs4(q), :], SM.rearrange("p (o i) -> p o i", o=1).to_broadcast([128, 4, 128]),
                bank4(4 + q), ALU.subtract),
                reads=[gcrow_b[q], c.cstb], writes=[F2b[q]])
        for h in range(H):
            P.add("act", lambda e, h=h: e.activation(F2[:, h, :], F2[:, h, :], AF.Exp, bias=sm[:, 0, h:h + 1], scale=1.0),
                  reads=[F2b[h // 4], sm_b], writes=[F2b[h // 4]])
        for h in range(H):
            hk = h // 2
            P.add("dve", lambda e, h=h, hk=hk: e.scalar_tensor_tensor(
                F1[:, h, :], c.ps[hk // 4][:, (hk % 4) * 128:(hk % 4 + 1) * 128], sm[:, 6, h:h + 1], F2[:, h, :],
                ALU.mult, ALU.mult), reads=[c.psb[hk // 4], sm_b, F2b[h // 4]], writes=[F1b[h // 4]])
        R, Rb = Rp[0]
        for hf in range(4):
            bk = 2 + (hf % 2)

            def trm(e, hf=hf, bk=bk):
                ins = None
                for hq in range(4):
                    h = hf * 4 + hq
                    ins = e.transpose(c.ps[bk][:, hq * 128:(hq + 1) * 128].bitcast(F32R), F1[:, h, :], identr)
                return ins
            P.add("pe", trm, reads=[F1b[hf], identr_b], writes=[c.psb[bk]])
            P.add("act", lambda e, hf=hf, R=R, bk=bk: e.activation(R[:, hs4(hf), :], bank4(bk), AF.Copy),
                  reads=[c.psb[bk]], writes=[Rb[hf]])
            P.add("dve", lambda e, hf=hf, bk=bk: e.tensor_tensor(
                Y[:, hs4(hf), :], bank4(bk),
                c.ident.rearrange("p (o i) -> p o i", o=1).to_broadcast([128, 4, 128]), ALU.add),
                reads=[c.psb[bk], c.cstb], writes=[Yb[hf]])
        RT, RTb = F1, F1b
        NLEV = 6
        for lev in range(1, NLEV + 1):
            Rn, Rnb = Rp[lev % 2]
            RTn, RTnb = RTp[lev % 2]
            last = (lev == NLEV)
            for hf in range(4):
                b_rt, b_r, b_y = (hf % 2), 2 + (hf % 2), 4 + (hf % 2)

                def mmrt(e, hf=hf, R=R, RT=RT, bk=b_rt):
                    ins = None
                    for hq in range(4):
                        h = hf * 4 + hq
                        ins = e.matmul(c.ps[bk][:, hq * 128:(hq + 1) * 128], R[:, h, :], RT[:, h, :], start=True, stop=True)
                    return ins
                P.add("pe", mmrt, reads=[Rb[hf], RTb[hf]], writes=[c.psb[b_rt]])
                P.add("act", lambda e, hf=hf, RTn=RTn, bk=b_rt: e.activation(RTn[:, hs4(hf), :], bank4(bk), AF.Copy),
                      reads=[c.psb[b_rt]], writes=[RTnb[hf]])
                if not last:
                    def mmr(e, hf=hf, R=R, RT=RT, bk=b_r):
                        ins = None
                        for hq in range(4):
                            h = hf * 4 + hq
                            ins = e.matmul(c.ps[bk][:, hq * 128:(hq + 1) * 128], RT[:, h, :], R[:, h, :], start=True, stop=True)
                        return ins
                    P.add("pe", mmr, reads=[Rb[hf], RTb[hf]], writes=[c.psb[b_r]])
                    P.add("dve", lambda e, hf=hf, Rn=Rn, bk=b_r: e.tensor_copy(Rn[:, hs4(hf), :], bank4(bk)),
                          reads=[c.psb[b_r]], writes=[Rnb[hf]])
                if hf >= 1:
                    hy = hf - 1
                    b_yy = 4 + (hy % 2)

                    def mmy(e, hf=hy, RTn=RTn, bk=b_yy):
                        ins = None
                        for hq in range(4):
                            h = hf * 4 + hq
                            ins = e.matmul(c.ps[bk][:, hq * 128:(hq + 1) * 128], RTn[:, h, :], Y[:, h, :], start=True, stop=True)
                        return ins
                    P.add("pe", mmy, reads=[RTnb[hy], Yb[hy]], writes=[c.psb[b_yy]])
                    P.add("dve", lambda e, hf=hy, bk=b_yy: e.tensor_tensor(Y[:, hs4(hf), :], bank4(bk), Y[:, hs4(hf), :].bitcast(F32), ALU.add),
                          reads=[c.psb[b_yy], Yb[hy]], writes=[Yb[hy]])
            hy = 3
            b_yy = 4 + (hy % 2)

            def mmy3(e, hf=hy, RTn=RTn, bk=b_yy):
                ins = None
                for hq in range(4):
                    h = hf * 4 + hq
                    ins = e.matmul(c.ps[bk][:, hq * 128:(hq + 1) * 128], RTn[:, h, :], Y[:, h, :], start=True, stop=True)
                return ins
            P.add("pe", mmy3, reads=[RTnb[hy], Yb[hy]], writes=[c.psb[b_yy]])
            P.add("dve", lambda e, hf=hy, bk=b_yy: e.tensor_tensor(Y[:, hs4(hf), :], bank4(bk), Y[:, hs4(hf), :].bitcast(F32), ALU.add),
                  reads=[c.psb[b_yy], Yb[hy]], writes=[Yb[hy]])
            R, Rb = Rn, Rnb
            RT, RTb = RTn, RTnb
        P.add("pool", lambda e, pb=pb, beta=beta: e.tensor_tensor(
            vbt[:], vtk[pb][:], beta.rearrange("p (h o) -> p h o", o=1).to_broadcast([128, H, 128]), ALU.mult),
            reads=[vtk_b[pb], be_b[pb]], writes=vbt_b)
        for par in range(2):
            P.add("pool", lambda e, par=par, pb=pb: e.tensor_tensor(
                kw[:, par::2, :], ktk[pb][:],
                sm[:, 5, par::2].rearrange("p (h o) -> p h o", o=1).to_broadcast([128, 8, 128]), ALU.mult),
                reads=[ktk_b[pb], sm_b], writes=kw_b)
            P.add("pool", lambda e, par=par, pb=pb: e.tensor_tensor(
                kdec[:, par::2, :], ktk[pb][:],
                sm[:, 4, par::2].rearrange("p (h o) -> p h o", o=1).to_broadcast([128, 8, 128]), ALU.mult),
                reads=[ktk_b[pb], sm_b], writes=kdec_b)
        for hf in range(4):
            P.add("act", lambda e, hf=hf: e.activation(TTb[:, hs4(hf), :], Y[:, hs4(hf), :].bitcast(F32), AF.Copy),
                  reads=[Yb[hf]], writes=[TTb_b[hf]])
        for hf in range(4):
            bu, bw_ = (hf % 2), 2 + (hf % 2)

            def mmu(e, hf=hf, bk=bu):
                ins = None
                for hq in range(4):
                    h = hf * 4 + hq
                    ins = e.matmul(c.ps[bk][:, hq * 128:(hq + 1) * 128], TTb[:, h, :], vbt[:, h, :], start=True, stop=True)
                return ins
            P.add("pe", mmu, reads=[TTb_b[hf]] + list(vbt_b), writes=[c.psb[bu]])
            P.add("act", lambda e, hf=hf, bk=bu: e.activation(F3[:, hs4(hf), :], bank4(bk), AF.Copy),
                  reads=[c.psb[bu]], writes=[F3b[hf]])

            def mmw(e, hf=hf, bk=bw_):
                ins = None
                for hq in range(4):
                    h = hf * 4 + hq
                    ins = e.matmul(c.ps[bk][:, hq * 128:(hq + 1) * 128], kw[:, h, :], TTb[:, h, :], start=True, stop=True)
                return ins
            P.add("pe", mmw, reads=[TTb_b[hf]] + list(kw_b), writes=[c.psb[bw_]])
            P.add("dve", lambda e, hf=hf, bk=bw_: e.tensor_copy(wT[:, hs4(hf), :], bank4(bk)),
                  reads=[c.psb[bw_]], writes=[wT_b[hf]])
        for hf in range(4):
            bk = 4 + (hf % 2)

            def mmws(e, hf=hf, bk=bk):
                ins = None
                for hq in range(4):
                    h = hf * 4 + hq
                    ins = e.matmul(c.ps[bk][:, hq * 128:(hq + 1) * 128], wT[:, h, :], Sbf[:, h, :], start=True, stop=True)
                return ins
            P.add("pe", mmws, reads=[wT_b[hf]] + list(Sbf_b), writes=[c.psb[bk]])
            P.add("dve", lambda e, hf=hf, bk=bk: e.tensor_tensor(vnew[:, hs4(hf), :], F3[:, hs4(hf), :], bank4(bk), ALU.subtract),
                  reads=[c.psb[bk], F3b[hf]], writes=[vnew_b[hf]])
        for hf in range(4):
            bk = 6 + (hf % 2)

            def mmo(e, hf=hf, bk=bk):
                ins = None
                for hq in range(4):
                    h = hf * 4 + hq
                    e.matmul(c.ps[bk][:, hq * 128:(hq + 1) * 128], qdT[:, h, :], Sbf[:, h, :], start=True, stop=False)
                    ins = e.matmul(c.ps[bk][:, hq * 128:(hq + 1) * 128], attnT[:, h, :], vnew[:, h, :], start=False, stop=True)
                return ins
            P.add("pe", mmo, reads=list(qdT_b) + list(Sbf_b) + [attnT_b[hf], vnew_b[hf]], writes=[c.psb[bk]])
            P.add("act", lambda e, hf=hf, bk=bk: e.activation(osb[:, hs4(hf), :], bank4(bk), AF.Copy),
                  reads=[c.psb[bk]], writes=[osb_b[hf]])
        P.dma(o_d[c0:c0 + 128, :], flat(osb), reads=list(osb_b), sembuf=osb_b[0])
        for hf in range(4):
            bk = (hf % 2)

            def mmds(e, hf=hf, bk=bk):
                ins = None
                for hq in range(4):
                    h = hf * 4 + hq
                    ins = e.matmul(c.ps[bk][:, hq * 128:(hq + 1) * 128], kdec[:, h, :], vnew[:, h, :], start=True, stop=True)
                return ins
            P.add("pe", mmds, reads=list(kdec_b) + [vnew_b[hf]], writes=[c.psb[bk]])
            for hq in range(4):
                h = hf * 4 + hq
                P.add("dve", lambda e, h=h, bk=bk, hq=hq: e.scalar_tensor_tensor(
                    S[:, h, :], S[:, h, :], sm[:, 3, h:h + 1], c.ps[bk][:, hq * 128:(hq + 1) * 128], ALU.mult, ALU.add),
                    reads=[c.psb[bk], sm_b, S_b[hf]], writes=[S_b[hf]])
            P.add("act", lambda e, hf=hf: e.activation(Sbf[:, hs4(hf), :], S[:, hs4(hf), :], AF.Copy),
                  reads=[S_b[hf]], writes=[Sbf_b[hf]])
    P.barrier()
    A.release(m0)


def stage_gdn_out(c, o_d, z_d, norm_w, w_out, src, dst, ln_g, ln_b, li, si, T):
    P, A = c.P, c.A
    m0 = A.mark()
    H = 16
    wo = A.alloc([128, H, D], BF16, "gwo")
    wo_b = P.bufs_n("gwo", H)
    w_v = w_out.rearrange("(j p) d -> p j d", p=128)
    for j in range(H):
        P.dma(wo[:, j, :], w_v[:, j, :], writes=[wo_b[j]], sembuf=wo_b[j], eng="pool")
    gt, bt, gb_b = load_ln_params(c, ln_g, ln_b, li, si)
    nw = A.alloc([128, 128], F32, "nw")
    nw_b = P.buf("nw")
    P.dma(nw[:], norm_w.partition_broadcast(128), writes=[nw_b], sembuf=nw_b)
    epsr = A.alloc([128, 1], F32, "epsr")
    P.add("dve", lambda e: e.memset(epsr[:], RMS_EPS), writes=[nw_b], reads=[nw_b])
    ot = [A.alloc([128, H, 128], F32, "ot") for _ in range(2)]
    zt = [A.alloc([128, H, 128], F32, "zt2") for _ in range(2)]
    ot_b, zt_b = P.bufs_n("ot", 2), P.bufs_n("zt2", 2)
    sqt = A.alloc([128, H, 128], F32, "sqt")
    sqt_b = P.buf("sqt")
    ssm = [A.alloc([128, 2, H], F32, "gsm2") for _ in range(2)]
    ssm_b = P.bufs_n("gsm2", 2)
    onb = [A.alloc([128, H, 128], BF16, "onb") for _ in range(3)]
    onb_b = P.bufs_n("onb", 3)
    onT = [A.alloc([128, H, 128], BF16, "onT") for _ in range(3)]
    onT_b = P.bufs_n("onT", 3)
    xs = [A.alloc([128, D], F32, "xs") for _ in range(3)]
    xs_b = P.bufs_n("xs", 3)
    lns = alloc_ln_small(c, 2)
    def g0(s):
        r0 = s * 128
        p2 = s % 2
        o, o_b, z, z_b = ot[p2], ot_b[p2], zt[p2], zt_b[p2]
        P.dma(o[:].rearrange("p h d -> p (h d)"), o_d[r0:r0 + 128, :], writes=[o_b], sembuf=o_b)
        P.dma(z[:].rearrange("p h d -> p (h d)"), z_d[r0:r0 + 128, :], writes=[z_b], sembuf=z_b)
        sm_, sm_b = ssm[p2], ssm_b[p2]
        P.add("act", lambda e: e.activation(sqt[:], o[:], AF.Square), reads=[o_b], writes=[sqt_b])
        P.add("dve", lambda e: e.tensor_reduce(sm_[:, 0, :], sqt[:], AX.X, ALU.add), reads=[sqt_b], writes=[sm_b])
        P.add("act", lambda e: e.activation(sm_[:, 1, :], sm_[:, 0, :], AF.Sqrt, bias=epsr[:], scale=1.0 / 128),
              reads=[sm_b, nw_b], writes=[sm_b])
        P.add("dve", lambda e: e.reciprocal(sm_[:, 1, :], sm_[:, 1, :]), reads=[sm_b], writes=[sm_b])
        P.add("dve", lambda e: e.tensor_tensor(
            z[:], z[:], nw[:].rearrange("p (o d) -> p o d", o=1).to_broadcast([128, H, 128]), ALU.mult),
            reads=[z_b, nw_b], writes=[z_b])
        on, on_b = onb[s % 3], onb_b[s % 3]
        for h in range(H):
            P.add("dve", lambda e, h=h: e.scalar_tensor_tensor(
                on[:, h, :], o[:, h, :], sm_[:, 1, h:h + 1], z[:, h, :], ALU.mult, ALU.mult),
                reads=[o_b, z_b, sm_b], writes=[on_b])

    def g1(s):
        on, on_b = onb[s % 3], onb_b[s % 3]
        oT, oT_b = onT[s % 3], onT_b[s % 3]
        for hf in range(2):
            pbv = psbf(c, hf)

            def tr(e, hf=hf, pbv=pbv):
                ins = None
                for hq in range(8):
                    ins = e.transpose(pbv[:, hq * 128:(hq + 1) * 128], on[:, hf * 8 + hq, :], c.identb)
                return ins
            P.add("pe", tr, reads=[on_b, c.cstbf_b], writes=[c.psb[hf]])
            if hf == 0:
                P.add("act", lambda e, pbv=pbv: e.activation(
                    oT[:, 0:8, :], pbv.rearrange("p (h d) -> p h d", h=8), AF.Copy), reads=[c.psb[0]], writes=[oT_b])
            else:
                P.add("dve", lambda e, pbv=pbv: e.tensor_copy(
                    oT[:, 8:16, :], pbv.rearrange("p (h d) -> p h d", h=8)), reads=[c.psb[1]], writes=[oT_b])

    def g2(s):
        r0 = s * 128
        oT, oT_b = onT[s % 3], onT_b[s % 3]
        banks = (4, 5) if s % 2 == 0 else (6, 7)
        for hh, bk in enumerate(banks):
            def mmo(e, hh=hh, bk=bk):
                ins = None
                for j in range(H):
                    ins = e.matmul(c.ps[bk][:, :], oT[:, j, :], wo[:, j, hh * 512:(hh + 1) * 512],
                                   start=(j == 0), stop=(j == H - 1))
                return ins
            P.add("pe", mmo, reads=[oT_b] + list(wo_b), writes=[c.psb[bk]])
        x, xb = xs[s % 3], xs_b[s % 3]
        resid_ln(c, src, dst, r0, banks, 1.0 / ALPHA, x, xb, gt, bt, gb_b, lns[s % 2])
    pipeline(T // 128, [g0, g1, g2])
    P.barrier()
    A.release(m0)


def build_program(T=SEQ):
    nc = bass.Bass("TRN2", target_bir_lowering=False)
    din = lambda n, s, d=F32: nc.dram_tensor(n, s, d, kind="ExternalInput").ap()
    dsc = lambda n, s, d=F32: nc.dram_tensor(n, s, d, kind="Internal").ap()
    x = din("x", [T, D])
    ln_g = din("ln_g", [2, 3, D])
    ln_b = din("ln_b", [2, 3, D])
    fpre_in = din("ffn_pre_w_in", [2, D, 2 * DFF])
    fpre_out = din("ffn_pre_w_out", [2, DFF, D])
    fpost_in = din("ffn_post_w_in", [2, D, 2 * DFF])
    fpost_out = din("ffn_post_w_out", [2, DFF, D])
    m_in = din("moba_w_in", [1, D, 3 * D])
    m_out = din("moba_w_out", [1, D, D])
    g_in = din("gdn_w_in", [1, D, GPROJ])
    g_conv = din("gdn_conv_w", [1, 4, 4096])
    g_alog = din("gdn_a_log", [1, 16])
    g_dtb = din("gdn_dt_bias", [1, 16])
    g_nw = din("gdn_norm_w", [1, 128])
    g_out = din("gdn_w_out", [1, 2048, D])
    consts = din("consts", [128, NCONST])
    rope = din("rope", [2, 128, T])
    y = nc.dram_tensor("y", [T, D], F32, kind="ExternalOutput").ap()
    hA = dsc("hA", [T, D])
    hB = dsc("hB", [T, D])
    qT_d = dsc("qT_d", [8, 128, T], BF16)
    kT_d = dsc("kT_d", [8, 128, T], BF16)
    v_d = dsc("v_d", [T, D], BF16)
    bT_d = dsc("bT_d", [8, 16, T], BF16)
    oT_d = dsc("oT_d", [8, 128, T], BF16)
    ktok_d = dsc("ktok_d", [T, 1024], BF16)
    vtok_d = dsc("vtok_d", [T, 2048], BF16)
    z_d = dsc("z_d", [T, 2048])
    gg_d = dsc("gg_d", [T, 16])
    beta_d = dsc("beta_d", [T, 16])
    o_d = dsc("o_d", [T, 2048])
    c = make_ctx(nc)
    load_small_consts(c)
    load_consts_full(c, consts)
    c.P.barrier()
    stage_ffn(c, x, hA, fpre_in[0], fpre_out[0], ln_g, ln_b, 0, 0, T)
    stage_moba_proj(c, hA, m_in[0], rope, qT_d, kT_d, v_d, bT_d, T)
    stage_moba_attn(c, qT_d, kT_d, v_d, bT_d, oT_d, T)
    stage_outproj_ln(c, oT_d, 8, m_out[0], hA, hB, ln_g, ln_b, 0, 1, 1.0 / ALPHA, T)
    stage_ffn(c, hB, hA, fpost_in[0], fpost_out[0], ln_g, ln_b, 0, 2, T)
    stage_ffn(c, hA, hB, fpre_in[1], fpre_out[1], ln_g, ln_b, 1, 0, T)
    stage_gdn_proj(c, hB, g_in[0], g_conv[0], g_alog, g_dtb, qT_d, kT_d, ktok_d, vtok_d, z_d, gg_d, beta_d, T)
    stage_gdn_scan(c, qT_d, kT_d, ktok_d, vtok_d, gg_d, beta_d, o_d, T)
    stage_gdn_out(c, o_d, z_d, g_nw, g_out[0], hB, hA, ln_g, ln_b, 1, 1, T)
    stage_ffn(c, hA, y, fpost_in[1], fpost_out[1], ln_g, ln_b, 1, 2, T)
    c.P.emit()
    return nc


_CACHE = {}


def kernel(x, ln_g, ln_b, ffn_pre_w_in, ffn_pre_w_out, ffn_post_w_in, ffn_post_w_out,
           moba_w_in, moba_w_out, gdn_w_in, gdn_conv_w, gdn_a_log, gdn_dt_bias, gdn_norm_w, gdn_w_out):
    B, T, _ = x.shape
    if "nc" not in _CACHE:
        _CACHE["nc"] = build_program(T)
    nc = _CACHE["nc"]
    f = lambda a: np.ascontiguousarray(np.asarray(a, dtype=np.float32))
    shared = dict(ln_g=f(ln_g), ln_b=f(ln_b), ffn_pre_w_in=f(ffn_pre_w_in), ffn_pre_w_out=f(ffn_pre_w_out),
                  ffn_post_w_in=f(ffn_post_w_in), ffn_post_w_out=f(ffn_post_w_out), moba_w_in=f(moba_w_in),
                  moba_w_out=f(moba_w_out), gdn_w_in=f(gdn_w_in), gdn_conv_w=f(gdn_conv_w), gdn_a_log=f(gdn_a_log),
                  gdn_dt_bias=f(gdn_dt_bias), gdn_norm_w=f(gdn_norm_w), gdn_w_out=f(gdn_w_out),
                  consts=make_consts(), rope=make_rope(T))
    xs = f(x)
    in_maps = [dict(shared, x=xs[b]) for b in range(B)]
    res = run_bass_kernel_spmd(nc, in_maps, core_ids=list(range(B)))
    return np.stack([np.asarray(r["y"], dtype=np.float32) for r in res.results], axis=0)
```

```python
import math
import numpy as np
import concourse.bass as bass
import concourse.mybir as mybir
from concourse.bass_utils import run_bass_kernel_spmd

F32 = mybir.dt.float32
BF16 = mybir.dt.bfloat16
F32R = mybir.dt.float32r
AF = mybir.ActivationFunctionType
ALU = mybir.AluOpType
AX = mybir.AxisListType

D = 1024
DFF = 2816
SEQ = 4096
NB = 8
ALPHA = (2 * 2) ** 0.25
LN_EPS = 1e-5
RMS_EPS = 1e-6


class Buf:
    __slots__ = ("name", "last_w", "readers", "dsem", "dcount", "excl")

    def __init__(self, name):
        self.name = name
        self.excl = False
        self.last_w = None
        self.readers = []
        self.dsem = None
        self.dcount = 0


class Op:
    __slots__ = ("eng", "fn", "deps", "sig", "need_sig", "is_dma", "dbuf", "dval", "idx", "dslot")

    def __init__(self, eng, fn, is_dma=False):
        self.eng = eng
        self.fn = fn
        self.deps = []
        self.sig = None
        self.need_sig = False
        self.is_dma = is_dma
        self.dbuf = None
        self.dval = 0
        self.idx = 0
        self.dslot = None


ENGS = ("pe", "act", "dve", "pool", "sp")


class Prog:
    def __init__(self, nc):
        self.nc = nc
        self.ops = {e: [] for e in ENGS}
        self.bufs = []
        self.dma_bufs = []
        self.nops = 0
        self.slots = []
        self.free_slots = []
        self.free_slots_sw = []

    def buf(self, name):
        b = Buf(name)
        self.bufs.append(b)
        return b

    def bufs_n(self, name, n):
        return [self.buf("%s%d" % (name, i)) for i in range(n)]

    def _track(self, op, reads, writes):
        seen = set()
        for b in reads:
            w = b.last_w
            if w is not None and id(w) not in seen:
                seen.add(id(w))
                op.deps.append((w, True))
            if b.excl:
                for r in b.readers:
                    if id(r) not in seen:
                        seen.add(id(r))
                        op.deps.append((r, False))
                b.readers = []
        for b in writes:
            w = b.last_w
            if w is not None and id(w) not in seen:
                seen.add(id(w))
                op.deps.append((w, False))
            for r in b.readers:
                if id(r) not in seen:
                    seen.add(id(r))
                    op.deps.append((r, False))
        for b in reads:
            b.readers.append(op)
        for b in writes:
            b.last_w = op
            b.readers = []

    def add(self, eng, fn, reads=(), writes=()):
        op = Op(eng, fn)
        self._track(op, reads, writes)
        op.idx = self.nops
        self.nops += 1
        self.ops[eng].append(op)
        return op

    def dma(self, out_ap, in_ap, reads=(), writes=(), sembuf=None, eng="sp"):
        op = Op(eng, None, is_dma=True)
        op.fn = (out_ap, in_ap)
        self._track(op, reads, writes)
        kind = 1 if eng == "pool" else 0
        if sembuf.dsem is None:
            fl = self.free_slots_sw if kind else self.free_slots
            if fl:
                sembuf.dsem = fl.pop()
            else:
                sembuf.dsem = [0, None, kind]
                self.slots.append(sembuf.dsem)
            self.dma_bufs.append(sembuf)
        assert sembuf.dsem[2] == kind, "mixing SW/HW DGE on one semaphore"
        sembuf.dsem[0] += 16
        op.dbuf = sembuf
        op.dslot = sembuf.dsem
        op.dval = sembuf.dsem[0]
        op.idx = self.nops
        self.nops += 1
        self.ops[eng].append(op)
        return op

    def barrier(self):
        lasts = []
        for e in ENGS:
            for o in reversed(self.ops[e]):
                if not o.is_dma and o.fn is not None:
                    lasts.append(o)
                    break
        dmas = [(b.dsem, b.dsem[0]) for b in self.dma_bufs]
        for b in self.dma_bufs:
            (self.free_slots_sw if b.dsem[2] else self.free_slots).append(b.dsem)
            b.dsem = None
        self.dma_bufs = []
        for e in ENGS:
            op = Op(e, None)
            op.deps = [(o, True) for o in lasts]
            op.dval = dmas
            op.idx = self.nops
            self.nops += 1
            self.ops[e].append(op)
        for b in self.bufs:
            b.last_w = None
            b.readers = []

    @staticmethod
    def _needs_wait(op, d, raw):
        if d.eng != op.eng or op.fn is None or op.is_dma:
            return True
        if op.eng == "pe":
            return False
        return True

    def emit(self):
        nc = self.nc
        for e in ENGS:
            for k, op in enumerate(self.ops[e]):
                op.idx = k
        for e in ENGS:
            for op in self.ops[e]:
                for d, raw in op.deps:
                    if d.is_dma:
                        continue
                    if self._needs_wait(op, d, raw):
                        d.need_sig = True
        for e in ENGS:
            c = 0
            for op in self.ops[e]:
                if op.need_sig:
                    c += 1
                    op.sig = c
        sems = {e: nc.alloc_semaphore("s_" + e) for e in ENGS}
        for i, sl in enumerate(self.slots):
            sl[1] = nc.alloc_semaphore("dsem%d" % i)
        engobj = {"pe": nc.tensor, "act": nc.scalar, "dve": nc.vector, "pool": nc.gpsimd, "sp": nc.sync}
        self.nwaits = 0
        with nc.Block() as block:
            def run(e, eng):
                seen = {}
                for op in self.ops[e]:
                    waits = {}
                    for d, raw in op.deps:
                        if d.is_dma:
                            key = ("d", id(d.dslot))
                            if waits.get(key, (None, 0))[1] < d.dval:
                                waits[key] = (d.dslot[1], d.dval)
                        else:
                            if not self._needs_wait(op, d, raw):
                                continue
                            key = ("e", d.eng)
                            if waits.get(key, (None, 0))[1] < d.sig:
                                waits[key] = (sems[d.eng], d.sig)
                    if op.fn is None and not op.is_dma:
                        for sl, v in op.dval:
                            waits[("d", id(sl))] = (sl[1], v)
                    for key, (s, v) in waits.items():
                        if seen.get(key, 0) >= v:
                            continue
                        seen[key] = v
                        eng.wait_ge(s, v)
                        self.nwaits += 1
                    if op.is_dma:
                        o, i = op.fn
                        eng.dma_start(out=o, in_=i).then_inc(op.dslot[1], 16)
                    elif op.fn is not None:
                        ins = op.fn(eng)
                        if op.need_sig:
                            ins.then_inc(sems[e], 1)

            @block.tensor
            def _(eng):
                run("pe", eng)

            @block.scalar
            def _(eng):
                run("act", eng)

            @block.vector
            def _(eng):
                run("dve", eng)

            @block.gpsimd
            def _(eng):
                run("pool", eng)

            @block.sync
            def _(eng):
                run("sp", eng)


class Arena:
    def __init__(self, nc, base=16512, limit=229344):
        self.nc = nc
        self.top = base
        self.limit = limit
        self.n = 0

    def mark(self):
        return self.top

    def release(self, m):
        self.top = m

    def alloc(self, shape, dtype, name=None):
        nbytes = int(np.prod(shape[1:])) * (2 if dtype == BF16 else 4)
        off = (self.top + 63) // 64 * 64
        assert off + nbytes <= self.limit, ("SBUF arena overflow", name, off, nbytes)
        self.top = off + nbytes
        self.n += 1
        t = self.nc.alloc_sbuf_tensor_at("%s_%d" % (name or "t", self.n), list(shape), dtype, offset=off)
        return t


class Ctx:
    pass


def make_ctx(nc):
    c = Ctx()
    c.nc = nc
    c.P = Prog(nc)
    c.A = Arena(nc)
    c.ps = [nc.alloc_psum_tensor("psb%d" % i, [128, 512], F32) for i in range(8)]
    c.psb = c.P.bufs_n("psb", 8)
    for b in c.psb:
        b.excl = True
    return c


def load_consts(c, consts_ap):
    P, A = c.P, c.A
    c.ident = A.alloc([128, 128], F32, "ident")
    c.ident_b = P.buf("ident")
    P.dma(c.ident[:], consts_ap[:, 0:128], writes=[c.ident_b], sembuf=c.ident_b)
    c.identb = A.alloc([128, 128], BF16, "identb")
    c.identb_b = P.buf("identb")
    P.add("dve", lambda e: e.tensor_copy(c.identb[:], c.ident[:]), reads=[c.ident_b], writes=[c.identb_b])


def bcast_rows(ap2d, nrows_part=128):
    return ap2d.partition_broadcast(nrows_part)


def ln_tile(c, xr, xr_b, gt, bt, gb_b, dst_ap, st):
    P = c.P
    eps = LN_EPS / (ALPHA * ALPHA)
    stats, mv, rstd, nmr = st["stats"], st["mv"], st["rstd"], st["nmr"]
    sb = st["b"]
    P.add("dve", lambda e: e.bn_stats(stats[:, 0, :], xr[:, 0:512]), reads=[xr_b], writes=[sb])
    P.add("dve", lambda e: e.bn_stats(stats[:, 1, :], xr[:, 512:1024]), reads=[xr_b], writes=[sb])
    P.add("dve", lambda e: e.bn_aggr(mv[:], stats[:].rearrange("p a b -> p (a b)")), reads=[sb], writes=[sb])
    P.add("act", lambda e: e.activation(rstd[:], mv[:, 1:2], AF.Sqrt, bias=c.epsln[:], scale=1.0),
          reads=[sb, c.cst_b], writes=[sb])
    P.add("dve", lambda e: e.reciprocal(rstd[:], rstd[:]), reads=[sb], writes=[sb])
    P.add("dve", lambda e: e.tensor_scalar(nmr[:], mv[:, 0:1], -1.0, rstd[:], ALU.mult, ALU.mult),
          reads=[sb], writes=[sb])
    P.add("act", lambda e: e.activation(xr[:], xr[:], AF.Identity, bias=nmr[:], scale=rstd[:]),
          reads=[sb, xr_b], writes=[xr_b])
    P.add("dve", lambda e: e.tensor_tensor(xr[:], xr[:], gt[:], ALU.mult), reads=[xr_b, gb_b], writes=[xr_b])
    P.add("dve", lambda e: e.tensor_tensor(xr[:], xr[:], bt[:], ALU.add), reads=[xr_b, gb_b], writes=[xr_b])
    P.dma(dst_ap, xr[:], reads=[xr_b], sembuf=xr_b)


def alloc_ln_small(c, n=2):
    out = []
    for i in range(n):
        st = {
            "stats": c.A.alloc([128, 2, 6], F32, "stats"),
            "mv": c.A.alloc([128, 2], F32, "mv"),
            "rstd": c.A.alloc([128, 1], F32, "rstd"),
            "nmr": c.A.alloc([128, 1], F32, "nmr"),
            "b": c.P.buf("lnsmall%d" % i),
        }
        out.append(st)
    return out


def load_ln_params(c, ln_g_ap, ln_b_ap, li, si):
    P, A = c.P, c.A
    gt = A.alloc([128, 1024], F32, "lng")
    bt = A.alloc([128, 1024], F32, "lnb")
    gb_b = P.buf("lngb")
    b2 = P.buf("lngb2")
    P.dma(gt[:], ln_g_ap[li, si:si + 1, :].partition_broadcast(128), writes=[gb_b], sembuf=gb_b)
    P.dma(bt[:], ln_b_ap[li, si:si + 1, :].partition_broadcast(128), writes=[b2], sembuf=b2)
    P.add("dve", lambda e: e.tensor_copy(bt[:, 0:1], bt[:, 0:1]), reads=[b2, gb_b], writes=[gb_b])
    return gt, bt, gb_b


def transpose_in(c, xs, xs_b, hT, hT_b, col0, banks):
    P = c.P
    for half in range(2):
        bk = banks[half]
        ps, psb = c.ps[bk], c.psb[bk]

        def f(e, half=half, ps=ps):
            ins = None
            for q in range(4):
                kc = half * 4 + q
                ins = e.transpose(ps[:, q * 128:(q + 1) * 128], xs[:, kc * 128:(kc + 1) * 128], c.ident[:])
            return ins
        P.add("pe", f, reads=[xs_b, c.ident_b], writes=[psb])
        eng = "act" if half == 0 else "dve"
        if eng == "act":
            P.add("act", lambda e, half=half, ps=ps: e.activation(
                hT[:, half * 4:half * 4 + 4, col0:col0 + 128],
                ps[:].rearrange("p (a b) -> p a b", a=4), AF.Copy),
                reads=[psb], writes=[hT_b])
        else:
            P.add("dve", lambda e, half=half, ps=ps: e.tensor_copy(
                hT[:, half * 4:half * 4 + 4, col0:col0 + 128],
                ps[:].rearrange("p (a b) -> p a b", a=4)),
                reads=[psb], writes=[hT_b])


def stage_ffn(c, src, dst, w_in, w_out, ln_g, ln_b, li, si, T):
    P, A, nc = c.P, c.A, c.nc
    m0 = A.mark()
    TT = 512
    NJ = DFF // 128
    win = A.alloc([128, 8, 2 * DFF], BF16, "win")
    wout = A.alloc([128, NJ, D], BF16, "wout")
    NWG = 11
    win_b = P.bufs_n("win", NWG)
    wout_b = P.bufs_n("wout", NJ)
    w_in_v = w_in.rearrange("(kc p) n -> p kc n", p=128)
    w_out_v = w_out.rearrange("(j p) d -> p j d", p=128)
    CW = 2 * DFF // NWG
    order = []
    for g in range(NWG // 2 + 1):
        for gg in (g, g + (NWG + 1) // 2):
            if gg < NWG and gg not in order:
                order.append(gg)
    for g in order:
        P.dma(win[:, :, g * CW:(g + 1) * CW], w_in_v[:, :, g * CW:(g + 1) * CW],
              writes=[win_b[g]], sembuf=win_b[g], eng="pool")
    for j in range(NJ):
        P.dma(wout[:, j, :], w_out_v[:, j, :], writes=[wout_b[j]], sembuf=wout_b[j], eng="pool")
    gt, bt, gb_b = load_ln_params(c, ln_g, ln_b, li, si)
    hT = A.alloc([128, 8, TT], BF16, "hT")
    hT_b = P.buf("hT")
    aT = A.alloc([128, NJ, TT], BF16, "aT")
    aT_b = P.bufs_n("aT", NJ)
    NX = 3
    xs = [A.alloc([128, D], F32, "xs") for _ in range(NX)]
    xs_b = P.bufs_n("xs", NX)
    sg = [A.alloc([128, TT], F32, "sg") for _ in range(2)]
    sg_b = P.bufs_n("sg", 2)
    lns = alloc_ln_small(c, 2)
    xi = 0
    cres = 0.5 / ALPHA
    for t in range(T // TT):
        r0 = t * TT
        for s in range(TT // 128):
            x, xb = xs[xi % NX], xs_b[xi % NX]
            xi += 1
            P.dma(x[:], src[r0 + s * 128:r0 + (s + 1) * 128, :], writes=[xb], sembuf=xb)
            transpose_in(c, x, xb, hT, hT_b, s * 128, (0, 1))
        if getattr(c, "cut", 9) <= 1:
            continue
        for j in range(NJ):
            gcol = j * 128
            ucol = DFF + j * 128
            bg, bu = (0, 1) if j % 2 == 0 else (2, 3)

            def mmg(e, col=gcol, bk=bg):
                ins = None
                for kc in range(8):
                    ins = e.matmul(c.ps[bk][:, 0:TT], win[:, kc, col:col + 128], hT[:, kc, :],
                                   start=(kc == 0), stop=(kc == 7))
                return ins
            P.add("pe", mmg, reads=[hT_b, win_b[gcol // CW]], writes=[c.psb[bg]])
            P.add("pe", lambda e, col=ucol, bk=bu: mmg(e, col, bk), reads=[hT_b, win_b[ucol // CW]],
                  writes=[c.psb[bu]])
            s_, s_b = sg[j % 2], sg_b[j % 2]
            P.add("act", lambda e, bk=bg, s_=s_: e.activation(s_[:], c.ps[bk][:, 0:TT], AF.Silu),
                  reads=[c.psb[bg]], writes=[s_b])
            P.add("dve", lambda e, bk=bu, s_=s_, j=j: e.tensor_tensor(aT[:, j, :], c.ps[bk][:, 0:TT], s_[:], ALU.mult),
                  reads=[c.psb[bu], s_b], writes=[aT_b[j]])
        if getattr(c, "cut", 9) <= 2:
            continue
        for s in range(TT // 128):
            bk0, bk1 = (4, 5) if s % 2 == 0 else (6, 7)
            for hh, bk in enumerate((bk0, bk1)):
                def mmo(e, hh=hh, bk=bk, s=s):
                    ins = None
                    for j in range(NJ):
                        ins = e.matmul(c.ps[bk][:, :], aT[:, j, s * 128:(s + 1) * 128],
                                       wout[:, j, hh * 512:(hh + 1) * 512], start=(j == 0), stop=(j == NJ - 1))
                    return ins
                P.add("pe", mmo, reads=list(aT_b) + list(wout_b), writes=[c.psb[bk]])
            if getattr(c, "cut", 9) <= 3:
                continue
            x, xb = xs[xi % NX], xs_b[xi % NX]
            xi += 1
            P.dma(x[:], src[r0 + s * 128:r0 + (s + 1) * 128, :], writes=[xb], sembuf=xb)
            for hh, bk in enumerate((bk0, bk1)):
                P.add("dve", lambda e, hh=hh, bk=bk, x=x: e.scalar_tensor_tensor(
                    x[:, hh * 512:(hh + 1) * 512], c.ps[bk][:, :], cres, x[:, hh * 512:(hh + 1) * 512],
                    ALU.mult, ALU.add), reads=[c.psb[bk], xb], writes=[xb])
            if getattr(c, "cut", 9) <= 4:
                P.dma(dst[r0 + s * 128:r0 + (s + 1) * 128, :], x[:], reads=[xb], sembuf=xb)
                continue
            ln_tile(c, x, xb, gt, bt, gb_b, dst[r0 + s * 128:r0 + (s + 1) * 128, :], lns[s % 2])
    P.barrier()
    A.release(m0)


def load_small_consts(c):
    P, A = c.P, c.A
    c.epsln = A.alloc([128, 1], F32, "epsln")
    c.cst_b = P.buf("cst")
    P.add("dve", lambda e: e.memset(c.epsln[:], LN_EPS / (ALPHA * ALPHA)), writes=[c.cst_b])
    c.one_col = A.alloc([128, 1], F32, "onecol")
    P.add("dve", lambda e: e.memset(c.one_col[:], 1.0), writes=[c.cst_b])


C_IDENT = 0
C_ROT = 128
C_PASTM = 256
C_PASTS = 512
C_ONES = 768
C_UT = 896
C_SMASK = 1024
C_IMASKT = 1152
NF32 = 1280
CB_IDENT = 0
CB_CM = 128
CB_EN = 640
CB_ONES = 2688
NBF = 2816
NCONST = NF32 + NBF
NEG = -30000.0


def make_consts():
    c = np.zeros((128, NCONST), np.float32)
    c[:, C_IDENT:C_IDENT + 128] = np.eye(128)
    rot = np.zeros((128, 128), np.float32)
    for p in range(64):
        rot[p, p + 64] = 1.0
        rot[p + 64, p] = -1.0
    c[:, C_ROT:C_ROT + 128] = rot
    k = np.arange(128)[:, None]
    q = np.arange(256)[None, :]
    c[:, NF32 + CB_CM:NF32 + CB_CM + 256] = np.where(q >= k, 0.0, NEG)
    c[:, NF32 + CB_CM + 256:NF32 + CB_CM + 512] = np.where(q >= k + 128, 0.0, NEG)
    en = np.zeros((128, 16, 128), np.float32)
    for n in range(16):
        en[n, n, :] = 1.0
    c[:, NF32 + CB_EN:NF32 + CB_EN + 2048] = en.reshape(128, 2048)
    j = np.arange(16)[:, None]
    n = np.arange(16)[None, :]
    c[:, C_PASTM:C_PASTM + 256] = np.where(n < j, 0.0, -1e30).reshape(1, 256)
    c[:, C_PASTS:C_PASTS + 256] = np.where(n < j, 1.0, 0.0).reshape(1, 256)
    c[:, C_ONES:C_ONES + 128] = 1.0
    c[:, NF32 + CB_ONES:NF32 + CB_ONES + 128] = 1.0
    c[:, NF32 + CB_IDENT:NF32 + CB_IDENT + 128] = np.eye(128)
    a = np.arange(128)
    c[:, C_UT:C_UT + 128] = (a[:, None] <= a[None, :]).astype(np.float32)
    c[:, C_SMASK:C_SMASK + 128] = np.where(a[:, None] > a[None, :], 0.0, NEG)
    c[:, C_IMASKT:C_IMASKT + 128] = np.where(a[None, :] >= a[:, None], 0.0, NEG)
    return c


def make_rope(T):
    half = 64
    inv = (10000.0 ** (-np.arange(half, dtype=np.float32) / np.float32(half))).astype(np.float32)
    ang = (np.arange(T, dtype=np.float32)[None, :] * inv[:, None]).astype(np.float32)
    cs = np.cos(ang).astype(np.float32)
    sn = np.sin(ang).astype(np.float32)
    out = np.zeros((2, 128, T), np.float32)
    out[0, :64] = cs
    out[0, 64:] = cs
    out[1, :64] = sn
    out[1, 64:] = sn
    return out


def load_consts_full(c, consts_ap):
    P, A = c.P, c.A
    c.cst = A.alloc([128, NF32], F32, "cst")
    c.cstb = P.buf("cstf")
    P.dma(c.cst[:], consts_ap[:, 0:NF32], writes=[c.cstb], sembuf=c.cstb)
    c.ident = c.cst[:, C_IDENT:C_IDENT + 128]
    c.ident_b = c.cstb
    c.cstbf = A.alloc([128, NBF], BF16, "cstbf")
    c.cstbf_b = P.buf("cstbf")
    m = A.mark()
    tmp = A.alloc([128, NBF], F32, "csttmp")
    tb = P.buf("csttmp")
    P.dma(tmp[:], consts_ap[:, NF32:NCONST], writes=[tb], sembuf=tb)
    P.add("dve", lambda e: e.tensor_copy(c.cstbf[:], tmp[:]), reads=[tb], writes=[c.cstbf_b])
    A.release(m)
    c.identb = c.cstbf[:, CB_IDENT:CB_IDENT + 128]
    c.identb_b = c.cstbf_b


import os
SKEW = int(os.environ.get('SKEW', '2'))


def pipeline(n, steps, skew=1):
    for it in range(n + (len(steps) - 1) * skew):
        for k, f in enumerate(steps):
            i = it - k * skew
            if 0 <= i < n:
                f(i)


class _V:
    def __init__(self, ap):
        self.ap = ap

    def __getitem__(self, k):
        return self.ap[k]


def resid_ln(c, src, dst, r0, banks, cres, x, xb, gt, bt, gb_b, st):
    P = c.P
    P.dma(x[:], src[r0:r0 + 128, :], writes=[xb], sembuf=xb)
    for hh, bk in enumerate(banks):
        P.add("dve", lambda e, hh=hh, bk=bk: e.scalar_tensor_tensor(
            x[:, hh * 512:(hh + 1) * 512], c.ps[bk][:, :], cres, x[:, hh * 512:(hh + 1) * 512],
            ALU.mult, ALU.add), reads=[c.psb[bk], xb], writes=[xb])
    ln_tile(c, x, xb, gt, bt, gb_b, dst[r0:r0 + 128, :], st)


def stage_moba_proj(c, src, w_in, rope, qT_d, kT_d, v_d, biasT_d, T):
    P, A = c.P, c.A
    m0 = A.mark()
    TT = 512
    win = A.alloc([128, 8, 3 * D], BF16, "mwin")
    win_b = P.bufs_n("mwin", 6)
    w_in_v = w_in.rearrange("(kc p) n -> p kc n", p=128)
    for g in (2, 3, 0, 1, 4, 5):
        P.dma(win[:, :, g * 512:(g + 1) * 512], w_in_v[:, :, g * 512:(g + 1) * 512],
              writes=[win_b[g]], sembuf=win_b[g], eng="pool")
    hT = A.alloc([128, 8, TT], BF16, "hT")
    hT_b = P.buf("hT")
    NX = 3
    xs = [A.alloc([128, D], F32, "xs") for _ in range(NX)]
    xs_b = P.bufs_n("xs", NX)
    cs = [A.alloc([128, 2, TT], F32, "cs") for _ in range(2)]
    cs_b = P.bufs_n("cs", 2)
    NQF = 4
    qf = [A.alloc([128, TT], F32, "qf") for _ in range(NQF)]
    qf_b = P.bufs_n("qf", NQF)
    t1 = [A.alloc([128, TT], F32, "t1") for _ in range(NQF)]
    t1_b = P.bufs_n("t1", NQF)
    kr32 = [A.alloc([128, TT], F32, "kr32") for _ in range(4)]
    kr32_b = P.bufs_n("kr32", 4)
    q32 = A.alloc([128, 8, TT], F32, "q32")
    q32_b = P.bufs_n("q32", 8)
    ob = [A.alloc([128, TT], BF16, "ob") for _ in range(3)]
    ob_b = P.bufs_n("ob", 3)
    vb = [A.alloc([128, D], BF16, "vb") for _ in range(2)]
    vb_b = P.bufs_n("vb", 2)
    kmean = A.alloc([128, 8, 16], F32, "kmean")
    kmean_b = P.buf("kmean")
    P.add("dve", lambda e: e.memset(kmean[:], 0.0), writes=[kmean_b])
    gm = [A.alloc([128, 8, 16], F32, "gm") for _ in range(2)]
    gm_b = P.bufs_n("gm", 2)
    top8 = [A.alloc([128, 8, 8], F32, "top8") for _ in range(2)]
    selt = [A.alloc([128, 8, 16], F32, "selt") for _ in range(2)]
    bT = [A.alloc([16, 8, 128], BF16, "bT") for _ in range(2)]
    bT_b = P.bufs_n("bT", 2)
    xi = 0
    oi = 0
    for t in range(T // TT):
        r0 = t * TT
        for s in range(4):
            x, xb = xs[xi % NX], xs_b[xi % NX]
            xi += 1
            P.dma(x[:], src[r0 + s * 128:r0 + (s + 1) * 128, :], writes=[xb], sembuf=xb)
            transpose_in(c, x, xb, hT, hT_b, s * 128, (0, 1))
        ct, ctb = cs[t % 2], cs_b[t % 2]
        P.dma(ct[:], rope[:, :, r0:r0 + TT].rearrange("a p t -> p a t"), writes=[ctb], sembuf=ctb)
        items = [(qk, h) for qk in (1, 0) for h in range(8)]

        def pm0(it, r0=r0, ct=ct, ctb=ctb):
            qk, h = items[it]
            col = qk * D + h * 128
            bk = 2 + (it % 2)

            def mmp(e):
                ins = None
                for kc in range(8):
                    ins = e.matmul(c.ps[bk][:, :], win[:, kc, col:col + 128], hT[:, kc, :],
                                   start=(kc == 0), stop=(kc == 7))
                return ins
            P.add("pe", mmp, reads=[hT_b, win_b[col // 512]], writes=[c.psb[bk]])
            f, fb = qf[it % NQF], qf_b[it % NQF]
            P.add("act", lambda e: e.activation(f[:], c.ps[bk][:, :], AF.Copy), reads=[c.psb[bk]], writes=[fb])

        def pm1(it, r0=r0, ct=ct, ctb=ctb):
            qk, h = items[it]
            bkr = 4 + (it % 2)
            f, fb = qf[it % NQF], qf_b[it % NQF]
            P.add("pe", lambda e: e.matmul(c.ps[bkr][:, :], c.cst[:, C_ROT:C_ROT + 128], f[:], start=True, stop=True),
                  reads=[fb, c.cstb], writes=[c.psb[bkr]])
            tt, ttb = t1[it % NQF], t1_b[it % NQF]
            P.add("dve", lambda e: e.tensor_tensor(tt[:], f[:], ct[:, 0, :], ALU.mult), reads=[fb, ctb], writes=[ttb])
            if qk == 1:
                d32, dst32_b = kr32[it % NQF][:], kr32_b[it % NQF]
            else:
                d32, dst32_b = q32[:, h, :], q32_b[h]
            P.add("dve", lambda e: e.tensor_tensor(d32, c.ps[bkr][:, :], ct[:, 1, :], ALU.mult),
                  reads=[c.psb[bkr], ctb], writes=[dst32_b])

        def pm2(it, r0=r0, ct=ct, ctb=ctb, t=t):
            qk, h = items[it]
            tt, ttb = t1[it % NQF], t1_b[it % NQF]
            if qk == 1:
                d32, dst32_b = kr32[it % NQF][:], kr32_b[it % NQF]
            else:
                d32, dst32_b = q32[:, h, :], q32_b[h]
            P.add("dve", lambda e: e.tensor_tensor(d32, d32, tt[:], ALU.add), reads=[ttb, dst32_b], writes=[dst32_b])
            o, o_b = ob[it % 3], ob_b[it % 3]
            P.add("act", lambda e: e.activation(o[:], d32, AF.Copy), reads=[dst32_b], writes=[o_b])
            if qk == 1:
                P.add("dve", lambda e: e.tensor_reduce(
                    kmean[:, h, 2 * t:2 * t + 2], d32.rearrange("p (a b) -> p a b", a=2), AX.X, ALU.add),
                    reads=[dst32_b], writes=[kmean_b])
                P.dma(kT_d[h, :, r0:r0 + TT], o[:], reads=[o_b], sembuf=o_b)
            else:
                P.dma(qT_d[h, :, r0:r0 + TT], o[:], reads=[o_b], sembuf=o_b)
        pipeline(16, [pm0, pm1, pm2], skew=SKEW)
        for s in range(4):
            for hh in range(2):
                def mmv(e, s=s, hh=hh):
                    ins = None
                    for kc in range(8):
                        ins = e.matmul(c.ps[hh][:, :], hT[:, kc, s * 128:(s + 1) * 128],
                                       win[:, kc, 2 * D + hh * 512:2 * D + (hh + 1) * 512],
                                       start=(kc == 0), stop=(kc == 7))
                    return ins
                P.add("pe", mmv, reads=[hT_b, win_b[4 + hh]], writes=[c.psb[hh]])
            v, v_b = vb[s % 2], vb_b[s % 2]
            P.add("act", lambda e, v=v: e.activation(v[:, 0:512], c.ps[0][:, :], AF.Copy),
                  reads=[c.psb[0]], writes=[v_b])
            P.add("dve", lambda e, v=v: e.tensor_copy(v[:, 512:1024], c.ps[1][:, :]),
                  reads=[c.psb[1]], writes=[v_b])
            P.dma(v_d[r0 + s * 128:r0 + (s + 1) * 128, :], v[:], reads=[v_b], sembuf=v_b)
        for s in range(4):
            jb = (r0 + s * 128) // 256
            g, g_b = gm[s % 2], gm_b[s % 2]
            t8, sl = top8[s % 2], selt[s % 2]

            def mmgate(e, s=s):
                ins = None
                for h in range(8):
                    ins = e.matmul(c.ps[6][:, h * 16:(h + 1) * 16], q32[:, h, s * 128:(s + 1) * 128],
                                   kmean[:, h, :], start=True, stop=True)
                return ins
            P.add("pe", mmgate, reads=list(q32_b) + [kmean_b], writes=[c.psb[6]])
            pm = c.cst[:, C_PASTM + jb * 16:C_PASTM + jb * 16 + 16]
            psel = c.cst[:, C_PASTS + jb * 16:C_PASTS + jb * 16 + 16]
            P.add("dve", lambda e, g=g, pm=pm: e.tensor_tensor(
                g[:], c.ps[6][:, 0:128].rearrange("p (h n) -> p h n", h=8),
                pm.rearrange("p (o n) -> p o n", o=1).to_broadcast([128, 8, 16]), ALU.add),
                reads=[c.psb[6], c.cstb], writes=[g_b])
            for h in range(8):
                P.add("dve", lambda e, g=g, t8=t8, h=h: e.max(t8[:, h, :], g[:, h, :]), reads=[g_b], writes=[g_b])
            P.add("dve", lambda e, g=g, t8=t8, sl=sl: e.tensor_tensor(
                sl[:], g[:], t8[:, :, 2:3].to_broadcast([128, 8, 16]), ALU.is_ge), reads=[g_b], writes=[g_b])
            P.add("dve", lambda e, sl=sl, psel=psel: e.tensor_tensor(
                sl[:], sl[:], psel.rearrange("p (o n) -> p o n", o=1).to_broadcast([128, 8, 16]), ALU.mult),
                reads=[g_b, c.cstb], writes=[g_b])
            P.add("dve", lambda e, sl=sl: e.tensor_scalar(sl[:], sl[:], -1.0, -NEG, ALU.add, ALU.mult),
                  reads=[g_b], writes=[g_b])
            for half in range(2):
                def tr(e, half=half, sl=sl):
                    ins = None
                    for hq in range(4):
                        h = half * 4 + hq
                        ins = e.transpose(c.ps[7][0:16, hq * 128:(hq + 1) * 128], sl[:, h, :], c.ident)
                    return ins
                P.add("pe", tr, reads=[g_b, c.ident_b], writes=[c.psb[7]])
                b, b_b = bT[s % 2], bT_b[s % 2]
                P.add("act", lambda e, b=b, half=half: e.activation(
                    b[:, half * 4:half * 4 + 4, :], c.ps[7][0:16, :].rearrange("p (a q) -> p a q", a=4), AF.Copy),
                    reads=[c.psb[7]], writes=[b_b])
            b, b_b = bT[s % 2], bT_b[s % 2]
            P.dma(biasT_d[:, :, r0 + s * 128:r0 + (s + 1) * 128].rearrange("h n q -> n h q"), b[:],
                  reads=[b_b], sembuf=b_b)
    P.barrier()
    A.release(m0)


def stage_moba_attn(c, qT_d, kT_d, v_d, biasT_d, oT_d, T):
    P, A = c.P, c.A
    m0 = A.mark()
    NBLK = T // 256
    scale = 128 ** -0.5
    kT = [A.alloc([128, T], BF16, "kT") for _ in range(2)]
    qT = [A.alloc([128, T], BF16, "qT") for _ in range(2)]
    vh = [A.alloc([128, T // 128, 128], BF16, "vh") for _ in range(2)]
    bT = [A.alloc([128, T], BF16, "bTh") for _ in range(2)]
    kT_b, qT_b, vh_b, bT_b = (P.bufs_n("kT", 2), P.bufs_n("qT", 2), P.bufs_n("vh", 2), P.bufs_n("bTh", 2))
    for i in range(2):
        P.add("dve", lambda e, i=i: e.memset(bT[i][:], 0.0), writes=[bT_b[i]])
    pT = [A.alloc([128, 256], BF16, "pT") for _ in range(3)]
    pT_b = P.bufs_n("pT", 3)
    rd = [A.alloc([128, 256], F32, "rd") for _ in range(2)]
    rd_b = P.bufs_n("rd", 2)
    oo = [A.alloc([128, 256], BF16, "oo") for _ in range(2)]
    oo_b = P.bufs_n("oo", 2)
    onesb = c.cstbf[:, CB_ONES:CB_ONES + 128]
    for h in range(8):
        hb = h % 2
        P.dma(kT[hb][:], kT_d[h, :, :], writes=[kT_b[hb]], sembuf=kT_b[hb])
        P.dma(qT[hb][:], qT_d[h, :, :], writes=[qT_b[hb]], sembuf=qT_b[hb])
        P.dma(vh[hb][:], v_d[:, h * 128:(h + 1) * 128].rearrange("(c p) d -> p c d", p=128),
              writes=[vh_b[hb]], sembuf=vh_b[hb])
        P.dma(bT[hb][0:16, :], biasT_d[h, :, :], writes=[bT_b[hb]], sembuf=bT_b[hb])
        pairs = [(j, kt) for j in range(NBLK) for kt in range(2 * j + 2)]

        def step_s(idx, hb=hb):
            j, kt = pairs[idx]
            q0 = j * 256
            sb_ = idx % 3
            n = kt // 2
            own = (n == j)

            def mms(e):
                e.matmul(c.ps[sb_][:, 0:256], kT[hb][:, kt * 128:(kt + 1) * 128], qT[hb][:, q0:q0 + 256],
                         start=True, stop=False)
                if own:
                    return e.matmul(c.ps[sb_][:, 0:256], c.identb,
                                    c.cstbf[:, CB_CM + (kt % 2) * 256:CB_CM + (kt % 2) * 256 + 256],
                                    start=False, stop=True)
                return e.matmul(c.ps[sb_][:, 0:256], c.cstbf[:, CB_EN + n * 128:CB_EN + (n + 1) * 128],
                                bT[hb][:, q0:q0 + 256], start=False, stop=True)
            P.add("pe", mms, reads=[kT_b[hb], qT_b[hb], bT_b[hb], c.cstbf_b], writes=[c.psb[sb_]])
            p, p_b = pT[idx % 3], pT_b[idx % 3]
            P.add("act", lambda e: e.activation(p[:], c.ps[sb_][:, 0:256], AF.Exp, scale=scale),
                  reads=[c.psb[sb_]], writes=[p_b])

        def step_pv(idx, hb=hb, h=h):
            j, kt = pairs[idx]
            q0 = j * 256
            nkt = 2 * j + 2
            ob_, db_ = 3 + (j % 2), 5 + (j % 2)
            p, p_b = pT[idx % 3], pT_b[idx % 3]

            def mmpv(e):
                e.matmul(c.ps[ob_][:, 0:256], vh[hb][:, kt, :], p[:], start=(kt == 0), stop=(kt == nkt - 1))
                return e.matmul(c.ps[db_][:, 0:256], onesb, p[:], start=(kt == 0), stop=(kt == nkt - 1))
            P.add("pe", mmpv, reads=[vh_b[hb], p_b, c.cstbf_b], writes=[c.psb[ob_], c.psb[db_]])
            if kt == nkt - 1:
                r, r_b = rd[j % 2], rd_b[j % 2]
                o, o_b = oo[j % 2], oo_b[j % 2]
                P.add("dve", lambda e: e.reciprocal(r[:], c.ps[db_][:, 0:256]), reads=[c.psb[db_]], writes=[r_b])
                P.add("dve", lambda e: e.tensor_tensor(o[:], c.ps[ob_][:, 0:256], r[:], ALU.mult),
                      reads=[c.psb[ob_], r_b], writes=[o_b])
                P.dma(oT_d[h, :, q0:q0 + 256], o[:], reads=[o_b], sembuf=o_b)
        pipeline(len(pairs), [step_s, step_pv], skew=SKEW)
    P.barrier()
    A.release(m0)


def stage_outproj_ln(c, actT_d, NK, w, src, dst, ln_g, ln_b, li, si, cres, T):
    P, A = c.P, c.A
    m0 = A.mark()
    TT = 512
    wo = A.alloc([128, NK, D], BF16, "wo")
    wo_b = P.bufs_n("wo", NK)
    w_v = w.rearrange("(j p) d -> p j d", p=128)
    for j in range(NK):
        P.dma(wo[:, j, :], w_v[:, j, :], writes=[wo_b[j]], sembuf=wo_b[j], eng="pool")
    gt, bt, gb_b = load_ln_params(c, ln_g, ln_b, li, si)
    aT = [A.alloc([128, NK, TT], BF16, "oaT") for _ in range(2)]
    aT_b = P.bufs_n("oaT", 2)
    xs = [A.alloc([128, D], F32, "xs") for _ in range(3)]
    xs_b = P.bufs_n("xs", 3)
    lns = alloc_ln_small(c, 2)
    xi = 0
    for t in range(T // TT):
        r0 = t * TT
        a, a_b = aT[t % 2], aT_b[t % 2]
        P.dma(a[:], actT_d[:, :, r0:r0 + TT].rearrange("k p t -> p k t"), writes=[a_b], sembuf=a_b)
        def o0(s, a=a, a_b=a_b):
            banks = (4, 5) if s % 2 == 0 else (6, 7)
            for hh, bk in enumerate(banks):
                def mmo(e, hh=hh, bk=bk):
                    ins = None
                    for j in range(NK):
                        ins = e.matmul(c.ps[bk][:, :], a[:, j, s * 128:(s + 1) * 128],
                                       wo[:, j, hh * 512:(hh + 1) * 512], start=(j == 0), stop=(j == NK - 1))
                    return ins
                P.add("pe", mmo, reads=[a_b] + list(wo_b), writes=[c.psb[bk]])

        def o1(s, r0=r0):
            banks = (4, 5) if s % 2 == 0 else (6, 7)
            k = (r0 // 128 + s)
            x, xb = xs[k % 3], xs_b[k % 3]
            resid_ln(c, src, dst, r0 + s * 128, banks, cres, x, xb, gt, bt, gb_b, lns[s % 2])
        pipeline(4, [o0, o1])
    P.barrier()
    A.release(m0)


GQ, GK, GV, GZ, GB_, GA_ = 0, 1024, 2048, 4096, 6144, 6160
GPROJ = 6176


def psbf(c, bk):
    return c.ps[bk][:].bitcast(BF16)


def stage_gdn_proj(c, src, w_in, conv_w, a_log, dt_bias, qT_d, kT_d, ktok_d, vtok_d, z_d, g_d, beta_d, T):
    P, A = c.P, c.A
    m0 = A.mark()
    TT = 512
    win = A.alloc([128, 8, GPROJ], BF16, "gwin")
    NG = 13
    win_b = P.bufs_n("gwin", NG)
    w_in_v = w_in.rearrange("(kc p) n -> p kc n", p=128)
    for g in range(NG):
        lo, hi = g * 512, min(GPROJ, (g + 1) * 512)
        P.dma(win[:, :, lo:hi], w_in_v[:, :, lo:hi], writes=[win_b[g]], sembuf=win_b[g], eng="pool")
    cw = A.alloc([128, 32, 4], F32, "cw")
    cw_b = P.buf("cw")
    cwl = A.alloc([32, 4, 128], F32, "cwl")
    cwl_b = P.buf("cwl")
    P.dma(cwl[:], conv_w.rearrange("i (cc p) -> cc i p", p=128), writes=[cwl_b], sembuf=cwl_b)

    def trcw(e):
        ins = None
        for i in range(4):
            ins = e.transpose(c.ps[7][:, i * 32:(i + 1) * 32], cwl[:, i, :], c.cst[0:32, C_IDENT:C_IDENT + 32])
        return ins
    P.add("pe", trcw, reads=[cwl_b, c.cstb], writes=[c.psb[7]])
    P.add("dve", lambda e: e.tensor_copy(cw[:].rearrange("p cc i -> p i cc"),
                                         c.ps[7][:, 0:128].rearrange("p (i cc) -> p i cc", i=4)),
          reads=[c.psb[7]], writes=[cw_b])
    Wd = A.alloc([128, 32, 4, 128], BF16, "Wd")
    Wd_b = P.buf("Wd")
    for cc in range(32):
        for i in range(4):
            P.add("dve", lambda e, cc=cc, i=i: e.tensor_scalar(Wd[:, cc, i, :], c.ident, cw[:, cc, i:i + 1], None, ALU.mult),
                  reads=[cw_b, c.cstb], writes=[Wd_b])
    negA = A.alloc([128, 16], F32, "negA")
    dtb = A.alloc([128, 16], F32, "dtb")
    gc_b = P.buf("gconst")
    b1, b2 = P.buf("alog"), P.buf("dtb")
    P.dma(negA[:], a_log.partition_broadcast(128), writes=[b1], sembuf=b1)
    P.dma(dtb[:], dt_bias.partition_broadcast(128), writes=[b2], sembuf=b2)
    P.add("act", lambda e: e.activation(negA[:], negA[:], AF.Exp), reads=[b1], writes=[gc_b])
    P.add("dve", lambda e: e.tensor_scalar(negA[:], negA[:], -1.0, None, ALU.mult), reads=[gc_b, b2], writes=[gc_b])
    epsq = A.alloc([128, 2], F32, "epsq")
    P.add("dve", lambda e: e.memset(epsq[:, 0:1], 128.0 * RMS_EPS), writes=[gc_b])
    P.add("dve", lambda e: e.memset(epsq[:, 1:2], RMS_EPS), reads=[gc_b], writes=[gc_b])
    halo = A.alloc([128, 32, 4], BF16, "halo")
    halo_b = P.bufs_n("halo", 32)
    P.add("dve", lambda e: e.memset(halo[:], 0.0), writes=list(halo_b))
    hT = A.alloc([128, 8, TT], BF16, "hT")
    hT_b = P.buf("hT")
    xs = [A.alloc([128, D], F32, "xs") for _ in range(3)]
    xs_b = P.bufs_n("xs", 3)
    NXP, NQS, NOB = 3, 3, 5
    xpre = [A.alloc([128, TT + 4], BF16, "xpre") for _ in range(NXP)]
    xpre_b = P.bufs_n("xpre", NXP)
    qs = [A.alloc([128, TT], F32, "qs") for _ in range(NQS)]
    qs_b = P.bufs_n("qs", NQS)
    sq = [A.alloc([128, TT], BF16, "sq") for _ in range(NQS)]
    sq_b = P.bufs_n("sq", NQS)
    rn = [A.alloc([128, TT], F32, "rn") for _ in range(2)]
    rn_b = P.bufs_n("rn", 2)
    ob = [A.alloc([128, TT], BF16, "gob") for _ in range(NOB)]
    ob_b = P.bufs_n("gob", NOB)
    tk = [A.alloc([128, 4, 128], BF16, "tk") for _ in range(3)]
    tk_b = P.bufs_n("tk", 3)
    zt = [A.alloc([128, 512], F32, "zt") for _ in range(3)]
    zt_b = P.bufs_n("zt", 3)
    zi = 0
    sm = [A.alloc([128, 4, 16], F32, "gsm") for _ in range(2)]
    sm_b = P.bufs_n("gsm", 2)
    onesb = c.cstbf[:, CB_ONES:CB_ONES + 128]
    xi = oi = ti = 0
    for t in range(T // TT):
        r0 = t * TT
        for s in range(4):
            x, xb = xs[xi % 3], xs_b[xi % 3]
            xi += 1
            P.dma(x[:], src[r0 + s * 128:r0 + (s + 1) * 128, :], writes=[xb], sembuf=xb)
            transpose_in(c, x, xb, hT, hT_b, s * 128, (0, 1))
        order = []
        for i in range(16):
            order += [i, 16 + i]
        def s0(pi_, r0=r0):
            cc = order[pi_]
            col = cc * 128
            bk = 2 + (pi_ % 2)

            def mmp(e):
                ins = None
                for kc in range(8):
                    ins = e.matmul(c.ps[bk][:, :], win[:, kc, col:col + 128], hT[:, kc, :],
                                   start=(kc == 0), stop=(kc == 7))
                return ins
            P.add("pe", mmp, reads=[hT_b, win_b[col // 512]], writes=[c.psb[bk]])
            xp, xp_b = xpre[pi_ % NXP], xpre_b[pi_ % NXP]
            P.add("dve", lambda e: e.tensor_copy(xp[:, 0:4], halo[:, cc, :]), reads=[halo_b[cc]], writes=[xp_b])
            P.add("act", lambda e: e.activation(xp[:, 4:TT + 4], c.ps[bk][:, :], AF.Copy),
                  reads=[c.psb[bk]], writes=[xp_b])
            P.add("dve", lambda e: e.tensor_copy(halo[:, cc, :], xp[:, TT:TT + 4]), reads=[xp_b], writes=[halo_b[cc]])

        def s1(pi_, r0=r0):
            cc = order[pi_]
            bkc = 4 + (pi_ % 2)
            xp, xp_b = xpre[pi_ % NXP], xpre_b[pi_ % NXP]

            def mmc(e):
                ins = None
                for i in range(4):
                    ins = e.matmul(c.ps[bkc][:, :], Wd[:, cc, i, :], xp[:, 1 + i:1 + i + TT], start=(i == 0), stop=(i == 3))
                return ins
            P.add("pe", mmc, reads=[xp_b, Wd_b], writes=[c.psb[bkc]])
            o, o_b = ob[pi_ % NOB], ob_b[pi_ % NOB]
            if cc < 16:
                q_, q_b = qs[pi_ % NQS], qs_b[pi_ % NQS]
                s_, s_b = sq[pi_ % NQS], sq_b[pi_ % NQS]
                P.add("act", lambda e: e.activation(q_[:], c.ps[bkc][:, :], AF.Silu), reads=[c.psb[bkc]], writes=[q_b])
                P.add("act", lambda e: e.activation(s_[:], q_[:], AF.Square), reads=[q_b], writes=[s_b])
            else:
                P.add("act", lambda e: e.activation(o[:], c.ps[bkc][:, :], AF.Silu), reads=[c.psb[bkc]], writes=[o_b])

        def s2(pi_, r0=r0):
            cc = order[pi_]
            if cc >= 16:
                return
            o, o_b = ob[pi_ % NOB], ob_b[pi_ % NOB]
            q_, q_b = qs[pi_ % NQS], qs_b[pi_ % NQS]
            s_, s_b = sq[pi_ % NQS], sq_b[pi_ % NQS]
            r_, r_b = rn[(pi_ // 2) % 2], rn_b[(pi_ // 2) % 2]
            b6 = 6 if (pi_ // 2) % 2 == 0 else 0
            P.add("pe", lambda e: e.matmul(c.ps[b6][:, :], onesb, s_[:], start=True, stop=True),
                  reads=[s_b, c.cstbf_b], writes=[c.psb[b6]])
            if cc < 8:
                P.add("act", lambda e: e.activation(r_[:], c.ps[b6][:, :], AF.Sqrt, bias=epsq[:, 0:1], scale=128.0),
                      reads=[c.psb[b6], gc_b], writes=[r_b])
            else:
                P.add("act", lambda e: e.activation(r_[:], c.ps[b6][:, :], AF.Sqrt, bias=epsq[:, 1:2], scale=1.0),
                      reads=[c.psb[b6], gc_b], writes=[r_b])
            P.add("dve", lambda e: e.reciprocal(r_[:], r_[:]), reads=[r_b], writes=[r_b])
            P.add("dve", lambda e: e.tensor_tensor(o[:], q_[:], r_[:], ALU.mult),
                  reads=[r_b, q_b], writes=[o_b])
            hd = cc % 8
            P.dma((qT_d if cc < 8 else kT_d)[hd, :, r0:r0 + TT], o[:], reads=[o_b], sembuf=o_b)

        def s3(pi_, r0=r0):
            cc = order[pi_]
            if cc < 8:
                return
            o, o_b = ob[pi_ % NOB], ob_b[pi_ % NOB]
            b7 = 7 if pi_ % 2 == 0 else 1
            pb = psbf(c, b7)

            def trk(e):
                ins = None
                for s in range(4):
                    ins = e.transpose(pb[:, s * 128:(s + 1) * 128], o[:, s * 128:(s + 1) * 128], c.identb)
                return ins
            P.add("pe", trk, reads=[o_b, c.cstbf_b], writes=[c.psb[b7]])
            k_, k_b = tk[pi_ % 3], tk_b[pi_ % 3]
            P.add("dve", lambda e: e.tensor_copy(k_[:], pb[:, 0:512].rearrange("p (s d) -> p s d", s=4)),
                  reads=[c.psb[b7]], writes=[k_b])
            if cc < 16:
                dd = ktok_d[r0:r0 + TT, (cc - 8) * 128:(cc - 7) * 128]
            else:
                dd = vtok_d[r0:r0 + TT, (cc - 16) * 128:(cc - 15) * 128]
            P.dma(dd.rearrange("(s p) d -> p s d", p=128), k_[:], reads=[k_b], sembuf=k_b)
        pipeline(32, [s0, s1, s2, s3], skew=SKEW)
        for s in range(4):
            for zq in range(4):
                z_, z_b = zt[zi % 3], zt_b[zi % 3]
                zi += 1
                bk = zq

                def mmz(e, s=s, zq=zq, bk=bk):
                    ins = None
                    for kc in range(8):
                        ins = e.matmul(c.ps[bk][:, :], hT[:, kc, s * 128:(s + 1) * 128],
                                       win[:, kc, GZ + zq * 512:GZ + (zq + 1) * 512], start=(kc == 0), stop=(kc == 7))
                    return ins
                P.add("pe", mmz, reads=[hT_b] + [win_b[(GZ + zq * 512) // 512]], writes=[c.psb[bk]])
                P.add("act", lambda e, z_=z_, bk=bk: e.activation(z_[:], c.ps[bk][:, :], AF.Silu),
                      reads=[c.psb[bk]], writes=[z_b])
                P.dma(z_d[r0 + s * 128:r0 + (s + 1) * 128, zq * 512:(zq + 1) * 512], z_[:], reads=[z_b], sembuf=z_b)

            def mmba(e, s=s):
                ins = None
                for kc in range(8):
                    ins = e.matmul(c.ps[6][:, 0:32], hT[:, kc, s * 128:(s + 1) * 128], win[:, kc, GB_:GB_ + 32],
                                   start=(kc == 0), stop=(kc == 7))
                return ins
            P.add("pe", mmba, reads=[hT_b, win_b[12]], writes=[c.psb[6]])
            m_, m_b = sm[s % 2], sm_b[s % 2]
            P.add("act", lambda e, m_=m_: e.activation(m_[:, 0, :], c.ps[6][:, 0:16], AF.Sigmoid), reads=[c.psb[6]], writes=[m_b])
            P.add("dve", lambda e, m_=m_: e.tensor_tensor(m_[:, 2, :], c.ps[6][:, 16:32], dtb[:], ALU.add),
                  reads=[c.psb[6], b2, gc_b], writes=[m_b])
            P.add("act", lambda e, m_=m_: e.activation(m_[:, 2, :], m_[:, 2, :], AF.Exp), reads=[m_b], writes=[m_b])
            P.add("act", lambda e, m_=m_: e.activation(m_[:, 3, :], m_[:, 2, :], AF.Ln, bias=c.one_col[:], scale=1.0),
                  reads=[m_b, c.cst_b], writes=[m_b])
            P.add("dve", lambda e, m_=m_: e.tensor_tensor(m_[:, 1, :], m_[:, 3, :], negA[:], ALU.mult),
                  reads=[m_b, gc_b], writes=[m_b])
            P.dma(beta_d[r0 + s * 128:r0 + (s + 1) * 128, :], m_[:, 0, :], reads=[m_b], sembuf=m_b)
            P.dma(g_d[r0 + s * 128:r0 + (s + 1) * 128, :], m_[:, 1, :], reads=[m_b], sembuf=m_b)
    P.barrier()
    A.release(m0)


def stage_gdn_scan(c, qT_d, kT_d, ktok_d, vtok_d, g_d, beta_d, o_d, T):
    P, A = c.P, c.A
    m0 = A.mark()
    NCH = T // 128
    H = 16
    UT = c.cst[:, C_UT:C_UT + 128]
    ones32 = c.cst[:, C_ONES:C_ONES + 128]
    SM = c.cst[:, C_SMASK:C_SMASK + 128]
    IMT = c.cst[:, C_IMASKT:C_IMASKT + 128]

    def f32t(name):
        return A.alloc([128, H, 128], F32, name), P.bufs_n(name, 4)

    def bf16t(name):
        return A.alloc([128, H, 128], BF16, name), P.bufs_n(name, 4)
    qTc = [A.alloc([128, 8, 128], BF16, "qTc") for _ in range(2)]
    kTc = [A.alloc([128, 8, 128], BF16, "kTc") for _ in range(2)]
    ktk = [A.alloc([128, 8, 128], BF16, "ktk") for _ in range(2)]
    vtk = [A.alloc([128, H, 128], BF16, "vtk") for _ in range(2)]
    gb = [A.alloc([128, 2, H], F32, "gb") for _ in range(2)]
    qTc_b, kTc_b, ktk_b, vtk_b = P.bufs_n("qTc", 2), P.bufs_n("kTc", 2), P.bufs_n("ktk", 2), P.bufs_n("vtk", 2)
    g_b, be_b = P.bufs_n("gld", 2), P.bufs_n("bld", 2)
    def f32rt(name):
        return A.alloc([128, H, 128], F32R, name), P.bufs_n(name, 4)
    F1, F1b = f32rt("F1")
    F2, F2b = f32t("F2")
    F3, F3b = f32t("F3")
    Rp = [f32rt("R%d" % i) for i in range(2)]
    RTp = [f32rt("RT%d" % i) for i in range(2)]
    Y, Yb = f32rt("Y")
    identr_t = A.alloc([128, 128], F32R, "identr")
    identr_b = P.buf("identr")
    P.add("dve", lambda e: e.tensor_copy(identr_t[:], c.ident), reads=[c.cstb], writes=[identr_b])
    identr = identr_t[:]
    attnT, attnT_b = bf16t("attnT")
    TTb, TTb_b = bf16t("TTb")
    vbt, vbt_b = bf16t("vbt")
    kw, kw_b = bf16t("kw")
    wT, wT_b = bf16t("wT")
    qdT, qdT_b = bf16t("qdT")
    kdec, kdec_b = bf16t("kdec")
    vnew, vnew_b = bf16t("vnew")
    osb, osb_b = f32t("osb")
    S, S_b = f32t("S")
    Sbf, Sbf_b = bf16t("Sbf")
    sm = A.alloc([128, 8, H], F32, "ssm")
    sm_b = P.buf("ssm")
    P.add("dve", lambda e: e.memset(S[:], 0.0), writes=S_b)
    P.add("dve", lambda e: e.memset(Sbf[:], 0.0), writes=Sbf_b)

    def flat(t):
        return t[:].rearrange("p h d -> p (h d)")

    def bank4(q):
        return c.ps[q][:].rearrange("p (h d) -> p h d", h=4)

    def hs4(q):
        return slice(4 * q, 4 * q + 4)

    for ch in range(NCH):
        c0 = ch * 128
        pb = ch % 2
        P.dma(qTc[pb][:], qT_d[:, :, c0:c0 + 128].rearrange("h p t -> p h t"), writes=[qTc_b[pb]], sembuf=qTc_b[pb])
        P.dma(kTc[pb][:], kT_d[:, :, c0:c0 + 128].rearrange("h p t -> p h t"), writes=[kTc_b[pb]], sembuf=kTc_b[pb])
        P.dma(ktk[pb][:], ktok_d[c0:c0 + 128, :].rearrange("p (h d) -> p h d", h=8), writes=[ktk_b[pb]], sembuf=ktk_b[pb])
        P.dma(vtk[pb][:], vtok_d[c0:c0 + 128, :].rearrange("p (h d) -> p h d", h=H), writes=[vtk_b[pb]], sembuf=vtk_b[pb])
        P.dma(gb[pb][:, 0, :], g_d[c0:c0 + 128, :], writes=[g_b[pb]], sembuf=g_b[pb])
        P.dma(gb[pb][:, 1, :], beta_d[c0:c0 + 128, :], writes=[be_b[pb]], sembuf=be_b[pb])
        g = gb[pb][:, 0, :]
        beta = gb[pb][:, 1, :]
        def mm1(e, g=g):
            e.matmul(c.ps[0][:, 0:16], UT, g, start=True, stop=True)
            return e.matmul(c.ps[0][:, 16:32], ones32, g, start=True, stop=True)
        P.add("pe", mm1, reads=[g_b[pb], c.cstb], writes=[c.psb[0]])
        P.add("act", lambda e: e.activation(sm[:, 0, :], c.ps[0][:, 0:16], AF.Copy), reads=[c.psb[0]], writes=[sm_b])
        P.add("act", lambda e: e.activation(sm[:, 1, :], c.ps[0][:, 0:16], AF.Identity, scale=-1.0), reads=[c.psb[0]], writes=[sm_b])
        P.add("act", lambda e: e.activation(sm[:, 2, :], c.ps[0][:, 0:16], AF.Exp), reads=[c.psb[0]], writes=[sm_b])
        P.add("act", lambda e: e.activation(sm[:, 3, :], c.ps[0][:, 16:32], AF.Exp), reads=[c.psb[0]], writes=[sm_b])
        P.add("act", lambda e: e.activation(sm[:, 7, :], c.ps[0][:, 16:32], AF.Identity, bias=0.0, scale=1.0),
              reads=[c.psb[0]], writes=[sm_b])
        P.add("dve", lambda e: e.tensor_tensor(sm[:, 7, :], sm[:, 7, :], sm[:, 0, :], ALU.subtract),
              reads=[sm_b], writes=[sm_b])
        P.add("act", lambda e: e.activation(sm[:, 4, :], sm[:, 7, :], AF.Exp), reads=[sm_b], writes=[sm_b])
        P.add("dve", lambda e, beta=beta: e.tensor_tensor(sm[:, 5, :], beta, sm[:, 2, :], ALU.mult),
              reads=[sm_b, be_b[pb]], writes=[sm_b])
        P.add("dve", lambda e, beta=beta: e.tensor_scalar(sm[:, 6, :], beta, -1.0, None, ALU.mult),
              reads=[sm_b, be_b[pb]], writes=[sm_b])
        for q in range(4):
            P.add("dve", lambda e, g=g, q=q: e.tensor_tensor(
                F3[:, hs4(q), :], UT.rearrange("p (o i) -> p o i", o=1).to_broadcast([128, 4, 128]),
                g[:, 4 * q:4 * q + 4].rearrange("p (h o) -> p h o", o=1).to_broadcast([128, 4, 128]), ALU.mult),
                reads=[g_b[pb], c.cstb], writes=[F3b[q]])
            P.add("pe", lambda e, q=q: e.matmul(c.ps[4 + q][:, :], ones32, F3[:, hs4(q), :].rearrange("p h d -> p (h d)"),
                                                start=True, stop=True), reads=[F3b[q], c.cstb], writes=[c.psb[4 + q]])
        gcrow_b = [c.psb[4 + q] for q in range(4)]
        for q in range(4):
            P.add("act", lambda e, q=q: e.activation(F3[:, hs4(q), :], bank4(4 + q), AF.Exp),
                  reads=[gcrow_b[q]], writes=[F3b[q]])
            P.add("dve", lambda e, q=q: e.tensor_tensor(
                F2[:, hs4(q), :], bank4(4 + q),
                IMT.rearrange("p (o i) -> p o i", o=1).to_broadcast([128, 4, 128]), ALU.add),
                reads=[gcrow_b[q], c.cstb], writes=[F2b[q]])
        for par in range(2):
            P.add("pool", lambda e, par=par, pb=pb: e.tensor_tensor(qdT[:, par::2, :], qTc[pb][:], F3[:, par::2, :], ALU.mult),
                  reads=list(F3b) + [qTc_b[pb]], writes=qdT_b)
        for h in range(H):
            P.add("act", lambda e, h=h: e.activation(F2[:, h, :], F2[:, h, :], AF.Exp, bias=sm[:, 1, h:h + 1], scale=1.0),
                  reads=[F2b[h // 4], sm_b], writes=[F2b[h // 4]])
        def mmkk(e, pb=pb):
            ins = None
            for hk in range(8):
                ins = e.matmul(c.ps[hk // 4][:, (hk % 4) * 128:(hk % 4 + 1) * 128], kTc[pb][:, hk, :], kTc[pb][:, hk, :],
                               start=True, stop=True)
            return ins
        P.add("pe", mmkk, reads=[kTc_b[pb]], writes=[c.psb[0], c.psb[1]])

        def mmqk(e, pb=pb):
            ins = None
            for hk in range(8):
                ins = e.matmul(c.ps[2 + hk // 4][:, (hk % 4) * 128:(hk % 4 + 1) * 128], kTc[pb][:, hk, :], qTc[pb][:, hk, :],
                               start=True, stop=True)
            return ins
        P.add("pe", mmqk, reads=[kTc_b[pb], qTc_b[pb]], writes=[c.psb[2], c.psb[3]])
        for hf in range(2):
            for par in range(2):
                P.add("dve", lambda e, par=par, hf=hf: e.tensor_tensor(
                    attnT[:, 8 * hf + par:8 * hf + 8:2, :], bank4(2 + hf), F2[:, 8 * hf + par:8 * hf + 8:2, :], ALU.mult),
                    reads=[c.psb[2 + hf], F2b[2 * hf], F2b[2 * hf + 1]], writes=[attnT_b[2 * hf], attnT_b[2 * hf + 1]])
        for q in range(4):
            P.add("dve", lambda e, q=q: e.tensor_tensor(
                F2[:, hs4(q), :], SM.rearrange("p (o i) -> p o i", o=1).to_broadcast([128, 4, 128]),
                bank4(4 + q), ALU.subtract),
                reads=[gcrow_b[q], c.cstb], writes=[F2b[q]])
        for h in range(H):
            P.add("act", lambda e, h=h: e.activation(F2[:, h, :], F2[:, h, :], AF.Exp, bias=sm[:, 0, h:h + 1], scale=1.0),
                  reads=[F2b[h // 4], sm_b], writes=[F2b[h // 4]])
        for h in range(H):
            hk = h // 2
            P.add("dve", lambda e, h=h, hk=hk: e.scalar_tensor_tensor(
                F1[:, h, :], c.ps[hk // 4][:, (hk % 4) * 128:(hk % 4 + 1) * 128], sm[:, 6, h:h + 1], F2[:, h, :],
                ALU.mult, ALU.mult), reads=[c.psb[hk // 4], sm_b, F2b[h // 4]], writes=[F1b[h // 4]])
        R, Rb = Rp[0]
        for hf in range(4):
            bk = 2 + (hf % 2)

            def trm(e, hf=hf, bk=bk):
                ins = None
                for hq in range(4):
                    h = hf * 4 + hq
                    ins = e.transpose(c.ps[bk][:, hq * 128:(hq + 1) * 128].bitcast(F32R), F1[:, h, :], identr)
                return ins
            P.add("pe", trm, reads=[F1b[hf], identr_b], writes=[c.psb[bk]])
            P.add("act", lambda e, hf=hf, R=R, bk=bk: e.activation(R[:, hs4(hf), :], bank4(bk), AF.Copy),
                  reads=[c.psb[bk]], writes=[Rb[hf]])
            P.add("dve", lambda e, hf=hf, bk=bk: e.tensor_tensor(
                Y[:, hs4(hf), :], bank4(bk),
                c.ident.rearrange("p (o i) -> p o i", o=1).to_broadcast([128, 4, 128]), ALU.add),
                reads=[c.psb[bk], c.cstb], writes=[Yb[hf]])
        RT, RTb = F1, F1b
        NLEV = 6
        for lev in range(1, NLEV + 1):
            Rn, Rnb = Rp[lev % 2]
            RTn, RTnb = RTp[lev % 2]
            last = (lev == NLEV)
            for hf in range(4):
                b_rt, b_r, b_y = (hf % 2), 2 + (hf % 2), 4 + (hf % 2)

                def mmrt(e, hf=hf, R=R, RT=RT, bk=b_rt):
                    ins = None
                    for hq in range(4):
                        h = hf * 4 + hq
                        ins = e.matmul(c.ps[bk][:, hq * 128:(hq + 1) * 128], R[:, h, :], RT[:, h, :], start=True, stop=True)
                    return ins
                P.add("pe", mmrt, reads=[Rb[hf], RTb[hf]], writes=[c.psb[b_rt]])
                P.add("act", lambda e, hf=hf, RTn=RTn, bk=b_rt: e.activation(RTn[:, hs4(hf), :], bank4(bk), AF.Copy),
                      reads=[c.psb[b_rt]], writes=[RTnb[hf]])
                if not last:
                    def mmr(e, hf=hf, R=R, RT=RT, bk=b_r):
                        ins = None
                        for hq in range(4):
                            h = hf * 4 + hq
                            ins = e.matmul(c.ps[bk][:, hq * 128:(hq + 1) * 128], RT[:, h, :], R[:, h, :], start=True, stop=True)
                        return ins
                    P.add("pe", mmr, reads=[Rb[hf], RTb[hf]], writes=[c.psb[b_r]])
                    P.add("dve", lambda e, hf=hf, Rn=Rn, bk=b_r: e.tensor_copy(Rn[:, hs4(hf), :], bank4(bk)),
                          reads=[c.psb[b_r]], writes=[Rnb[hf]])
                if hf >= 1:
                    hy = hf - 1
                    b_yy = 4 + (hy % 2)

                    def mmy(e, hf=hy, RTn=RTn, bk=b_yy):
                        ins = None
                        for hq in range(4):
                            h = hf * 4 + hq
                            ins = e.matmul(c.ps[bk][:, hq * 128:(hq + 1) * 128], RTn[:, h, :], Y[:, h, :], start=True, stop=True)
                        return ins
                    P.add("pe", mmy, reads=[RTnb[hy], Yb[hy]], writes=[c.psb[b_yy]])
                    P.add("dve", lambda e, hf=hy, bk=b_yy: e.tensor_tensor(Y[:, hs4(hf), :], bank4(bk), Y[:, hs4(hf), :].bitcast(F32), ALU.add),
                          reads=[c.psb[b_yy], Yb[hy]], writes=[Yb[hy]])
            hy = 3
            b_yy = 4 + (hy % 2)

            def mmy3(e, hf=hy, RTn=RTn, bk=b_yy):
                ins = None
                for hq in range(4):
                    h = hf * 4 + hq
                    ins = e.matmul(c.ps[bk][:, hq * 128:(hq + 1) * 128], RTn[:, h, :], Y[:, h, :], start=True, stop=True)
                return ins
            P.add("pe", mmy3, reads=[RTnb[hy], Yb[hy]], writes=[c.psb[b_yy]])
            P.add("dve", lambda e, hf=hy, bk=b_yy: e.tensor_tensor(Y[:, hs4(hf), :], bank4(bk), Y[:, hs4(hf), :].bitcast(F32), ALU.add),
                  reads=[c.psb[b_yy], Yb[hy]], writes=[Yb[hy]])
            R, Rb = Rn, Rnb
            RT, RTb = RTn, RTnb
        P.add("pool", lambda e, pb=pb, beta=beta: e.tensor_tensor(
            vbt[:], vtk[pb][:], beta.rearrange("p (h o) -> p h o", o=1).to_broadcast([128, H, 128]), ALU.mult),
            reads=[vtk_b[pb], be_b[pb]], writes=vbt_b)
        for par in range(2):
            P.add("pool", lambda e, par=par, pb=pb: e.tensor_tensor(
                kw[:, par::2, :], ktk[pb][:],
                sm[:, 5, par::2].rearrange("p (h o) -> p h o", o=1).to_broadcast([128, 8, 128]), ALU.mult),
                reads=[ktk_b[pb], sm_b], writes=kw_b)
            P.add("pool", lambda e, par=par, pb=pb: e.tensor_tensor(
                kdec[:, par::2, :], ktk[pb][:],
                sm[:, 4, par::2].rearrange("p (h o) -> p h o", o=1).to_broadcast([128, 8, 128]), ALU.mult),
                reads=[ktk_b[pb], sm_b], writes=kdec_b)
        for hf in range(4):
            P.add("act", lambda e, hf=hf: e.activation(TTb[:, hs4(hf), :], Y[:, hs4(hf), :].bitcast(F32), AF.Copy),
                  reads=[Yb[hf]], writes=[TTb_b[hf]])
        for hf in range(4):
            bu, bw_ = (hf % 2), 2 + (hf % 2)

            def mmu(e, hf=hf, bk=bu):
                ins = None
                for hq in range(4):
                    h = hf * 4 + hq
                    ins = e.matmul(c.ps[bk][:, hq * 128:(hq + 1) * 128], TTb[:, h, :], vbt[:, h, :], start=True, stop=True)
                return ins
            P.add("pe", mmu, reads=[TTb_b[hf]] + list(vbt_b), writes=[c.psb[bu]])
            P.add("act", lambda e, hf=hf, bk=bu: e.activation(F3[:, hs4(hf), :], bank4(bk), AF.Copy),
                  reads=[c.psb[bu]], writes=[F3b[hf]])

            def mmw(e, hf=hf, bk=bw_):
                ins = None
                for hq in range(4):
                    h = hf * 4 + hq
                    ins = e.matmul(c.ps[bk][:, hq * 128:(hq + 1) * 128], kw[:, h, :], TTb[:, h, :], start=True, stop=True)
                return ins
            P.add("pe", mmw, reads=[TTb_b[hf]] + list(kw_b), writes=[c.psb[bw_]])
            P.add("dve", lambda e, hf=hf, bk=bw_: e.tensor_copy(wT[:, hs4(hf), :], bank4(bk)),
                  reads=[c.psb[bw_]], writes=[wT_b[hf]])
        for hf in range(4):
            bk = 4 + (hf % 2)

            def mmws(e, hf=hf, bk=bk):
                ins = None
                for hq in range(4):
                    h = hf * 4 + hq
                    ins = e.matmul(c.ps[bk][:, hq * 128:(hq + 1) * 128], wT[:, h, :], Sbf[:, h, :], start=True, stop=True)
                return ins
            P.add("pe", mmws, reads=[wT_b[hf]] + list(Sbf_b), writes=[c.psb[bk]])
            P.add("dve", lambda e, hf=hf, bk=bk: e.tensor_tensor(vnew[:, hs4(hf), :], F3[:, hs4(hf), :], bank4(bk), ALU.subtract),
                  reads=[c.psb[bk], F3b[hf]], writes=[vnew_b[hf]])
        for hf in range(4):
            bk = 6 + (hf % 2)

            def mmo(e, hf=hf, bk=bk):
                ins = None
                for hq in range(4):
                    h = hf * 4 + hq
                    e.matmul(c.ps[bk][:, hq * 128:(hq + 1) * 128], qdT[:, h, :], Sbf[:, h, :], start=True, stop=False)
                    ins = e.matmul(c.ps[bk][:, hq * 128:(hq + 1) * 128], attnT[:, h, :], vnew[:, h, :], start=False, stop=True)
                return ins
            P.add("pe", mmo, reads=list(qdT_b) + list(Sbf_b) + [attnT_b[hf], vnew_b[hf]], writes=[c.psb[bk]])
            P.add("act", lambda e, hf=hf, bk=bk: e.activation(osb[:, hs4(hf), :], bank4(bk), AF.Copy),
                  reads=[c.psb[bk]], writes=[osb_b[hf]])
        P.dma(o_d[c0:c0 + 128, :], flat(osb), reads=list(osb_b), sembuf=osb_b[0])
        for hf in range(4):
            bk = (hf % 2)

            def mmds(e, hf=hf, bk=bk):
                ins = None
                for hq in range(4):
                    h = hf * 4 + hq
                    ins = e.matmul(c.ps[bk][:, hq * 128:(hq + 1) * 128], kdec[:, h, :], vnew[:, h, :], start=True, stop=True)
                return ins
            P.add("pe", mmds, reads=list(kdec_b) + [vnew_b[hf]], writes=[c.psb[bk]])
            for hq in range(4):
                h = hf * 4 + hq
                P.add("dve", lambda e, h=h, bk=bk, hq=hq: e.scalar_tensor_tensor(
                    S[:, h, :], S[:, h, :], sm[:, 3, h:h + 1], c.ps[bk][:, hq * 128:(hq + 1) * 128], ALU.mult, ALU.add),
                    reads=[c.psb[bk], sm_b, S_b[hf]], writes=[S_b[hf]])
            P.add("act", lambda e, hf=hf: e.activation(Sbf[:, hs4(hf), :], S[:, hs4(hf), :], AF.Copy),
                  reads=[S_b[hf]], writes=[Sbf_b[hf]])
    P.barrier()
    A.release(m0)


def stage_gdn_out(c, o_d, z_d, norm_w, w_out, src, dst, ln_g, ln_b, li, si, T):
    P, A = c.P, c.A
    m0 = A.mark()
    H = 16
    wo = A.alloc([128, H, D], BF16, "gwo")
    wo_b = P.bufs_n("gwo", H)
    w_v = w_out.rearrange("(j p) d -> p j d", p=128)
    for j in range(H):
        P.dma(wo[:, j, :], w_v[:, j, :], writes=[wo_b[j]], sembuf=wo_b[j], eng="pool")
    gt, bt, gb_b = load_ln_params(c, ln_g, ln_b, li, si)
    nw = A.alloc([128, 128], F32, "nw")
    nw_b = P.buf("nw")
    P.dma(nw[:], norm_w.partition_broadcast(128), writes=[nw_b], sembuf=nw_b)
    epsr = A.alloc([128, 1], F32, "epsr")
    P.add("dve", lambda e: e.memset(epsr[:], RMS_EPS), writes=[nw_b], reads=[nw_b])
    ot = [A.alloc([128, H, 128], F32, "ot") for _ in range(2)]
    zt = [A.alloc([128, H, 128], F32, "zt2") for _ in range(2)]
    ot_b, zt_b = P.bufs_n("ot", 2), P.bufs_n("zt2", 2)
    sqt = A.alloc([128, H, 128], F32, "sqt")
    sqt_b = P.buf("sqt")
    ssm = [A.alloc([128, 2, H], F32, "gsm2") for _ in range(2)]
    ssm_b = P.bufs_n("gsm2", 2)
    onb = [A.alloc([128, H, 128], BF16, "onb") for _ in range(3)]
    onb_b = P.bufs_n("onb", 3)
    onT = [A.alloc([128, H, 128], BF16, "onT") for _ in range(3)]
    onT_b = P.bufs_n("onT", 3)
    xs = [A.alloc([128, D], F32, "xs") for _ in range(3)]
    xs_b = P.bufs_n("xs", 3)
    lns = alloc_ln_small(c, 2)
    def g0(s):
        r0 = s * 128
        p2 = s % 2
        o, o_b, z, z_b = ot[p2], ot_b[p2], zt[p2], zt_b[p2]
        P.dma(o[:].rearrange("p h d -> p (h d)"), o_d[r0:r0 + 128, :], writes=[o_b], sembuf=o_b)
        P.dma(z[:].rearrange("p h d -> p (h d)"), z_d[r0:r0 + 128, :], writes=[z_b], sembuf=z_b)
        sm_, sm_b = ssm[p2], ssm_b[p2]
        P.add("act", lambda e: e.activation(sqt[:], o[:], AF.Square), reads=[o_b], writes=[sqt_b])
        P.add("dve", lambda e: e.tensor_reduce(sm_[:, 0, :], sqt[:], AX.X, ALU.add), reads=[sqt_b], writes=[sm_b])
        P.add("act", lambda e: e.activation(sm_[:, 1, :], sm_[:, 0, :], AF.Sqrt, bias=epsr[:], scale=1.0 / 128),
              reads=[sm_b, nw_b], writes=[sm_b])
        P.add("dve", lambda e: e.reciprocal(sm_[:, 1, :], sm_[:, 1, :]), reads=[sm_b], writes=[sm_b])
        P.add("dve", lambda e: e.tensor_tensor(
            z[:], z[:], nw[:].rearrange("p (o d) -> p o d", o=1).to_broadcast([128, H, 128]), ALU.mult),
            reads=[z_b, nw_b], writes=[z_b])
        on, on_b = onb[s % 3], onb_b[s % 3]
        for h in range(H):
            P.add("dve", lambda e, h=h: e.scalar_tensor_tensor(
                on[:, h, :], o[:, h, :], sm_[:, 1, h:h + 1], z[:, h, :], ALU.mult, ALU.mult),
                reads=[o_b, z_b, sm_b], writes=[on_b])

    def g1(s):
        on, on_b = onb[s % 3], onb_b[s % 3]
        oT, oT_b = onT[s % 3], onT_b[s % 3]
        for hf in range(2):
            pbv = psbf(c, hf)

            def tr(e, hf=hf, pbv=pbv):
                ins = None
                for hq in range(8):
                    ins = e.transpose(pbv[:, hq * 128:(hq + 1) * 128], on[:, hf * 8 + hq, :], c.identb)
                return ins
            P.add("pe", tr, reads=[on_b, c.cstbf_b], writes=[c.psb[hf]])
            if hf == 0:
                P.add("act", lambda e, pbv=pbv: e.activation(
                    oT[:, 0:8, :], pbv.rearrange("p (h d) -> p h d", h=8), AF.Copy), reads=[c.psb[0]], writes=[oT_b])
            else:
                P.add("dve", lambda e, pbv=pbv: e.tensor_copy(
                    oT[:, 8:16, :], pbv.rearrange("p (h d) -> p h d", h=8)), reads=[c.psb[1]], writes=[oT_b])

    def g2(s):
        r0 = s * 128
        oT, oT_b = onT[s % 3], onT_b[s % 3]
        banks = (4, 5) if s % 2 == 0 else (6, 7)
        for hh, bk in enumerate(banks):
            def mmo(e, hh=hh, bk=bk):
                ins = None
                for j in range(H):
                    ins = e.matmul(c.ps[bk][:, :], oT[:, j, :], wo[:, j, hh * 512:(hh + 1) * 512],
                                   start=(j == 0), stop=(j == H - 1))
                return ins
            P.add("pe", mmo, reads=[oT_b] + list(wo_b), writes=[c.psb[bk]])
        x, xb = xs[s % 3], xs_b[s % 3]
        resid_ln(c, src, dst, r0, banks, 1.0 / ALPHA, x, xb, gt, bt, gb_b, lns[s % 2])
    pipeline(T // 128, [g0, g1, g2])
    P.barrier()
    A.release(m0)


def build_program(T=SEQ):
    nc = bass.Bass("TRN2", target_bir_lowering=False)
    din = lambda n, s, d=F32: nc.dram_tensor(n, s, d, kind="ExternalInput").ap()
    dsc = lambda n, s, d=F32: nc.dram_tensor(n, s, d, kind="Internal").ap()
    x = din("x", [T, D])
    ln_g = din("ln_g", [2, 3, D])
    ln_b = din("ln_b", [2, 3, D])
    fpre_in = din("ffn_pre_w_in", [2, D, 2 * DFF])
    fpre_out = din("ffn_pre_w_out", [2, DFF, D])
    fpost_in = din("ffn_post_w_in", [2, D, 2 * DFF])
    fpost_out = din("ffn_post_w_out", [2, DFF, D])
    m_in = din("moba_w_in", [1, D, 3 * D])
    m_out = din("moba_w_out", [1, D, D])
    g_in = din("gdn_w_in", [1, D, GPROJ])
    g_conv = din("gdn_conv_w", [1, 4, 4096])
    g_alog = din("gdn_a_log", [1, 16])
    g_dtb = din("gdn_dt_bias", [1, 16])
    g_nw = din("gdn_norm_w", [1, 128])
    g_out = din("gdn_w_out", [1, 2048, D])
    consts = din("consts", [128, NCONST])
    rope = din("rope", [2, 128, T])
    y = nc.dram_tensor("y", [T, D], F32, kind="ExternalOutput").ap()
    hA = dsc("hA", [T, D])
    hB = dsc("hB", [T, D])
    qT_d = dsc("qT_d", [8, 128, T], BF16)
    kT_d = dsc("kT_d", [8, 128, T], BF16)
    v_d = dsc("v_d", [T, D], BF16)
    bT_d = dsc("bT_d", [8, 16, T], BF16)
    oT_d = dsc("oT_d", [8, 128, T], BF16)
    ktok_d = dsc("ktok_d", [T, 1024], BF16)
    vtok_d = dsc("vtok_d", [T, 2048], BF16)
    z_d = dsc("z_d", [T, 2048])
    gg_d = dsc("gg_d", [T, 16])
    beta_d = dsc("beta_d", [T, 16])
    o_d = dsc("o_d", [T, 2048])
    c = make_ctx(nc)
    load_small_consts(c)
    load_consts_full(c, consts)
    c.P.barrier()
    stage_ffn(c, x, hA, fpre_in[0], fpre_out[0], ln_g, ln_b, 0, 0, T)
    stage_moba_proj(c, hA, m_in[0], rope, qT_d, kT_d, v_d, bT_d, T)
    stage_moba_attn(c, qT_d, kT_d, v_d, bT_d, oT_d, T)
    stage_outproj_ln(c, oT_d, 8, m_out[0], hA, hB, ln_g, ln_b, 0, 1, 1.0 / ALPHA, T)
    stage_ffn(c, hB, hA, fpost_in[0], fpost_out[0], ln_g, ln_b, 0, 2, T)
    stage_ffn(c, hA, hB, fpre_in[1], fpre_out[1], ln_g, ln_b, 1, 0, T)
    stage_gdn_proj(c, hB, g_in[0], g_conv[0], g_alog, g_dtb, qT_d, kT_d, ktok_d, vtok_d, z_d, gg_d, beta_d, T)
    stage_gdn_scan2(c, qT_d, kT_d, ktok_d, vtok_d, gg_d, beta_d, o_d, T)
    stage_gdn_out(c, o_d, z_d, g_nw, g_out[0], hB, hA, ln_g, ln_b, 1, 1, T)
    stage_ffn(c, hA, y, fpost_in[1], fpost_out[1], ln_g, ln_b, 1, 2, T)
    c.P.emit()
    return nc


_CACHE = {}


def kernel(x, ln_g, ln_b, ffn_pre_w_in, ffn_pre_w_out, ffn_post_w_in, ffn_post_w_out,
           moba_w_in, moba_w_out, gdn_w_in, gdn_conv_w, gdn_a_log, gdn_dt_bias, gdn_norm_w, gdn_w_out):
    B, T, _ = x.shape
    if "nc" not in _CACHE:
        _CACHE["nc"] = build_program(T)
    nc = _CACHE["nc"]
    f = lambda a: np.ascontiguousarray(np.asarray(a, dtype=np.float32))
    shared = dict(ln_g=f(ln_g), ln_b=f(ln_b), ffn_pre_w_in=f(ffn_pre_w_in), ffn_pre_w_out=f(ffn_pre_w_out),
                  ffn_post_w_in=f(ffn_post_w_in), ffn_post_w_out=f(ffn_post_w_out), moba_w_in=f(moba_w_in),
                  moba_w_out=f(moba_w_out), gdn_w_in=f(gdn_w_in), gdn_conv_w=f(gdn_conv_w), gdn_a_log=f(gdn_a_log),
                  gdn_dt_bias=f(gdn_dt_bias), gdn_norm_w=f(gdn_norm_w), gdn_w_out=f(gdn_w_out),
                  consts=make_consts(), rope=make_rope(T))
    xs = f(x)
    in_maps = [dict(shared, x=xs[b]) for b in range(B)]
    res = run_bass_kernel_spmd(nc, in_maps, core_ids=list(range(B)))
    return np.stack([np.asarray(r["y"], dtype=np.float32) for r in res.results], axis=0)


class _Rec:
    def __init__(self):
        self.items = []

    def add(self, *a, **k):
        self.items.append(("add", a, k))

    def dma(self, *a, **k):
        self.items.append(("dma", a, k))


def _replay(P, items):
    for kind, a, k in items:
        if kind == "add":
            P.add(*a, **k)
        else:
            P.dma(*a, **k)


def _merge(la, lb):
    out = []
    na, nb = len(la), len(lb)
    ia = ib = 0
    while ia < na or ib < nb:
        if ib >= nb or (ia < na and ia * nb <= ib * na):
            out.append(la[ia])
            ia += 1
        else:
            out.append(lb[ib])
            ib += 1
    return out


def stage_gdn_scan2(c, qT_d, kT_d, ktok_d, vtok_d, g_d, beta_d, o_d, T):
    P0, A = c.P, c.A
    m0 = A.mark()
    NCH = T // 128
    H = 16
    UT = c.cst[:, C_UT:C_UT + 128]
    ones32 = c.cst[:, C_ONES:C_ONES + 128]
    SM = c.cst[:, C_SMASK:C_SMASK + 128]
    IMT = c.cst[:, C_IMASKT:C_IMASKT + 128]

    def tl(dt, name, nb=4):
        return A.alloc([128, H, 128], dt, name), P0.bufs_n(name, nb)
    qTc = [A.alloc([128, 8, 128], BF16, "qTc") for _ in range(2)]
    kTc = [A.alloc([128, 8, 128], BF16, "kTc") for _ in range(2)]
    ktk = [A.alloc([128, 8, 128], BF16, "ktk") for _ in range(2)]
    vtk = [A.alloc([128, H, 128], BF16, "vtk") for _ in range(2)]
    gb = [A.alloc([128, 2, H], F32, "gb") for _ in range(2)]
    qTc_b, kTc_b, ktk_b, vtk_b = P0.bufs_n("qTc", 2), P0.bufs_n("kTc", 2), P0.bufs_n("ktk", 2), P0.bufs_n("vtk", 2)
    g_b, be_b = P0.bufs_n("gld", 2), P0.bufs_n("bld", 2)
    F1 = [tl(F32R, "F1_%d" % i) for i in range(2)]
    F2, F2b = tl(F32, "F2")
    F3a, F3ab = tl(F32, "F3a")
    F3, F3b = tl(F32, "F3u")
    Rp = [tl(F32R, "R%d" % i) for i in range(2)]
    RTp = [tl(F32R, "RT%d" % i) for i in range(2)]
    Y, Yb = tl(F32R, "Y")
    attnT2 = [tl(BF16, "attnT%d" % i) for i in range(2)]
    qdT2 = [tl(BF16, "qdT%d" % i) for i in range(2)]
    kw2 = [tl(BF16, "kw%d" % i, 1) for i in range(2)]
    kdec2 = [tl(BF16, "kdec%d" % i, 1) for i in range(2)]
    vbt2 = [tl(BF16, "vbt%d" % i, 1) for i in range(2)]
    TTb, TTb_b = tl(BF16, "TTb")
    wT, wT_b = tl(BF16, "wT")
    vnew, vnew_b = tl(BF16, "vnew")
    osb, osb_b = tl(F32, "osb")
    S, S_b = tl(F32, "S")
    Sbf, Sbf_b = tl(BF16, "Sbf")
    sm2 = [A.alloc([128, 8, H], F32, "ssm") for _ in range(2)]
    sm2_b = P0.bufs_n("ssm", 2)
    identr_t = A.alloc([128, 128], F32R, "identr")
    identr_b = P0.buf("identr")
    P0.add("dve", lambda e: e.tensor_copy(identr_t[:], c.ident), reads=[c.cstb], writes=[identr_b])
    identr = identr_t[:]
    P0.add("dve", lambda e: e.memset(S[:], 0.0), writes=S_b)
    P0.add("dve", lambda e: e.memset(Sbf[:], 0.0), writes=Sbf_b)

    def flat(t):
        return t[:].rearrange("p h d -> p (h d)")

    def bank4(q):
        return c.ps[q][:].rearrange("p (h d) -> p h d", h=4)

    def hs4(q):
        return slice(4 * q, 4 * q + 4)

    def phaseA(P, ch):
        c0 = ch * 128
        pb = ch % 2
        F1t, F1b = F1[pb]
        attnT, attnT_b = attnT2[pb]
        qdT, qdT_b = qdT2[pb]
        kw, kw_b = kw2[pb]
        kdec, kdec_b = kdec2[pb]
        vbt, vbt_b = vbt2[pb]
        sm, sm_b = sm2[pb], sm2_b[pb]
        P.dma(gb[pb][:, 0, :], g_d[c0:c0 + 128, :], writes=[g_b[pb]], sembuf=g_b[pb])
        P.dma(gb[pb][:, 1, :], beta_d[c0:c0 + 128, :], writes=[be_b[pb]], sembuf=be_b[pb])
        P.dma(qTc[pb][:], qT_d[:, :, c0:c0 + 128].rearrange("h p t -> p h t"), writes=[qTc_b[pb]], sembuf=qTc_b[pb])
        P.dma(kTc[pb][:], kT_d[:, :, c0:c0 + 128].rearrange("h p t -> p h t"), writes=[kTc_b[pb]], sembuf=kTc_b[pb])
        P.dma(ktk[pb][:], ktok_d[c0:c0 + 128, :].rearrange("p (h d) -> p h d", h=8), writes=[ktk_b[pb]], sembuf=ktk_b[pb])
        P.dma(vtk[pb][:], vtok_d[c0:c0 + 128, :].rearrange("p (h d) -> p h d", h=H), writes=[vtk_b[pb]], sembuf=vtk_b[pb])
        g = gb[pb][:, 0, :]
        beta = gb[pb][:, 1, :]

        def mm1(e):
            e.matmul(c.ps[4][:, 0:16], UT, g, start=True, stop=True)
            return e.matmul(c.ps[4][:, 16:32], ones32, g, start=True, stop=True)
        P.add("pe", mm1, reads=[g_b[pb], c.cstb], writes=[c.psb[4]])
        P.add("act", lambda e: e.activation(sm[:, 0, :], c.ps[4][:, 0:16], AF.Copy), reads=[c.psb[4]], writes=[sm_b])
        P.add("act", lambda e: e.activation(sm[:, 1, :], c.ps[4][:, 0:16], AF.Identity, scale=-1.0), reads=[c.psb[4]], writes=[sm_b])
        P.add("act", lambda e: e.activation(sm[:, 2, :], c.ps[4][:, 0:16], AF.Exp), reads=[c.psb[4]], writes=[sm_b])
        P.add("act", lambda e: e.activation(sm[:, 3, :], c.ps[4][:, 16:32], AF.Exp), reads=[c.psb[4]], writes=[sm_b])
        P.add("act", lambda e: e.activation(sm[:, 7, :], c.ps[4][:, 16:32], AF.Identity, bias=0.0, scale=1.0),
              reads=[c.psb[4]], writes=[sm_b])
        P.add("dve", lambda e: e.tensor_tensor(sm[:, 7, :], sm[:, 7, :], sm[:, 0, :], ALU.subtract), reads=[sm_b], writes=[sm_b])
        P.add("act", lambda e: e.activation(sm[:, 4, :], sm[:, 7, :], AF.Exp), reads=[sm_b], writes=[sm_b])
        P.add("dve", lambda e: e.tensor_tensor(sm[:, 5, :], beta, sm[:, 2, :], ALU.mult), reads=[sm_b, be_b[pb]], writes=[sm_b])
        P.add("dve", lambda e: e.tensor_scalar(sm[:, 6, :], beta, -1.0, None, ALU.mult), reads=[sm_b, be_b[pb]], writes=[sm_b])
        P.add("pool", lambda e: e.tensor_tensor(
            vbt[:], vtk[pb][:], beta.rearrange("p (h o) -> p h o", o=1).to_broadcast([128, H, 128]), ALU.mult),
            reads=[vtk_b[pb], be_b[pb]], writes=vbt_b)
        for par in range(2):
            P.add("pool", lambda e, par=par: e.tensor_tensor(
                kw[:, par::2, :], ktk[pb][:],
                sm[:, 5, par::2].rearrange("p (h o) -> p h o", o=1).to_broadcast([128, 8, 128]), ALU.mult),
                reads=[ktk_b[pb], sm_b], writes=kw_b)
            P.add("pool", lambda e, par=par: e.tensor_tensor(
                kdec[:, par::2, :], ktk[pb][:],
                sm[:, 4, par::2].rearrange("p (h o) -> p h o", o=1).to_broadcast([128, 8, 128]), ALU.mult),
                reads=[ktk_b[pb], sm_b], writes=kdec_b)
        for half in range(2):
            gq = (2 * half, 2 * half + 1)
            gbank = {gq[0]: 5, gq[1]: 6}
            for q in gq:
                bk = gbank[q]
                P.add("dve", lambda e, q=q: e.tensor_tensor(
                    F3a[:, hs4(q), :], UT.rearrange("p (o i) -> p o i", o=1).to_broadcast([128, 4, 128]),
                    g[:, 4 * q:4 * q + 4].rearrange("p (h o) -> p h o", o=1).to_broadcast([128, 4, 128]), ALU.mult),
                    reads=[g_b[pb], c.cstb], writes=[F3ab[q]])
                P.add("pe", lambda e, q=q, bk=bk: e.matmul(c.ps[bk][:, :], ones32, F3a[:, hs4(q), :].rearrange("p h d -> p (h d)"),
                                                           start=True, stop=True), reads=[F3ab[q], c.cstb], writes=[c.psb[bk]])
                P.add("act", lambda e, q=q, bk=bk: e.activation(F3a[:, hs4(q), :], bank4(bk), AF.Exp),
                      reads=[c.psb[bk]], writes=[F3ab[q]])
                P.add("dve", lambda e, q=q, bk=bk: e.tensor_tensor(
                    F2[:, hs4(q), :], bank4(bk),
                    IMT.rearrange("p (o i) -> p o i", o=1).to_broadcast([128, 4, 128]), ALU.add),
                    reads=[c.psb[bk], c.cstb], writes=[F2b[q]])
            for par in range(2):
                P.add("pool", lambda e, par=par, half=half: e.tensor_tensor(
                    qdT[:, 8 * half + par:8 * half + 8:2, :], qTc[pb][:, 4 * half:4 * half + 4, :],
                    F3a[:, 8 * half + par:8 * half + 8:2, :], ALU.mult),
                    reads=[F3ab[gq[0]], F3ab[gq[1]], qTc_b[pb]], writes=[qdT_b[gq[0]], qdT_b[gq[1]]])
            for h in range(8 * half, 8 * half + 8):
                P.add("act", lambda e, h=h: e.activation(F2[:, h, :], F2[:, h, :], AF.Exp, bias=sm[:, 1, h:h + 1], scale=1.0),
                      reads=[F2b[h // 4], sm_b], writes=[F2b[h // 4]])

            def mmqk(e, half=half):
                ins = None
                for hq in range(4):
                    hk = 4 * half + hq
                    ins = e.matmul(c.ps[7][:, hq * 128:(hq + 1) * 128], kTc[pb][:, hk, :], qTc[pb][:, hk, :], start=True, stop=True)
                return ins
            P.add("pe", mmqk, reads=[kTc_b[pb], qTc_b[pb]], writes=[c.psb[7]])
            for par in range(2):
                P.add("dve", lambda e, par=par, half=half: e.tensor_tensor(
                    attnT[:, 8 * half + par:8 * half + 8:2, :], bank4(7), F2[:, 8 * half + par:8 * half + 8:2, :], ALU.mult),
                    reads=[c.psb[7], F2b[gq[0]], F2b[gq[1]]], writes=[attnT_b[gq[0]], attnT_b[gq[1]]])
            for q in gq:
                bk = gbank[q]
                P.add("dve", lambda e, q=q, bk=bk: e.tensor_tensor(
                    F2[:, hs4(q), :], SM.rearrange("p (o i) -> p o i", o=1).to_broadcast([128, 4, 128]),
                    bank4(bk), ALU.subtract),
                    reads=[c.psb[bk], c.cstb], writes=[F2b[q]])
            for h in range(8 * half, 8 * half + 8):
                P.add("act", lambda e, h=h: e.activation(F2[:, h, :], F2[:, h, :], AF.Exp, bias=sm[:, 0, h:h + 1], scale=1.0),
                      reads=[F2b[h // 4], sm_b], writes=[F2b[h // 4]])

            def mmkk(e, half=half):
                ins = None
                for hq in range(4):
                    hk = 4 * half + hq
                    ins = e.matmul(c.ps[4][:, hq * 128:(hq + 1) * 128], kTc[pb][:, hk, :], kTc[pb][:, hk, :], start=True, stop=True)
                return ins
            P.add("pe", mmkk, reads=[kTc_b[pb]], writes=[c.psb[4]])
            for h in range(8 * half, 8 * half + 8):
                hq = (h // 2) % 4
                P.add("dve", lambda e, h=h, hq=hq: e.scalar_tensor_tensor(
                    F1t[:, h, :], c.ps[4][:, hq * 128:(hq + 1) * 128], sm[:, 6, h:h + 1], F2[:, h, :],
                    ALU.mult, ALU.mult), reads=[c.psb[4], sm_b, F2b[h // 4]], writes=[F1b[h // 4]])

    def phaseB(P, ch):
        c0 = ch * 128
        pb = ch % 2
        F1t, F1b = F1[pb]
        attnT, attnT_b = attnT2[pb]
        qdT, qdT_b = qdT2[pb]
        kw, kw_b = kw2[pb]
        kdec, kdec_b = kdec2[pb]
        vbt, vbt_b = vbt2[pb]
        sm, sm_b = sm2[pb], sm2_b[pb]
        R, Rb = Rp[0]
        for hf in range(4):
            bk = 2 + (hf % 2)

            def trm(e, hf=hf, bk=bk):
                ins = None
                for hq in range(4):
                    h = hf * 4 + hq
                    ins = e.transpose(c.ps[bk][:, hq * 128:(hq + 1) * 128].bitcast(F32R), F1t[:, h, :], identr)
                return ins
            P.add("pe", trm, reads=[F1b[hf], identr_b], writes=[c.psb[bk]])
            P.add("act", lambda e, hf=hf, R=R, bk=bk: e.activation(R[:, hs4(hf), :], bank4(bk), AF.Copy),
                  reads=[c.psb[bk]], writes=[Rb[hf]])
            P.add("dve", lambda e, hf=hf, bk=bk: e.tensor_tensor(
                Y[:, hs4(hf), :], bank4(bk),
                c.ident.rearrange("p (o i) -> p o i", o=1).to_broadcast([128, 4, 128]), ALU.add),
                reads=[c.psb[bk], c.cstb], writes=[Yb[hf]])
        RT, RTb = F1t, F1b
        NLEV = 6
        for lev in range(1, NLEV + 1):
            Rn, Rnb = Rp[lev % 2]
            RTn, RTnb = RTp[lev % 2]
            last = (lev == NLEV)

            def emit_y(hy, RTn=RTn, RTnb=RTnb):
                bk = hy % 2

                def mmy(e):
                    ins = None
                    for hq in range(4):
                        h = hy * 4 + hq
                        ins = e.matmul(c.ps[bk][:, hq * 128:(hq + 1) * 128], RTn[:, h, :], Y[:, h, :], start=True, stop=True)
                    return ins
                P.add("pe", mmy, reads=[RTnb[hy], Yb[hy]], writes=[c.psb[bk]])
                P.add("dve", lambda e: e.tensor_tensor(Y[:, hs4(hy), :], bank4(bk), Y[:, hs4(hy), :].bitcast(F32), ALU.add),
                      reads=[c.psb[bk], Yb[hy]], writes=[Yb[hy]])
            for hf in range(4):
                b_rt, b_r = (hf % 2), 2 + (hf % 2)

                def mmrt(e, hf=hf, R=R, RT=RT, bk=b_rt):
                    ins = None
                    for hq in range(4):
                        h = hf * 4 + hq
                        ins = e.matmul(c.ps[bk][:, hq * 128:(hq + 1) * 128], R[:, h, :], RT[:, h, :], start=True, stop=True)
                    return ins
                P.add("pe", mmrt, reads=[Rb[hf], RTb[hf]], writes=[c.psb[b_rt]])
                P.add("act", lambda e, hf=hf, RTn=RTn, bk=b_rt: e.activation(RTn[:, hs4(hf), :], bank4(bk), AF.Copy),
                      reads=[c.psb[b_rt]], writes=[RTnb[hf]])
                if not last:
                    def mmr(e, hf=hf, R=R, RT=RT, bk=b_r):
                        ins = None
                        for hq in range(4):
                            h = hf * 4 + hq
                            ins = e.matmul(c.ps[bk][:, hq * 128:(hq + 1) * 128], RT[:, h, :], R[:, h, :], start=True, stop=True)
                        return ins
                    P.add("pe", mmr, reads=[Rb[hf], RTb[hf]], writes=[c.psb[b_r]])
                    P.add("dve", lambda e, hf=hf, Rn=Rn, bk=b_r: e.tensor_copy(Rn[:, hs4(hf), :], bank4(bk)),
                          reads=[c.psb[b_r]], writes=[Rnb[hf]])
                if hf >= 1:
                    emit_y(hf - 1)
            emit_y(3)
            R, Rb = Rn, Rnb
            RT, RTb = RTn, RTnb
        for hf in range(4):
            P.add("act", lambda e, hf=hf: e.activation(TTb[:, hs4(hf), :], Y[:, hs4(hf), :].bitcast(F32), AF.Copy),
                  reads=[Yb[hf]], writes=[TTb_b[hf]])
        for hf in range(4):
            bu, bw_ = (hf % 2), 2 + (hf % 2)

            def mmu(e, hf=hf, bk=bu):
                ins = None
                for hq in range(4):
                    h = hf * 4 + hq
                    ins = e.matmul(c.ps[bk][:, hq * 128:(hq + 1) * 128], TTb[:, h, :], vbt[:, h, :], start=True, stop=True)
                return ins
            P.add("pe", mmu, reads=[TTb_b[hf]] + list(vbt_b), writes=[c.psb[bu]])
            P.add("act", lambda e, hf=hf, bk=bu: e.activation(F3[:, hs4(hf), :], bank4(bk), AF.Copy),
                  reads=[c.psb[bu]], writes=[F3b[hf]])

            def mmw(e, hf=hf, bk=bw_):
                ins = None
                for hq in range(4):
                    h = hf * 4 + hq
                    ins = e.matmul(c.ps[bk][:, hq * 128:(hq + 1) * 128], kw[:, h, :], TTb[:, h, :], start=True, stop=True)
                return ins
            P.add("pe", mmw, reads=[TTb_b[hf]] + list(kw_b), writes=[c.psb[bw_]])
            P.add("dve", lambda e, hf=hf, bk=bw_: e.tensor_copy(wT[:, hs4(hf), :], bank4(bk)),
                  reads=[c.psb[bw_]], writes=[wT_b[hf]])
        for hf in range(4):
            bk = (hf % 2)

            def mmws(e, hf=hf, bk=bk):
                ins = None
                for hq in range(4):
                    h = hf * 4 + hq
                    ins = e.matmul(c.ps[bk][:, hq * 128:(hq + 1) * 128], wT[:, h, :], Sbf[:, h, :], start=True, stop=True)
                return ins
            P.add("pe", mmws, reads=[wT_b[hf]] + list(Sbf_b), writes=[c.psb[bk]])
            P.add("dve", lambda e, hf=hf, bk=bk: e.tensor_tensor(vnew[:, hs4(hf), :], F3[:, hs4(hf), :], bank4(bk), ALU.subtract),
                  reads=[c.psb[bk], F3b[hf]], writes=[vnew_b[hf]])
        for hf in range(4):
            bk = 2 + (hf % 2)

            def mmo(e, hf=hf, bk=bk):
                ins = None
                for hq in range(4):
                    h = hf * 4 + hq
                    e.matmul(c.ps[bk][:, hq * 128:(hq + 1) * 128], qdT[:, h, :], Sbf[:, h, :], start=True, stop=False)
                    ins = e.matmul(c.ps[bk][:, hq * 128:(hq + 1) * 128], attnT[:, h, :], vnew[:, h, :], start=False, stop=True)
                return ins
            P.add("pe", mmo, reads=[qdT_b[hf], attnT_b[hf], vnew_b[hf]] + list(Sbf_b), writes=[c.psb[bk]])
            P.add("act", lambda e, hf=hf, bk=bk: e.activation(osb[:, hs4(hf), :], bank4(bk), AF.Copy),
                  reads=[c.psb[bk]], writes=[osb_b[hf]])
        P.dma(o_d[c0:c0 + 128, :], flat(osb), reads=list(osb_b), sembuf=osb_b[0])
        for hf in range(4):
            bk = (hf % 2)

            def mmds(e, hf=hf, bk=bk):
                ins = None
                for hq in range(4):
                    h = hf * 4 + hq
                    ins = e.matmul(c.ps[bk][:, hq * 128:(hq + 1) * 128], kdec[:, h, :], vnew[:, h, :], start=True, stop=True)
                return ins
            P.add("pe", mmds, reads=list(kdec_b) + [vnew_b[hf]], writes=[c.psb[bk]])
            for hq in range(4):
                h = hf * 4 + hq
                P.add("dve", lambda e, h=h, bk=bk, hq=hq: e.scalar_tensor_tensor(
                    S[:, h, :], S[:, h, :], sm[:, 3, h:h + 1], c.ps[bk][:, hq * 128:(hq + 1) * 128], ALU.mult, ALU.add),
                    reads=[c.psb[bk], sm_b, S_b[hf]], writes=[S_b[hf]])
            P.add("act", lambda e, hf=hf: e.activation(Sbf[:, hs4(hf), :], S[:, hs4(hf), :], AF.Copy),
                  reads=[S_b[hf]], writes=[Sbf_b[hf]])

    ra = _Rec()
    phaseA(ra, 0)
    _replay(P0, ra.items)
    for ch in range(NCH):
        rb = _Rec()
        phaseB(rb, ch)
        ra = _Rec()
        if ch + 1 < NCH:
            phaseA(ra, ch + 1)
        _replay(P0, _merge(rb.items, ra.items))
    P0.barrier()
    A.release(m0)
```
